# Optimizing a Trainium2 kernel written in Bass

```python
import math
import jax, jax.numpy as jnp
from jax import lax
import numpy as np

D_MODEL = 1024
BATCH = 16
SEQ = 4096
DEPTH = 4

CHUNK = 64
Q_BLOCK = 128
N_MIXERS = 3
RMS_EPS = 1e-6
DA_HEADS = D_MODEL // 128
DA_HEAD_DIM = 64
DA_V_DIM = 2 * DA_HEAD_DIM
ROPE_THETA = 500000.0
ROPE_DIM = DA_HEAD_DIM // 4
CONV_KERNEL = 31
GDN_K_HEADS = D_MODEL // 128
GDN_V_HEADS = 2 * GDN_K_HEADS
GDN_HEAD_DIM = 128
GDN_CONV = 4
GDN_KEY_DIM = GDN_K_HEADS * GDN_HEAD_DIM
GDN_VAL_DIM = GDN_V_HEADS * GDN_HEAD_DIM
GDN_IN_DIM = 2 * GDN_KEY_DIM + 2 * GDN_VAL_DIM + 2 * GDN_V_HEADS
D_FF = 256 * ((8 * D_MODEL // 3 + 255) // 256)
FFN_CONV = 3
N_ATTN = (DEPTH + 2) // 3
N_CONVM = (DEPTH + 1) // 3
N_GDN = DEPTH // 3

kernel_name = "hybrid_diffattn_conformer_gdn_trunk"


def rmsnorm(x, g, eps=RMS_EPS):
    xf = x.astype(jnp.float32)
    y = xf * lax.rsqrt(jnp.mean(xf * xf, axis=-1, keepdims=True) + eps)
    return (y * g.astype(jnp.float32)).astype(x.dtype)


def causal_dwconv(x, w):
    K, C = w.shape
    return lax.conv_general_dilated(x, w[:, None, :].astype(x.dtype), window_strides=(1,),
                                    padding=[(K - 1, 0)], dimension_numbers=("NWC", "WIO", "NWC"),
                                    feature_group_count=C)


def partial_rope(x, positions):
    half = ROPE_DIM // 2
    inv_freq = jnp.power(ROPE_THETA, -jnp.arange(half, dtype=jnp.float32) * 2.0 / ROPE_DIM)
    ang = positions.astype(jnp.float32)[:, :, None] * inv_freq
    cos = jnp.cos(ang)[:, :, None, None, :].astype(x.dtype)
    sin = jnp.sin(ang)[:, :, None, None, :].astype(x.dtype)
    x1 = x[..., :half]
    x2 = x[..., half:ROPE_DIM]
    return jnp.concatenate([x1 * cos - x2 * sin, x2 * cos + x1 * sin, x[..., ROPE_DIM:]], axis=-1)


def lambda_init_fn(layer):
    return 0.8 - 0.6 * math.exp(-0.3 * layer)


def diff_attention(h, positions, w_qkv, lam_p, subln, w_o, lambda_init):
    B, S, _ = h.shape
    qkv = h @ w_qkv
    nq = DA_HEADS * 2 * DA_HEAD_DIM
    q = qkv[..., :nq].reshape(B, S, DA_HEADS, 2, DA_HEAD_DIM)
    k = qkv[..., nq:2 * nq].reshape(B, S, DA_HEADS, 2, DA_HEAD_DIM)
    v = qkv[..., 2 * nq:].reshape(B, S, DA_HEADS, DA_V_DIM)
    q = partial_rope(q, positions) * (DA_HEAD_DIM ** -0.5)
    k = partial_rope(k, positions)
    lp = lam_p.astype(jnp.float32)
    lam = jnp.exp(jnp.sum(lp[0] * lp[1])) - jnp.exp(jnp.sum(lp[2] * lp[3])) + lambda_init
    cid = np.arange(S) // CHUNK
    outs = []
    for qb in range(S // Q_BLOCK):
        s0, s1 = qb * Q_BLOCK, (qb + 1) * Q_BLOCK
        mask = jnp.asarray(cid[s0:s1, None] >= cid[None, :s1])
        scores = jnp.einsum('bqhmd,bkhmd->bhmqk', q[:, s0:s1], k[:, :s1]).astype(jnp.float32)
        p = jax.nn.softmax(jnp.where(mask, scores, -jnp.inf), axis=-1)
        a = p[:, :, 0] - lam * p[:, :, 1]
        outs.append(jnp.einsum('bhqk,bkhe->bqhe', a.astype(v.dtype), v[:, :s1]))
    o = jnp.concatenate(outs, axis=1)
    o = rmsnorm(o, subln, eps=1e-5) * (1.0 - lambda_init)
    return o.reshape(B, S, DA_HEADS * DA_V_DIM) @ w_o


def conformer_conv(h, w_in, b_in, dw, dw_b, ln_g, ln_b, w_out, b_out):
    u = h @ w_in + b_in
    a, g = jnp.split(u, 2, axis=-1)
    u = a * jax.nn.sigmoid(g)
    u = causal_dwconv(u, dw) + dw_b
    uf = u.astype(jnp.float32)
    mu = jnp.mean(uf, axis=-1, keepdims=True)
    var = jnp.mean(jnp.square(uf - mu), axis=-1, keepdims=True)
    u = ((uf - mu) * lax.rsqrt(var + 1e-5) * ln_g.astype(jnp.float32) + ln_b.astype(jnp.float32)).astype(h.dtype)
    return jax.nn.silu(u) @ w_out + b_out


def l2norm(x, eps=1e-6):
    return x * lax.rsqrt(jnp.sum(x * x, axis=-1, keepdims=True) + eps)


def chunk_gated_delta_rule(q, k, v, g, beta):
    B, S, H, DK = q.shape
    DV = v.shape[-1]
    C = CHUNK
    N = S // C

    def to_chunks(t):
        return t.reshape(B, N, C, H, -1).transpose(0, 3, 1, 2, 4)

    q, k, v = to_chunks(q), to_chunks(k), to_chunks(v)
    g = g.reshape(B, N, C, H).transpose(0, 3, 1, 2)
    beta = beta.reshape(B, N, C, H).transpose(0, 3, 1, 2)
    gc = jnp.cumsum(g, axis=-1)
    tril = jnp.tril(jnp.ones((C, C), dtype=bool))
    strict = jnp.tril(jnp.ones((C, C), dtype=bool), k=-1)
    decay = jnp.exp(jnp.where(tril, gc[..., :, None] - gc[..., None, :], -jnp.inf))
    k_beta = k * beta[..., None]
    v_beta = v * beta[..., None]
    L = jnp.where(strict, jnp.einsum('bhnid,bhnjd->bhnij', k_beta, k) * decay, 0.0)
    eye = jnp.eye(C, dtype=jnp.float32)
    T = lax.linalg.triangular_solve(eye + L, jnp.broadcast_to(eye, L.shape), left_side=True, lower=True)
    u = jnp.einsum('bhnij,bhnje->bhnie', T, v_beta)
    w = jnp.einsum('bhnij,bhnjd->bhnid', T, k_beta * jnp.exp(gc)[..., None])
    a_qk = jnp.einsum('bhnid,bhnjd->bhnij', q, k) * decay
    xs = tuple(jnp.moveaxis(t, 2, 0) for t in (q, k, u, w, a_qk, gc))

    def step(state, inp):
        q_n, k_n, u_n, w_n, a_n, gc_n = inp
        v_new = u_n - jnp.einsum('bhcd,bhde->bhce', w_n, state)
        o = (jnp.einsum('bhcd,bhde->bhce', q_n * jnp.exp(gc_n)[..., None], state)
             + jnp.einsum('bhij,bhje->bhie', a_n, v_new))
        g_last = gc_n[..., -1]
        state = (state * jnp.exp(g_last)[..., None, None]
                 + jnp.einsum('bhcd,bhce->bhde', k_n * jnp.exp(g_last[..., None] - gc_n)[..., None], v_new))
        return state, o

    state0 = jnp.zeros((B, H, DK, DV), jnp.float32)
    _, o = lax.scan(step, state0, xs)
    return o.transpose(1, 0, 3, 2, 4).reshape(B, S, H, DV)


def gated_deltanet(h, w_in, conv_w, a_log, dt_bias, norm_g, w_o):
    B, S, _ = h.shape
    proj = h @ w_in
    n_qkv = 2 * GDN_KEY_DIM + GDN_VAL_DIM
    qkv = jax.nn.silu(causal_dwconv(proj[..., :n_qkv], conv_w))
    z = proj[..., n_qkv:n_qkv + GDN_VAL_DIM].reshape(B, S, GDN_V_HEADS, GDN_HEAD_DIM)
    b = proj[..., n_qkv + GDN_VAL_DIM:n_qkv + GDN_VAL_DIM + GDN_V_HEADS].astype(jnp.float32)
    a = proj[..., n_qkv + GDN_VAL_DIM + GDN_V_HEADS:].astype(jnp.float32)
    qkv = qkv.astype(jnp.float32)
    q = qkv[..., :GDN_KEY_DIM].reshape(B, S, GDN_K_HEADS, GDN_HEAD_DIM)
    k = qkv[..., GDN_KEY_DIM:2 * GDN_KEY_DIM].reshape(B, S, GDN_K_HEADS, GDN_HEAD_DIM)
    v = qkv[..., 2 * GDN_KEY_DIM:].reshape(B, S, GDN_V_HEADS, GDN_HEAD_DIM)
    rep = GDN_V_HEADS // GDN_K_HEADS
    q = jnp.repeat(l2norm(q) * (GDN_HEAD_DIM ** -0.5), rep, axis=2)
    k = jnp.repeat(l2norm(k), rep, axis=2)
    beta = jax.nn.sigmoid(b)
    g = -jnp.exp(a_log.astype(jnp.float32)) * jax.nn.softplus(a + dt_bias.astype(jnp.float32))
    o = chunk_gated_delta_rule(q, k, v, g, beta).astype(h.dtype)
    o = rmsnorm(o, norm_g) * jax.nn.silu(z)
    return o.reshape(B, S, GDN_VAL_DIM) @ w_o


def conv_ffn(h, w_in, dw, dw_b, w_out):
    u = causal_dwconv(h @ w_in, dw) + dw_b
    g, val = jnp.split(u, 2, axis=-1)
    return (jax.nn.silu(g) * val) @ w_out


def setup_inputs(seed: int = 0) -> dict:
    key = jax.random.key(seed)
    ks = iter(jax.random.split(key, 32))
    f32 = jnp.float32

    def wgt(shape, fan_in):
        return jax.random.normal(next(ks), shape, f32) * fan_in ** -0.5

    def gain(shape):
        return 1.0 + 0.02 * jax.random.normal(next(ks), shape, f32)

    def bias(shape):
        return 0.02 * jax.random.normal(next(ks), shape, f32)

    x = jax.random.normal(next(ks), (BATCH, SEQ, D_MODEL), f32)
    offset = jax.random.randint(next(ks), (BATCH, 1), 0, 4096, dtype=jnp.int32)
    positions = offset + jnp.arange(SEQ, dtype=jnp.int32)[None, :]
    norm_mix = gain((DEPTH, D_MODEL))
    norm_ffn = gain((DEPTH, D_MODEL))
    norm_final = gain((D_MODEL,))
    attn_w_qkv = wgt((N_ATTN, D_MODEL, 4 * DA_HEADS * DA_HEAD_DIM + DA_HEADS * DA_V_DIM), D_MODEL)
    attn_lambda = 0.1 * jax.random.normal(next(ks), (N_ATTN, 4, DA_HEAD_DIM), f32)
    attn_subln = gain((N_ATTN, DA_V_DIM))
    attn_w_o = wgt((N_ATTN, DA_HEADS * DA_V_DIM, D_MODEL), DA_HEADS * DA_V_DIM)
    conv_w_in = wgt((N_CONVM, D_MODEL, 2 * D_MODEL), D_MODEL)
    conv_b_in = bias((N_CONVM, 2 * D_MODEL))
    conv_dw = wgt((N_CONVM, CONV_KERNEL, D_MODEL), CONV_KERNEL)
    conv_dw_b = bias((N_CONVM, D_MODEL))
    conv_ln_g = gain((N_CONVM, D_MODEL))
    conv_ln_b = bias((N_CONVM, D_MODEL))
    conv_w_out = wgt((N_CONVM, D_MODEL, D_MODEL), D_MODEL)
    conv_b_out = bias((N_CONVM, D_MODEL))
    gdn_w_in = wgt((N_GDN, D_MODEL, GDN_IN_DIM), D_MODEL)
    gdn_conv = wgt((N_GDN, GDN_CONV, 2 * GDN_KEY_DIM + GDN_VAL_DIM), GDN_CONV)
    gdn_a_log = jnp.log(jax.random.uniform(next(ks), (N_GDN, GDN_V_HEADS), f32, 1.0, 16.0))
    dt = jnp.exp(jax.random.uniform(next(ks), (N_GDN, GDN_V_HEADS), f32, math.log(1e-3), math.log(1e-1)))
    gdn_dt_bias = dt + jnp.log(-jnp.expm1(-dt))
    gdn_norm = gain((N_GDN, GDN_HEAD_DIM))
    gdn_w_o = wgt((N_GDN, GDN_VAL_DIM, D_MODEL), GDN_VAL_DIM)
    ffn_w_in = wgt((DEPTH, D_MODEL, 2 * D_FF), D_MODEL)
    ffn_dw = wgt((DEPTH, FFN_CONV, 2 * D_FF), FFN_CONV)
    ffn_dw_b = bias((DEPTH, 2 * D_FF))
    ffn_w_out = wgt((DEPTH, D_FF, D_MODEL), D_FF)
    return {"x": x, "positions": positions, "norm_mix": norm_mix, "norm_ffn": norm_ffn,
            "norm_final": norm_final, "attn_w_qkv": attn_w_qkv, "attn_lambda": attn_lambda,
            "attn_subln": attn_subln, "attn_w_o": attn_w_o, "conv_w_in": conv_w_in,
            "conv_b_in": conv_b_in, "conv_dw": conv_dw, "conv_dw_b": conv_dw_b,
            "conv_ln_g": conv_ln_g, "conv_ln_b": conv_ln_b, "conv_w_out": conv_w_out,
            "conv_b_out": conv_b_out, "gdn_w_in": gdn_w_in, "gdn_conv": gdn_conv,
            "gdn_a_log": gdn_a_log, "gdn_dt_bias": gdn_dt_bias, "gdn_norm": gdn_norm,
            "gdn_w_o": gdn_w_o, "ffn_w_in": ffn_w_in, "ffn_dw": ffn_dw, "ffn_dw_b": ffn_dw_b,
            "ffn_w_out": ffn_w_out}


def reference(x, positions, norm_mix, norm_ffn, norm_final, attn_w_qkv, attn_lambda, attn_subln,
              attn_w_o, conv_w_in, conv_b_in, conv_dw, conv_dw_b, conv_ln_g, conv_ln_b, conv_w_out,
              conv_b_out, gdn_w_in, gdn_conv, gdn_a_log, gdn_dt_bias, gdn_norm, gdn_w_o,
              ffn_w_in, ffn_dw, ffn_dw_b, ffn_w_out):
    h = x
    for i in range(DEPTH):
        j = i // N_MIXERS
        kind = i % N_MIXERS
        hn = rmsnorm(h, norm_mix[i])
        if kind == 0:
            mix = diff_attention(hn, positions, attn_w_qkv[j], attn_lambda[j], attn_subln[j],
                                 attn_w_o[j], lambda_init_fn(i))
        elif kind == 1:
            mix = conformer_conv(hn, conv_w_in[j], conv_b_in[j], conv_dw[j], conv_dw_b[j],
                                 conv_ln_g[j], conv_ln_b[j], conv_w_out[j], conv_b_out[j])
        else:
            mix = gated_deltanet(hn, gdn_w_in[j], gdn_conv[j], gdn_a_log[j], gdn_dt_bias[j],
                                 gdn_norm[j], gdn_w_o[j])
        h = h + mix
        h = h + conv_ffn(rmsnorm(h, norm_ffn[i]), ffn_w_in[i], ffn_dw[i], ffn_dw_b[i], ffn_w_out[i])
    return rmsnorm(h, norm_final)
```

```python
import contextlib
import math
import os
DBG = os.environ.get('KDBG', '')
import numpy as np
import concourse.bass as bass
import concourse.mybir as mybir
from concourse.bass_utils import run_bass_kernel_spmd

F32 = mybir.dt.float32
BF16 = mybir.dt.bfloat16
I32 = mybir.dt.int32
AF = mybir.ActivationFunctionType
ALU = mybir.AluOpType
AX = mybir.AxisListType

D = 1024
DEPTH = 4
DFF = 2816
TT = 512
ENGS = ("pe", "act", "dve", "pool", "sp")
NDSEM = 12


class Tok:
    __slots__ = ("w", "r", "name")

    def __init__(self, name=""):
        self.w = None
        self.r = {}
        self.name = name


class Op:
    __slots__ = ("fn", "waits", "signal", "dma")

    def __init__(self, fn):
        self.fn = fn
        self.waits = []
        self.signal = False
        self.dma = None


class KB:
    def __init__(self, nc):
        self.nc = nc
        self.q = {e: [] for e in ENGS}
        self.seen = {e: {} for e in ENGS}
        self.dma_cnt = {e: 0 for e in ENGS}
        self.dma_val = {e: [0] * NDSEM for e in ENGS}
        self.dma_bg = {e: [False] * NDSEM for e in ENGS}
        self.last_real = {e: None for e in ENGS}

    def _deps(self, r, w):
        deps = []
        for t in r:
            if t.w is not None:
                deps.append(t.w)
        for t in w:
            if t.w is not None:
                deps.append(t.w)
            deps.extend(t.r.values())
        return deps

    def _add_waits(self, eng, op, deps, strict=False):
        seen = self.seen[eng]
        best = {}
        for d in deps:
            if d[0] == "e":
                _, e, i = d
                if e == eng and not strict:
                    continue
                key = ("e", e)
                val = i
            else:
                _, e, slot, val = d
                key = ("d", e, slot)
            if seen.get(key, -1) >= val:
                continue
            if best.get(key, -1) < val:
                best[key] = val
        for key, val in best.items():
            seen[key] = val
            if key[0] == "e":
                self.q[key[1]][val].signal = True
                op.waits.append(("e", key[1], val))
            else:
                op.waits.append(("d", key[1], key[2], val))

    def op(self, eng, fn, r=(), w=(), strict=False):
        o = Op(fn)
        self._add_waits(eng, o, self._deps(r, w), strict)
        idx = len(self.q[eng])
        self.q[eng].append(o)
        self.last_real[eng] = idx
        me = ("e", eng, idx)
        for t in r:
            t.r[("e", eng)] = me
        for t in w:
            t.w = me
            t.r = {}
        return me

    def dma(self, eng, out, in_, r=(), w=(), bg=False, **kw):
        n = self.dma_cnt[eng]
        self.dma_cnt[eng] += 1
        slot = n % NDSEM
        prev = self.dma_val[eng][slot]
        val = prev + 16
        self.dma_val[eng][slot] = val
        self.dma_bg[eng][slot] = bg
        o = Op(lambda e: e.dma_start(out=out, in_=in_, **kw))
        o.dma = (slot, val)
        deps = self._deps(r, w)
        if prev > 0:
            deps.append(("d", eng, slot, prev))
        self._add_waits(eng, o, deps)
        self.q[eng].append(o)
        me = ("d", eng, slot, val)
        for t in r:
            t.r[("d", eng, slot)] = me
        for t in w:
            t.w = me
            t.r = {}
        return me

    def barrier(self):
        deps = []
        for e in ENGS:
            if self.last_real[e] is not None:
                deps.append(("e", e, self.last_real[e]))
            for s in range(NDSEM):
                if self.dma_val[e][s] > 0 and not self.dma_bg[e][s]:
                    deps.append(("d", e, s, self.dma_val[e][s]))
        for e in ENGS:
            o = Op(None)
            self._add_waits(e, o, deps)
            self.q[e].append(o)

    def emit(self, final_waits=()):
        nc = self.nc
        fo = Op(None)
        self._add_waits("sp", fo, list(final_waits))
        self.q["sp"].append(fo)
        sigval = {}
        for e in ENGS:
            c = 0
            vals = []
            for o in self.q[e]:
                if o.signal:
                    c += 1
                vals.append(c)
            sigval[e] = vals
        with contextlib.ExitStack() as st:
            esem = {e: st.enter_context(nc.semaphore("s_" + e)) for e in ENGS}
            dsem = {e: [st.enter_context(nc.semaphore("d_%s%d" % (e, i))) for i in range(NDSEM)]
                    for e in ("sp", "act", "pool") if self.dma_cnt[e] > 0}
            block = st.enter_context(nc.Block())

            def run(e, eng):
                for o in self.q[e]:
                    for wt in o.waits:
                        if wt[0] == "e":
                            eng.wait_ge(esem[wt[1]], sigval[wt[1]][wt[2]])
                        else:
                            eng.wait_ge(dsem[wt[1]][wt[2]], wt[3])
                    if o.fn is None:
                        continue
                    ins = o.fn(eng)
                    if o.dma is not None:
                        ins.then_inc(dsem[e][o.dma[0]], 16)
                    elif o.signal:
                        ins.then_inc(esem[e], 1)

            @block.tensor
            def _(eng):
                run("pe", eng)

            @block.scalar
            def _(eng):
                run("act", eng)

            @block.vector
            def _(eng):
                run("dve", eng)

            @block.gpsimd
            def _(eng):
                run("pool", eng)

            @block.sync
            def _(eng):
                run("sp", eng)
        self.ninstr = {e: len(self.q[e]) for e in ENGS}
        self.sigmax = {e: (sigval[e][-1] if sigval[e] else 0) for e in ENGS}


def pack_lhsT(W):
    K, N = W.shape
    return np.ascontiguousarray(W.reshape(K // 128, 128, N // 128, 128).transpose(2, 1, 0, 3))


def pack_rhs(W):
    K, N = W.shape
    return np.ascontiguousarray(W.reshape(K // 128, 128, N).transpose(1, 0, 2))


def colpack(v):
    v = np.asarray(v, np.float32).reshape(-1)
    return np.ascontiguousarray(v.reshape(-1, 128).T)


class ConstPack:
    def __init__(self):
        self.cols = []
        self.off = {}
        self.n = 0

    def add(self, name, arr):
        arr = np.asarray(arr, np.float32)
        assert arr.shape[0] == 128
        self.off[name] = (self.n, arr.shape[1])
        self.cols.append(arr)
        self.n += arr.shape[1]

    def build(self):
        return np.ascontiguousarray(np.concatenate(self.cols, axis=1))


class WPack:
    def __init__(self):
        self.parts = []
        self.off = {}
        self.n = 0

    def add(self, name, arr):
        arr = np.ascontiguousarray(arr, dtype=np.float32)
        self.off[name] = (self.n, arr.shape)
        self.parts.append(arr.reshape(-1))
        self.n += arr.size

    def build(self):
        return np.concatenate(self.parts)


def lambda_init_fn(layer):
    return 0.8 - 0.6 * math.exp(-0.3 * layer)


def rope_perm_lhsT():
    P = np.zeros((128, 128), np.float32)
    for blk in (0, 64):
        for i in range(8):
            P[blk + i, blk + i + 8] = -1.0
            P[blk + 8 + i, blk + i] = 1.0
    return np.ascontiguousarray(P.T)


def inv_freq_col():
    half = 8
    f = np.power(np.float32(500000.0), -np.arange(half, dtype=np.float32) * np.float32(2.0) / np.float32(16)).astype(np.float32)
    col = np.zeros((128, 1), np.float32)
    for blk in (0, 64):
        for i in range(8):
            col[blk + i, 0] = f[i]
            col[blk + 8 + i, 0] = f[i]
    return col


def host_pack(inp):
    cp = ConstPack()
    wp = WPack()
    cp.add("ident", np.eye(128, dtype=np.float32))
    cp.add("ones", np.ones((128, 128), np.float32))
    cp.add("ropeP", rope_perm_lhsT())
    cp.add("invf", inv_freq_col())
    ii = np.arange(128)[:, None]
    jj = np.arange(128)[None, :]
    cp.add("triT", (ii <= jj).astype(np.float32))
    cp.add("negtriT", -(ii <= jj).astype(np.float32))
    cp.add("maskL", np.where(ii > jj, 0.0, -30000.0).astype(np.float32))
    cp.add("maskU", np.where(jj >= ii, 0.0, 30000.0).astype(np.float32))
    cp.add("mstrict", (ii > jj).astype(np.float32))
    cp.add("mbd32", ((ii // 32) == (jj // 32)).astype(np.float32))
    cp.add("moff64", (((ii // 64) == (jj // 64)) & ((ii // 32) != (jj // 32))).astype(np.float32))
    cp.add("moff128", ((ii // 64) != (jj // 64)).astype(np.float32))
    for l in range(DEPTH):
        cp.add("nm%d" % l, colpack(inp["norm_mix"][l]))
        cp.add("nf%d" % l, colpack(inp["norm_ffn"][l]))
        dw = np.asarray(inp["ffn_dw"][l], np.float32)
        cp.add("fdw%d" % l, np.concatenate([colpack(dw[k]) for k in range(3)], axis=1))
        cp.add("fdb%d" % l, colpack(inp["ffn_dw_b"][l]))
    cp.add("nfin", colpack(inp["norm_final"]))
    for j in range(2):
        cp.add("sln%d" % j, colpack(inp["attn_subln"][j]))
        cp.add("lam%d" % j, np.broadcast_to(np.asarray(inp["attn_lambda"][j], np.float32).reshape(1, 256), (128, 256)))
    cp.add("cbin", colpack(inp["conv_b_in"][0]))
    cdw = np.asarray(inp["conv_dw"][0], np.float32)
    cp.add("cdw", np.ascontiguousarray(np.stack([colpack(cdw[k]) for k in range(31)], axis=2).reshape(128, 8 * 31)))
    cp.add("cdwb", colpack(inp["conv_dw_b"][0]))
    cp.add("clg", colpack(inp["conv_ln_g"][0]))
    cp.add("clb", colpack(inp["conv_ln_b"][0]))
    cp.add("cbo", colpack(inp["conv_b_out"][0]))
    gcv = np.asarray(inp["gdn_conv"][0], np.float32)
    cp.add("gconv", np.concatenate([colpack(gcv[k]) for k in range(4)], axis=1))
    cp.add("galog", np.broadcast_to(np.asarray(inp["gdn_a_log"][0], np.float32).reshape(1, 16), (128, 16)))
    cp.add("gdtb", np.broadcast_to(np.asarray(inp["gdn_dt_bias"][0], np.float32).reshape(1, 16), (128, 16)))
    cp.add("gnorm", np.broadcast_to(np.asarray(inp["gdn_norm"][0], np.float32).reshape(1, 128), (128, 128)))

    def add_ffn(l):
        win = pack_lhsT(np.asarray(inp["ffn_w_in"][l], np.float32))
        wp.add("fin%d" % l, np.concatenate([win[:22], win[22:]], axis=3))
        wp.add("fout%d" % l, pack_lhsT(np.asarray(inp["ffn_w_out"][l], np.float32)))

    def add_attn(j):
        w = np.asarray(inp["attn_w_qkv"][j], np.float32)
        wp.add("qk%d" % j, pack_lhsT(w[:, :2048]))
        wp.add("wv%d" % j, pack_rhs(w[:, 2048:]))
        wp.add("wo%d" % j, pack_lhsT(np.asarray(inp["attn_w_o"][j], np.float32)))

    add_attn(0)
    add_ffn(0)
    wp.add("cin", pack_lhsT(np.asarray(inp["conv_w_in"][0], np.float32)))
    wp.add("cout", pack_lhsT(np.asarray(inp["conv_w_out"][0], np.float32)))
    add_ffn(1)
    gw = np.asarray(inp["gdn_w_in"][0], np.float32)
    wp.add("gin", pack_lhsT(gw[:, :6144]))
    wp.add("gba", pack_rhs(gw[:, 6144:]))
    wp.add("gwo", pack_lhsT(np.asarray(inp["gdn_w_o"][0], np.float32)))
    add_ffn(2)
    add_attn(1)
    add_ffn(3)
    return cp, wp


class Prog:
    def __init__(self, S, NSEQ, cp, wp, plan):
        self.S, self.NSEQ = S, NSEQ
        self.NT = S * NSEQ
        self.cp, self.wp, self.plan = cp, wp, plan
        self.ntile = self.NT // TT
        self.tps = S // TT

    def alloc(self, nf32):
        off = self.sb_off
        self.sb_off += int(nf32)
        assert self.sb_off <= self.NBIG, ("sbuf overflow", self.sb_off, self.NBIG)
        return off

    def f32(self, nf32, shape=None):
        off = self.alloc(nf32)
        ap = self.big[:, off:off + nf32]
        return ap

    def bf(self, nbf):
        nf = (nbf + 1) // 2
        off = self.alloc(nf)
        return self.big[:, off:off + nf].bitcast(BF16)

    def phase_reset(self):
        self.kb.barrier()
        self.sb_off = self.sb_persist

    def cc(self, name, i=0, n=1):
        o, w = self.cp.off[name]
        return self.cst[:, o + i:o + i + n]

    def wview(self, name):
        off, shape = self.wp.off[name]
        n = int(np.prod(shape))
        return self.wbf[off:off + n], shape

    def build(self):
        nc = bass.Bass("TRN2", target_bir_lowering=False)
        self.nc = nc
        NT = self.NT
        self.x = nc.dram_tensor("x", [NT, D], F32, kind="ExternalInput").ap()
        self.pos = nc.dram_tensor("pos", [1, NT], I32, kind="ExternalInput").ap()
        self.cst_d = nc.dram_tensor("consts", [128, self.cp.n], F32, kind="ExternalInput").ap()
        self.wts = nc.dram_tensor("wts", [self.wp.n], F32, kind="ExternalInput").ap()
        self.out = nc.dram_tensor("out", [NT, D], F32, kind="ExternalOutput").ap()
        self.wbf = nc.dram_tensor("wbf", [self.wp.n], BF16, kind="Internal").ap()
        self.hT = nc.dram_tensor("hT", [D, NT], F32, kind="Internal").ap()
        self.dram_extra()
        with contextlib.ExitStack() as st:
            self.NBIG = 51800
            self.big = st.enter_context(nc.sbuf_tensor("big", [128, self.NBIG], F32))[:]
            self.ps = [st.enter_context(nc.psum_tensor("ps%d" % i, [128, 512], F32))[:] for i in range(8)]
            self.kb = KB(nc)
            self.sb_off = 0
            self.setup()
            self.sb_persist = self.sb_off
            fin = []
            for ph in self.plan:
                self.phase_reset()
                r = getattr(self, "ph_" + ph[0])(*ph[1:])
                if r:
                    fin += r
            self.kb.emit(final_waits=fin)
        return nc

    def dram_extra(self):
        nc, NT = self.nc, self.NT
        kind = "ExternalOutput" if 'dump' in DBG else "Internal"
        self.QT = nc.dram_tensor("QT", [D, NT], BF16, kind=kind).ap()
        self.KT = nc.dram_tensor("KT", [D, NT], BF16, kind=kind).ap()
        self.Vt = nc.dram_tensor("Vt", [NT, D], BF16, kind=kind).ap()
        self.AOT = nc.dram_tensor("AOT", [2 * D, NT], BF16, kind=kind).ap()
        self.Vtok = nc.dram_tensor("Vtok", [NT, 2 * D], BF16, kind=kind).ap()
        self.ZsT = nc.dram_tensor("ZsT", [2 * D, NT], BF16, kind=kind).ap()
        self.Gtok = nc.dram_tensor("Gtok", [NT, 32], F32, kind=kind).ap()
        if 'dump' in DBG:
            self.dbgf = nc.dram_tensor("dbgf", [128, 4096], F32, kind=kind).ap()
            self.dbgb = nc.dram_tensor("dbgb", [128, 4096], BF16, kind=kind).ap()

    def setup(self):
        kb = self.kb
        ncst = self.cp.n
        self.cst = self.f32(ncst)
        self.t_cst = Tok("cst")
        kb.dma("sp", self.cst, self.cst_d, w=[self.t_cst])
        self.identb = self.bf(128)
        self.onesb = self.bf(128)
        self.ropePb = self.bf(128)
        self.epscol = self.f32(4)

        def mk(e):
            e.tensor_copy(out=self.identb, in_=self.cc("ident", 0, 128))
            e.tensor_copy(out=self.onesb, in_=self.cc("ones", 0, 128))
            e.memset(self.epscol[:, 0:1], 1e-6)
            e.memset(self.epscol[:, 1:2], 1e-5)
            e.memset(self.epscol[:, 2:3], -math.pi)
            e.memset(self.epscol[:, 3:4], 0.0)
            return e.tensor_copy(out=self.ropePb, in_=self.cc("ropeP", 0, 128))
        kb.op("dve", mk, r=[self.t_cst], w=[self.t_cst])
        self.t_w = {}
        CH = 128 * 16384
        self.cast_later = []
        for name, (off, shape) in self.wp.off.items():
            n = int(np.prod(shape))
            t = Tok("w_" + name)
            self.t_w[name] = t
            if name not in ("qk0", "wv0", "wo0"):
                self.cast_later.append((name, off, n, t))
                continue
            o = 0
            while o < n:
                m = min(CH, n - o)
                assert m % 128 == 0
                src = self.wts[off + o:off + o + m].rearrange("(p x) -> p x", p=128)
                dst = self.wbf[off + o:off + o + m].rearrange("(p x) -> p x", p=128)
                kb.dma("pool", dst, src, w=[t], bg=True)
                o += m

    @staticmethod
    def run_pipe(items, depth):
        for t in range(len(items) + depth):
            for k in range(depth - 1, -1, -1):
                i = t - k
                if 0 <= i < len(items) and k < len(items[i]) and items[i][k] is not None:
                    items[i][k]()

    def norm_bufs(self):
        nb = {}
        nb["h"] = [self.f32(8 * TT).rearrange("p (c t) -> p c t", c=8) for _ in range(2)]
        nb["xn"] = [self.bf(8 * TT).rearrange("p (c t) -> p c t", c=8) for _ in range(2)]
        nb["sq"] = self.bf(8 * TT).rearrange("p (c t) -> p c t", c=8)
        nb["rstd"] = self.f32(TT)
        nb["t_h"] = [Tok("h0"), Tok("h1")]
        nb["t_xn"] = [Tok("xn0"), Tok("xn1")]
        nb["t_sq"] = Tok("sq")
        nb["t_rstd"] = Tok("rstd")
        return nb

    def hT_tile(self, n):
        return self.hT[:, n * TT:(n + 1) * TT].rearrange("(c p) t -> p c t", p=128)

    def load_h(self, nb, n, eng="sp"):
        par = n % 2
        self.kb.dma(eng, nb["h"][par], self.hT_tile(n), r=[self.t_hT[n]], w=[nb["t_h"][par]])

    def rmsnorm(self, nb, n, gname, psbank, t_ps, out_f32=None, eps_i=0):
        kb = self.kb
        par = n % 2
        h, xn, sq, rstd = nb["h"][par], nb["xn"][par], nb["sq"], nb["rstd"]
        ps = self.ps[psbank]
        kb.op("act", lambda e: e.activation(out=sq, in_=h, func=AF.Square), r=[nb["t_h"][par]], w=[nb["t_sq"]])

        def mm(e):
            for c in range(8):
                ins = e.matmul(ps, self.onesb, sq[:, c, :], start=(c == 0), stop=(c == 7))
            return ins
        kb.op("pe", mm, r=[nb["t_sq"], self.t_cst], w=[t_ps])
        def rs_(e):
            e.activation(out=rstd, in_=ps, func=AF.Ln, scale=1.0 / D, bias=self.epscol[:, eps_i:eps_i + 1])
            return e.activation(out=rstd, in_=rstd, func=AF.Exp, scale=-0.5)
        kb.op("act", rs_, r=[t_ps, self.t_cst], w=[nb["t_rstd"]])
        dst = xn if out_f32 is None else out_f32

        def sc(e):
            for c in range(8):
                ins = e.scalar_tensor_tensor(out=dst[:, c, :], in0=h[:, c, :], scalar=self.cc(gname, c), in1=rstd,
                                             op0=ALU.mult, op1=ALU.mult)
            return ins
        kb.op("dve", sc, r=[nb["t_h"][par], nb["t_rstd"], self.t_cst], w=[nb["t_xn"][par]])

    def ph_pro(self):
        kb = self.kb
        self.t_hT = [Tok("hT%d" % n) for n in range(self.ntile)]
        xin = [self.f32(4 * D).rearrange("p (b d) -> p b d", b=4) for _ in range(2)]
        hst = [self.f32(8 * TT).rearrange("p (c t) -> p c t", c=8) for _ in range(2)]
        t_xin = [Tok(), Tok()]
        t_hst = [Tok(), Tok()]
        t_ps = [Tok() for _ in range(8)]
        ident = self.cc("ident", 0, 128)
        for n in range(self.ntile):
            par = n % 2
            src = self.x[n * TT:(n + 1) * TT, :].rearrange("(b p) d -> p b d", p=128)
            kb.dma("sp", xin[par], src, w=[t_xin[par]])
            for c in range(8):
                bank = c % 8

                def tr(e, c=c, par=par, bank=bank):
                    for b in range(4):
                        ins = e.transpose(self.ps[bank][:, b * 128:(b + 1) * 128], xin[par][:, b, c * 128:(c + 1) * 128], ident)
                    return ins
                kb.op("pe", tr, r=[t_xin[par], self.t_cst], w=[t_ps[bank]])
                eng = "act" if c % 2 == 0 else "dve"
                if eng == "act":
                    kb.op("act", lambda e, c=c, par=par, bank=bank: e.copy(out=hst[par][:, c, :], in_=self.ps[bank]),
                          r=[t_ps[bank]], w=[t_hst[par]])
                else:
                    kb.op("dve", lambda e, c=c, par=par, bank=bank: e.tensor_copy(out=hst[par][:, c, :], in_=self.ps[bank]),
                          r=[t_ps[bank]], w=[t_hst[par]])
            kb.dma("sp", self.hT_tile(n), hst[par], r=[t_hst[par]], w=[self.t_hT[n]])
        CH = 128 * 16384
        for name, off, nel, t in self.cast_later:
            o = 0
            first = True
            while o < nel:
                m = min(CH, nel - o)
                src = self.wts[off + o:off + o + m].rearrange("(p x) -> p x", p=128)
                dst = self.wbf[off + o:off + o + m].rearrange("(p x) -> p x", p=128)
                kb.dma("pool", dst, src, r=(self.t_hT if first else []), w=[t], bg=True)
                first = False
                o += m

    def ph_fin(self):
        kb = self.kb
        nb = self.norm_bufs()
        xf = [self.f32(8 * TT).rearrange("p (c t) -> p c t", c=8) for _ in range(2)]
        ost = [self.f32(4 * D).rearrange("p (b d) -> p b d", b=4) for _ in range(2)]
        t_ost = [Tok(), Tok()]
        t_ps = [Tok() for _ in range(8)]
        ident = self.cc("ident", 0, 128)
        fin = []
        self.load_h(nb, 0)
        for n in range(self.ntile):
            par = n % 2
            if n + 1 < self.ntile:
                self.load_h(nb, n + 1)
            self.rmsnorm(nb, n, "nfin", 7, t_ps[7], out_f32=xf[par])
            k = 0
            for b in range(4):
                for half in range(2):
                    bank = k % 6
                    k += 1

                    def tr(e, b=b, half=half, bank=bank, par=par):
                        for cc_ in range(4):
                            c = half * 4 + cc_
                            ins = e.transpose(self.ps[bank][:, cc_ * 128:(cc_ + 1) * 128], xf[par][:, c, b * 128:(b + 1) * 128], ident)
                        return ins
                    kb.op("pe", tr, r=[nb["t_xn"][par], self.t_cst], w=[t_ps[bank]])
                    dst = ost[par][:, b, half * 512:(half + 1) * 512]
                    if k % 2 == 0:
                        kb.op("act", lambda e, dst=dst, bank=bank: e.copy(out=dst, in_=self.ps[bank]), r=[t_ps[bank]], w=[t_ost[par]])
                    else:
                        kb.op("dve", lambda e, dst=dst, bank=bank: e.tensor_copy(out=dst, in_=self.ps[bank]), r=[t_ps[bank]], w=[t_ost[par]])
            dstd = self.out[n * TT:(n + 1) * TT, :].rearrange("(b p) d -> p b d", p=128)
            fin.append(kb.dma("sp", dstd, ost[par], r=[t_ost[par]], w=[Tok()]))
        return fin

    def ph_ffn(self, l):
        kb = self.kb
        nb = self.norm_bufs()
        NJ = 22
        act = self.bf(NJ * TT).rearrange("p (j t) -> p j t", j=NJ)
        t_act = [Tok() for _ in range(NJ)]
        NU, NY = 3, 6
        ust = [self.f32(2 * (TT + 2)).rearrange("p (a t) -> p a t", a=2) for _ in range(NU)]
        yb = [self.f32(2 * TT).rearrange("p (a t) -> p a t", a=2) for _ in range(NY)]
        t_ust = [Tok() for _ in range(NU)]
        t_y = [Tok() for _ in range(NY)]
        hb = self.f32(NJ * 4).rearrange("p (j a t) -> p j a t", j=NJ, a=2)
        NWB = 4
        win = [self.bf(2 * 8 * 256).rearrange("p (j k c) -> p j k c", j=2, k=8) for _ in range(NWB)]
        t_win = [Tok() for _ in range(NWB)]
        NOB = 3
        wout = [self.bf(NJ * 128).rearrange("p (k c) -> p k c", k=NJ) for _ in range(NOB)]
        t_wout = [Tok() for _ in range(NOB)]
        t_ps = [Tok() for _ in range(8)]
        wi_ap, _ = self.wview("fin%d" % l)
        wo_ap, _ = self.wview("fout%d" % l)
        wi_v = wi_ap.rearrange("(g j p x) -> g p j x", j=2, p=128, x=8 * 256)
        wo_v = wo_ap.rearrange("(m p x) -> m p x", p=128, x=NJ * 128)
        tw_in, tw_out = self.t_w["fin%d" % l], self.t_w["fout%d" % l]
        dwo, _ = self.cp.off["fdw%d" % l]
        dbo, _ = self.cp.off["fdb%d" % l]
        cst = self.cst

        def dwc(k, ch):
            return cst[:, dwo + k * 44 + ch:dwo + k * 44 + ch + 1]

        def dbc(ch):
            return cst[:, dbo + ch:dbo + ch + 1]
        cnt = {"g": 0, "o": 0, "c": 0}
        items = []

        def rms_item(n):
            def f():
                self.rmsnorm(nb, n, "nf%d" % l, 7, t_ps[7])
            return [f]

        def load_item(n):
            def f():
                self.load_h(nb, n)
            return [f]

        def wout_item(n):
            par = n % 2
            h = nb["h"][par]

            def f():
                for m in range(8):
                    ob = cnt["o"] % NOB
                    cnt["o"] += 1
                    kb.dma("sp", wout[ob].rearrange("p k c -> p (k c)"), wo_v[m], r=[tw_out], w=[t_wout[ob]])
                    bank = 6 + m % 2

                    def mo(e, ob=ob, bank=bank):
                        for k in range(NJ):
                            ins = e.matmul(self.ps[bank], wout[ob][:, k, :], act[:, k, :], start=(k == 0), stop=(k == NJ - 1))
                        return ins
                    kb.op("pe", mo, r=[t_wout[ob]] + t_act, w=[t_ps[bank]])
                    kb.op("dve", lambda e, m=m, bank=bank: e.tensor_tensor(out=h[:, m, :], in0=h[:, m, :], in1=self.ps[bank], op=ALU.add),
                          r=[t_ps[bank]], w=[nb["t_h"][par]])
                kb.dma("pool", self.hT_tile(n), h, r=[nb["t_h"][par]], w=[self.t_hT[n]])
            return [f]

        def chunk_item(n, j):
            par = n % 2
            first = (n % self.tps == 0)
            c = cnt["c"]
            cnt["c"] += 1
            pa, pb = (c % 3) * 2, (c % 3) * 2 + 1
            ub = c % NU
            yi = c % NY
            u, y = ust[ub], yb[yi]
            xn = nb["xn"][par]
            jj = j % 2
            if jj == 0:
                wbx = cnt["g"] % NWB
                cnt["g"] += 1
            else:
                wbx = (cnt["g"] - 1) % NWB
            g = j // 2

            def s0():
                if jj == 0:
                    kb.dma("sp", win[wbx].rearrange("p j k c -> p j (k c)"), wi_v[g], r=[tw_in], w=[t_win[wbx]])

                def mm(e):
                    for a, bank in ((0, pa), (1, pb)):
                        for k in range(8):
                            ins = e.matmul(self.ps[bank], win[wbx][:, jj, k, a * 128:(a + 1) * 128], xn[:, k, :],
                                           start=(k == 0), stop=(k == 7))
                    return ins
                kb.op("pe", mm, r=[t_win[wbx], nb["t_xn"][par]], w=[t_ps[pa], t_ps[pb]])

            def s1():
                if first:
                    kb.op("act", lambda e: e.activation(out=u[:, :, 0:2], in_=u[:, :, 0:2], func=AF.Copy, scale=0.0), w=[t_ust[ub]])
                else:
                    kb.op("act", lambda e: e.copy(out=u[:, :, 0:2], in_=hb[:, j, :, :]), w=[t_ust[ub]])

                def ev(e):
                    for a, bank in ((0, pa), (1, pb)):
                        ch = j + 22 * a
                        e.copy(out=u[:, a, 2:2 + TT], in_=self.ps[bank])
                        ins = e.activation(out=y[:, a, :], in_=self.ps[bank], func=AF.Identity, scale=dwc(2, ch), bias=dbc(ch))
                    return ins
                kb.op("act", ev, r=[t_ps[pa], t_ps[pb], self.t_cst], w=[t_ust[ub], t_y[yi]])
                kb.op("act", lambda e: e.copy(out=hb[:, j, :, :], in_=u[:, :, TT:TT + 2]), r=[t_ust[ub]])

            def s2():
                def cv(e):
                    for a in range(2):
                        ch = j + 22 * a
                        e.scalar_tensor_tensor(out=y[:, a, :], in0=u[:, a, 1:1 + TT], scalar=dwc(1, ch), in1=y[:, a, :], op0=ALU.mult, op1=ALU.add)
                        ins = e.scalar_tensor_tensor(out=y[:, a, :], in0=u[:, a, 0:TT], scalar=dwc(0, ch), in1=y[:, a, :], op0=ALU.mult, op1=ALU.add)
                    return ins
                kb.op("dve", cv, r=[t_ust[ub], self.t_cst], w=[t_y[yi]])

            def s3():
                kb.op("act", lambda e: e.activation(out=y[:, 0, :], in_=y[:, 0, :], func=AF.Silu), r=[t_y[yi]], w=[t_y[yi]])

            def s4():
                kb.op("pool", lambda e: e.tensor_tensor(out=act[:, j, :], in0=y[:, 0, :], in1=y[:, 1, :], op=ALU.mult), r=[t_y[yi]], w=[t_act[j]])
            return [s0, s1, s2, s3, s4]

        items.append(load_item(0))
        items.append(rms_item(0))
        for n in range(self.ntile):
            for j in range(NJ):
                items.append(chunk_item(n, j))
                if j == 2 and n > 0:
                    items.append(wout_item(n - 1))
                if j == 4 and n + 1 < self.ntile:
                    items.append(load_item(n + 1))
                if j == 10 and n + 1 < self.ntile:
                    items.append(rms_item(n + 1))
        for _ in range(4):
            items.append([])
        items.append(wout_item(self.ntile - 1))
        depth = 5
        for t in range(len(items) + depth):
            for k in range(depth - 1, -1, -1):
                i = t - k
                if 0 <= i < len(items) and k < len(items[i]):
                    items[i][k]()

    def ph_attnA(self, l, j):
        kb = self.kb
        nb = self.norm_bufs()
        wqk = self.bf(16 * 8 * 128).rearrange("p (m k c) -> p m k c", m=16, k=8)
        wv = self.bf(8 * 1024).rearrange("p (k c) -> p k c", k=8)
        t_wqk, t_wv = Tok(), Tok()
        a, _ = self.wview("qk%d" % j)
        kb.dma("sp", wqk.rearrange("p m k c -> p m (k c)"), a.rearrange("(m p x) -> p m x", p=128, x=1024), r=[self.t_w["qk%d" % j]], w=[t_wqk])
        a, _ = self.wview("wv%d" % j)
        kb.dma("sp", wv.rearrange("p k c -> p (k c)"), a.rearrange("(p x) -> p x", p=128), r=[self.t_w["wv%d" % j]], w=[t_wv])
        posi = self.f32(TT).bitcast(I32)
        ang = self.f32(TT)
        kf = self.f32(TT)
        ki = self.f32(TT).bitcast(I32)
        rr = self.f32(TT)
        msk = self.f32(TT)
        cosT = self.f32(TT)
        sinT = self.f32(TT)
        t_pos, t_tab, t_tmp = Tok(), Tok(), Tok()
        qst = [self.bf(16 * TT).rearrange("p (c t) -> p c t", c=16) for _ in range(2)]
        vst = [self.bf(4 * 1024).rearrange("p (b c) -> p b c", b=4) for _ in range(2)]
        t_qst = [Tok(), Tok()]
        t_vst = [Tok(), Tok()]
        t_ps = [Tok() for _ in range(8)]
        TWO_PI = 2.0 * math.pi
        C1 = 6.28125
        C2 = TWO_PI - C1
        self.t_QT = [Tok() for _ in range(self.ntile)]
        cosT2 = [cosT, self.f32(TT)]
        sinT2 = [sinT, self.f32(TT)]
        t_tab2 = [Tok(), Tok()]
        qraw = [self.bf(TT) for _ in range(3)]
        t1 = [self.f32(TT) for _ in range(3)]
        t2_one = self.f32(TT)
        t2 = [t2_one, t2_one, t2_one]
        t_qraw = [Tok() for _ in range(3)]
        t_t1 = [Tok() for _ in range(3)]
        cnt = {"c": 0, "v": 0}

        def tab_item(n):
            par = n % 2
            cT, sT, tt = cosT2[par], sinT2[par], t_tab2[par]

            def f():
                kb.dma("sp", posi, self.pos[0:1, n * TT:(n + 1) * TT].to_broadcast([128, TT]), w=[t_pos])

                def tab(e):
                    e.tensor_copy(out=ang, in_=posi)
                    e.tensor_scalar(out=ang, in0=ang, scalar1=self.cc("invf"), scalar2=None, op0=ALU.mult)
                    for which, dst in ((0, sT), (1, cT)):
                        src = ang
                        if which == 1:
                            e.tensor_scalar(out=rr, in0=ang, scalar1=float(math.pi / 2), scalar2=None, op0=ALU.add)
                            src = rr
                        e.tensor_scalar(out=kf, in0=src, scalar1=float(1.0 / TWO_PI), scalar2=None, op0=ALU.mult)
                        e.tensor_copy(out=ki, in_=kf)
                        e.tensor_copy(out=kf, in_=ki)
                        e.scalar_tensor_tensor(out=rr, in0=kf, scalar=-C1, in1=src, op0=ALU.mult, op1=ALU.add)
                        e.scalar_tensor_tensor(out=rr, in0=kf, scalar=-C2, in1=rr, op0=ALU.mult, op1=ALU.add)
                        e.tensor_single_scalar(out=msk, in_=rr, scalar=float(math.pi), op=ALU.is_gt)
                        e.scalar_tensor_tensor(out=rr, in0=msk, scalar=-TWO_PI, in1=rr, op0=ALU.mult, op1=ALU.add)
                        e.tensor_single_scalar(out=msk, in_=rr, scalar=float(-math.pi), op=ALU.is_lt)
                        ins = e.scalar_tensor_tensor(out=dst, in0=msk, scalar=TWO_PI, in1=rr, op0=ALU.mult, op1=ALU.add)
                    return ins
                kb.op("dve", tab, r=[t_pos, self.t_cst], w=[tt, t_tmp])

                def sinact(e):
                    e.activation(out=sT, in_=sT, func=AF.Sin)
                    return e.activation(out=cT, in_=cT, func=AF.Sin)
                kb.op("act", sinact, r=[tt], w=[tt])
            return [f]

        def qk_item(n, c):
            par = n % 2
            xn = nb["xn"][par]
            cT, sT, tt = cosT2[par], sinT2[par], t_tab2[par]
            i = cnt["c"]
            cnt["c"] += 1
            b1 = i % 3
            b2 = 3 + i % 2
            rb = i % 3
            sc = 0.125 if c < 8 else 1.0

            def s0():
                def mm(e):
                    for k in range(8):
                        ins = e.matmul(self.ps[b1], wqk[:, c, k, :], xn[:, k, :], start=(k == 0), stop=(k == 7))
                    return ins
                kb.op("pe", mm, r=[t_wqk, nb["t_xn"][par]], w=[t_ps[b1]])

            def s1():
                kb.op("act", lambda e: e.copy(out=qraw[rb], in_=self.ps[b1]), r=[t_ps[b1]], w=[t_qraw[rb], t_ps[b1]])

            def s2():
                kb.op("pe", lambda e: e.matmul(self.ps[b2], self.ropePb, qraw[rb], start=True, stop=True),
                      r=[t_qraw[rb], self.t_cst], w=[t_ps[b2]])
                kb.op("dve", lambda e: e.scalar_tensor_tensor(out=t1[rb], in0=self.ps[b1], scalar=sc, in1=cT, op0=ALU.mult, op1=ALU.mult),
                      r=[tt], w=[t_t1[rb], t_ps[b1]])

            def s3():
                def fin(e):
                    e.scalar_tensor_tensor(out=t2[rb], in0=self.ps[b2], scalar=sc, in1=sT, op0=ALU.mult, op1=ALU.mult)
                    return e.tensor_tensor(out=qst[par][:, c, :], in0=t1[rb], in1=t2[rb], op=ALU.add)
                kb.op("dve", fin, r=[t_ps[b2], tt, t_t1[rb]], w=[t_t1[rb], t_qst[par]])
            return [s0, s1, s2, s3]

        def v_item(n, b, half):
            par = n % 2
            xn = nb["xn"][par]
            bank = 5 + cnt["v"] % 2
            cnt["v"] += 1

            def s0():
                def mv(e):
                    for k in range(8):
                        ins = e.matmul(self.ps[bank], xn[:, k, b * 128:(b + 1) * 128], wv[:, k, half * 512:(half + 1) * 512],
                                       start=(k == 0), stop=(k == 7))
                    return ins
                kb.op("pe", mv, r=[t_wv, nb["t_xn"][par]], w=[t_ps[bank]])

            def s1():
                kb.op("act", lambda e: e.copy(out=vst[par][:, b, half * 512:(half + 1) * 512], in_=self.ps[bank]), r=[t_ps[bank]], w=[t_vst[par]])
            return [s0, s1]

        def store_item(n):
            par = n % 2
            cols = slice(n * TT, (n + 1) * TT)

            def f():
                kb.dma("sp", self.QT[:, cols].rearrange("(c p) t -> p c t", p=128), qst[par][:, 0:8, :], r=[t_qst[par]], w=[self.t_QT[n]])
                kb.dma("sp", self.KT[:, cols].rearrange("(c p) t -> p c t", p=128), qst[par][:, 8:16, :], r=[t_qst[par]], w=[Tok()])
                kb.dma("sp", self.Vt[cols, :].rearrange("(b p) e -> p b e", p=128), vst[par], r=[t_vst[par]], w=[Tok()])
            return [None, None, None, None, f]

        items = [[lambda: self.load_h(nb, 0)], [lambda: self.rmsnorm(nb, 0, "nm%d" % l, 7, t_ps[7])], tab_item(0)]
        for n in range(self.ntile):
            for c in range(16):
                items.append(qk_item(n, c))
                if c == 3 and n + 1 < self.ntile:
                    items.append([lambda n=n: self.load_h(nb, n + 1)])
                if c == 8 and n + 1 < self.ntile:
                    items.append([lambda n=n: self.rmsnorm(nb, n + 1, "nm%d" % l, 7, t_ps[7])])
                if c == 10 and n + 1 < self.ntile:
                    items.append(tab_item(n + 1))
            for b in range(4):
                for half in range(2):
                    items.append(v_item(n, b, half))
            items.append(store_item(n))
        self.run_pipe(items, 5)

    def ph_attnB(self, l, j):
        kb = self.kb
        S, NSEQ = self.S, self.NSEQ
        nblk = S // 128
        tq = S // TT
        li = lambda_init_fn(l)
        qT = [self.bf(S) for _ in range(2)]
        kT = [self.bf(S) for _ in range(2)]
        vv = [self.bf(nblk * 128).rearrange("p (b e) -> p b e", b=nblk) for _ in range(2)]
        t_qq = [Tok(), Tok()]
        t_kk = [Tok(), Tok()]
        t_vv = [Tok(), Tok()]
        pT = [self.bf(2 * TT).rearrange("p (m t) -> p m t", m=2) for _ in range(2)]
        t_pT = [Tok(), Tok()]
        t_S = [Tok(), Tok()]
        t_acc = Tok()
        rrec = self.f32(2 * TT).rearrange("p (m t) -> p m t", m=2)
        of = self.f32(TT)
        o1f = self.f32(TT)
        sqb = self.bf(TT)
        rstd = self.f32(TT)
        aob = [self.bf(TT) for _ in range(2)]
        t_fin = Tok()
        t_sq = Tok()
        t_rstd = Tok()
        t_ao = [Tok(), Tok()]
        small = self.f32(128 + 8)
        t_small = Tok()
        lp = self.cc("lam%d" % j, 0, 256)
        neglam = small[:, 130:131]
        gcol = small[:, 131:132]

        lp4 = lp.rearrange("p (a b d) -> p a b d", a=2, b=2)
        kb.op("pool", lambda e: e.tensor_tensor(out=small[:, 0:128].rearrange("p (a d) -> p a d", a=2), in0=lp4[:, :, 0, :], in1=lp4[:, :, 1, :], op=ALU.mult),
              r=[self.t_cst], w=[t_small])
        kb.op("dve", lambda e: e.reduce_sum(out=small[:, 128:130], in_=small[:, 0:128].rearrange("p (a d) -> p a d", a=2), axis=AX.X),
              r=[t_small], w=[t_small])
        kb.op("act", lambda e: e.activation(out=small[:, 128:130], in_=small[:, 128:130], func=AF.Exp), r=[t_small], w=[t_small])

        def lam2(e):
            e.scalar_tensor_tensor(out=neglam, in0=small[:, 129:130], scalar=float(-li), in1=small[:, 128:129], op0=ALU.add, op1=ALU.subtract)
            return e.tensor_scalar(out=gcol, in0=self.cc("sln%d" % j), scalar1=float(1.0 - li), scalar2=None, op0=ALU.mult)
        kb.op("dve", lam2, r=[t_small, self.t_cst], w=[t_small])
        if 'dump' in DBG:
            kb.dma("sp", self.dbgf[:, 2048:2048 + 136], small, r=[t_small], w=[Tok()])
            kb.dma("sp", self.dbgf[:, 2304:2304 + 256], lp, r=[self.t_cst], w=[Tok()])

        units = [(s, h) for s in range(NSEQ) for h in range(8)]

        def load(u):
            s, h = units[u]
            par = u % 2
            rows = slice(h * 128, (h + 1) * 128)
            cols = slice(s * S, (s + 1) * S)
            rq = [self.t_QT[n] for n in range(s * tq, (s + 1) * tq)]
            kb.dma("sp", qT[par], self.QT[rows, cols], r=rq, w=[t_qq[par]])
            kb.dma("sp", kT[par], self.KT[rows, cols], r=rq, w=[t_kk[par]])
            kb.dma("sp", vv[par], self.Vt[cols, rows].rearrange("(b p) e -> p b e", p=128), r=rq, w=[t_vv[par]])
        self.t_AO = [Tok() for _ in range(self.ntile)]
        pending = []
        fi = [0]

        O0s = self.f32(TT)
        O1s = self.f32(TT)
        lnl = self.f32(2 * TT).rearrange("p (m t) -> p m t", m=2)
        t_os, t_lnl = Tok(), Tok()

        def finalize_a(u, qt):
            def fa(e):
                e.activation(out=lnl[:, 0, :], in_=self.ps[6], func=AF.Ln)
                return e.activation(out=lnl[:, 1, :], in_=self.ps[7], func=AF.Ln)
            kb.op("act", fa, r=[t_acc], w=[t_lnl])

            def fd(e):
                e.tensor_copy(out=O0s, in_=self.ps[4])
                return e.tensor_copy(out=O1s, in_=self.ps[5])
            kb.op("dve", fd, r=[t_acc], w=[t_os])
            kb.op("act", lambda e: e.activation(out=rrec, in_=lnl, func=AF.Exp, scale=-1.0), r=[t_lnl], w=[t_lnl])

            def f1(e):
                e.tensor_tensor(out=of, in0=O0s, in1=rrec[:, 0, :], op=ALU.mult)
                e.tensor_tensor(out=o1f, in0=O1s, in1=rrec[:, 1, :], op=ALU.mult)
                return e.scalar_tensor_tensor(out=of, in0=o1f, scalar=neglam, in1=of, op0=ALU.mult, op1=ALU.add)
            kb.op("dve", f1, r=[t_os, t_lnl, t_small], w=[t_fin, t_os])
            kb.op("act", lambda e: e.activation(out=sqb, in_=of, func=AF.Square), r=[t_fin], w=[t_sq])
            pending.append((u, qt))

        def finalize_b(set_):
            if not pending:
                return
            u, qt = pending.pop(0)
            s, h = units[u]
            bank = 2 * set_
            kb.op("pe", lambda e: e.matmul(self.ps[bank], self.onesb, sqb, start=True, stop=True), r=[t_sq, self.t_cst], w=[t_S[set_]])

            def fr(e):
                e.activation(out=rstd, in_=self.ps[bank], func=AF.Ln, scale=1.0 / 128, bias=self.epscol[:, 1:2])
                return e.activation(out=rstd, in_=rstd, func=AF.Exp, scale=-0.5)
            kb.op("act", fr, r=[t_S[set_], self.t_cst], w=[t_rstd])
            ab = fi[0] % 2
            fi[0] += 1
            kb.op("dve", lambda e: e.scalar_tensor_tensor(out=aob[ab], in0=of, scalar=gcol, in1=rstd, op0=ALU.mult, op1=ALU.mult),
                  r=[t_rstd, t_fin, t_small], w=[t_ao[ab]])
            n = s * tq + qt
            kb.dma("sp", self.AOT[h * 128:(h + 1) * 128, n * TT:(n + 1) * TT], aob[ab], r=[t_ao[ab]], w=[self.t_AO[n]])

        load(0)
        for u in range(len(units)):
            par = u % 2
            if u + 1 < len(units):
                load(u + 1)
            q_, k_, v_ = qT[par], kT[par], vv[par]
            for qt in range(tq):
                nk = 4 * qt + 4

                def c0_of(kbk):
                    jd = kbk - 4 * qt
                    return 128 * jd if jd > 0 else 0

                def issue_S(kbk, qt=qt, q_=q_, k_=k_, par=par):
                    set_ = kbk % 2
                    c0 = c0_of(kbk)

                    def f(e):
                        for m in range(2):
                            ins = e.matmul(self.ps[2 * set_ + m][:, c0:TT], k_[m * 64:(m + 1) * 64, kbk * 128:(kbk + 1) * 128],
                                           q_[m * 64:(m + 1) * 64, qt * TT + c0:(qt + 1) * TT], start=True, stop=True)
                        return ins
                    kb.op("pe", f, r=[t_qq[par], t_kk[par]], w=[t_S[set_]])

                    def ex(e):
                        for m in range(2):
                            ins = e.activation(out=pT[set_][:, m, c0:TT], in_=self.ps[2 * set_ + m][:, c0:TT], func=AF.Exp)
                        return ins
                    kb.op("act", ex, r=[t_S[set_]], w=[t_pT[set_]])
                    if kbk >= 4 * qt:
                        kb.op("dve", lambda e: e.memset(pT[set_][64:128, :, c0:c0 + 64], 0.0), w=[t_pT[set_]])
                    if 'dump' in DBG and u == 1 and qt == 0:
                        kb.dma("sp", self.dbgb[:, kbk * 1024:(kbk + 1) * 1024], pT[set_].rearrange("p m t -> p (m t)"), r=[t_pT[set_]], w=[Tok()])

                def issue_PV(kbk, qt=qt, nk=nk, v_=v_, par=par):
                    set_ = kbk % 2
                    c0 = c0_of(kbk)

                    def f(e):
                        for m in range(2):
                            e.matmul(self.ps[4 + m][:, c0:TT], v_[:, kbk, :], pT[set_][:, m, c0:TT], start=(kbk == 0), stop=(kbk == nk - 1))
                        for m in range(2):
                            ins = e.matmul(self.ps[6 + m][:, c0:TT], self.onesb, pT[set_][:, m, c0:TT], start=(kbk == 0), stop=(kbk == nk - 1))
                        return ins
                    kb.op("pe", f, r=[t_pT[set_], t_vv[par], self.t_cst], w=[t_acc])
                issue_S(0)
                for kbk in range(nk):
                    if kbk + 1 < nk:
                        issue_S(kbk + 1)
                    if kbk == 2:
                        finalize_b(0)
                    issue_PV(kbk)
                finalize_a(u, qt)
        self.kb.barrier() if False else None
        while pending:
            finalize_b(0)

    def ph_proj(self, wname, Kc, bias=None):
        kb = self.kb
        hb_ = [self.f32(8 * TT).rearrange("p (c t) -> p c t", c=8) for _ in range(2)]
        src = [self.bf(Kc * TT).rearrange("p (k t) -> p k t", k=Kc) for _ in range(2)]
        w = self.bf(8 * Kc * 128).rearrange("p (m k c) -> p m k c", m=8, k=Kc)
        t_h = [Tok(), Tok()]
        t_src = [Tok(), Tok()]
        t_w = Tok()
        t_ps = [Tok() for _ in range(8)]
        a, _ = self.wview(wname)
        kb.dma("sp", w.rearrange("p m k c -> p m (k c)"), a.rearrange("(m p x) -> p m x", p=128, x=Kc * 128), r=[self.t_w[wname]], w=[t_w])

        def load(n):
            par = n % 2
            kb.dma("sp", hb_[par], self.hT_tile(n), r=[self.t_hT[n]], w=[t_h[par]])
            kb.dma("sp", src[par], self.AOT[0:Kc * 128, n * TT:(n + 1) * TT].rearrange("(k p) t -> p k t", p=128), r=[self.t_AO[n]], w=[t_src[par]])
        load(0)
        for n in range(self.ntile):
            par = n % 2
            if n + 1 < self.ntile:
                load(n + 1)
            for m in range(8):
                bank = m % 4

                def mm(e, m=m, bank=bank, par=par):
                    for k in range(Kc):
                        ins = e.matmul(self.ps[bank], w[:, m, k, :], src[par][:, k, :], start=(k == 0), stop=(k == Kc - 1))
                    return ins
                kb.op("pe", mm, r=[t_w, t_src[par]], w=[t_ps[bank]])
                if bias is None:
                    kb.op("dve", lambda e, m=m, bank=bank, par=par: e.tensor_tensor(out=hb_[par][:, m, :], in0=hb_[par][:, m, :], in1=self.ps[bank], op=ALU.add),
                          r=[t_ps[bank]], w=[t_h[par]])
                else:
                    kb.op("dve", lambda e, m=m, bank=bank, par=par: e.scalar_tensor_tensor(out=hb_[par][:, m, :], in0=self.ps[bank], scalar=self.cc(bias, m), in1=hb_[par][:, m, :], op0=ALU.add, op1=ALU.add),
                          r=[t_ps[bank], self.t_cst], w=[t_h[par]])
            kb.dma("sp", self.hT_tile(n), hb_[par], r=[t_h[par]], w=[self.t_hT[n]])


    def ph_conf(self, l):
        kb = self.kb
        KC = 31
        H = KC - 1
        h1 = self.f32(8 * TT).rearrange("p (c t) -> p c t", c=8)
        xn1 = self.bf(8 * TT).rearrange("p (c t) -> p c t", c=8)
        sq1 = self.bf(8 * TT).rearrange("p (c t) -> p c t", c=8)
        t_h1, t_xn1 = Tok(), Tok()
        nb = {"h": [h1, h1], "xn": [xn1, xn1], "sq": sq1, "rstd": self.f32(TT), "t_h": [t_h1, t_h1],
              "t_xn": [t_xn1, t_xn1], "t_sq": Tok(), "t_rstd": Tok()}
        win = self.bf(16 * 8 * 128).rearrange("p (m k c) -> p m k c", m=16, k=8)
        wout = self.bf(8 * 8 * 128).rearrange("p (m k c) -> p m k c", m=8, k=8)
        diag = self.bf(8 * KC * 128).rearrange("p (c k q) -> p c k q", c=8, k=KC)
        glu = self.bf(8 * (TT + H)).rearrange("p (c t) -> p c t", c=8)
        y = self.f32(8 * TT).rearrange("p (c t) -> p c t", c=8)
        sb = self.bf(8 * TT).rearrange("p (c t) -> p c t", c=8)
        sg = [self.f32(TT) for _ in range(2)]
        mean = self.f32(TT)
        rs = self.f32(TT)
        mr = self.f32(TT)
        t_win, t_wout, t_diag, t_glu, t_y, t_s, t_stat = Tok(), Tok(), Tok(), Tok(), Tok(), Tok(), Tok()
        t_sg = [Tok(), Tok()]
        t_ps = [Tok() for _ in range(8)]
        a, _ = self.wview("cin")
        kb.dma("sp", win.rearrange("p m k c -> p m (k c)"), a.rearrange("(m p x) -> p m x", p=128, x=1024), r=[self.t_w["cin"]], w=[t_win])
        a, _ = self.wview("cout")
        kb.dma("sp", wout.rearrange("p m k c -> p m (k c)"), a.rearrange("(m p x) -> p m x", p=128, x=1024), r=[self.t_w["cout"]], w=[t_wout])
        ident = self.cc("ident", 0, 128)
        for c in range(8):
            kb.op("dve", lambda e, c=c: e.tensor_tensor(out=diag[:, c, :, :], in0=ident.unsqueeze(1).to_broadcast([128, KC, 128]),
                                                          in1=self.cc("cdw", c * KC, KC).unsqueeze(2).to_broadcast([128, KC, 128]), op=ALU.mult),
                  r=[self.t_cst], w=[t_diag])
        ybf, ysq = xn1, sq1
        for n in range(self.ntile):
            first = (n % self.tps == 0)
            self.load_h(nb, n)
            self.rmsnorm(nb, n, "nm%d" % l, 6, t_ps[6])
            if first:
                kb.op("pool", lambda e: e.memset(glu[:, :, 0:H], 0.0), w=[t_glu])
            for c in range(8):
                pa, pg = (c % 2) * 2, (c % 2) * 2 + 1
                si = c % 2

                def mm(e, c=c, pa=pa, pg=pg):
                    for (mch, bank) in ((c, pa), (8 + c, pg)):
                        for k in range(8):
                            ins = e.matmul(self.ps[bank], win[:, mch, k, :], xn1[:, k, :], start=(k == 0), stop=(k == 7))
                    return ins
                kb.op("pe", mm, r=[t_win, t_xn1], w=[t_ps[pa], t_ps[pg]])
                kb.op("act", lambda e, c=c, pg=pg, si=si: e.activation(out=sg[si], in_=self.ps[pg], func=AF.Sigmoid, bias=self.cc("cbin", 8 + c)),
                      r=[t_ps[pg], self.t_cst], w=[t_sg[si]])
                kb.op("dve", lambda e, c=c, pa=pa, si=si: e.scalar_tensor_tensor(out=glu[:, c, H:H + TT], in0=self.ps[pa], scalar=self.cc("cbin", c), in1=sg[si],
                                                                                 op0=ALU.add, op1=ALU.mult),
                      r=[t_ps[pa], t_sg[si], self.t_cst], w=[t_glu])
            for c in range(8):
                bank = 4 + c % 2

                NPE = 19

                def cv(e, c=c, bank=bank):
                    for k in range(NPE):
                        ins = e.matmul(self.ps[bank], diag[:, c, k, :], glu[:, c, k:k + TT], start=(k == 0), stop=(k == NPE - 1))
                    return ins
                kb.op("pe", cv, r=[t_diag, t_glu], w=[t_ps[bank]])
                kb.op("act", lambda e, c=c, bank=bank: e.activation(out=y[:, c, :], in_=self.ps[bank], func=AF.Identity, bias=self.cc("cdwb", c)),
                      r=[t_ps[bank], self.t_cst], w=[t_y])

                def dv(e, c=c):
                    for k in range(NPE, KC):
                        ins = e.scalar_tensor_tensor(out=y[:, c, :], in0=glu[:, c, k:k + TT], scalar=self.cc("cdw", c * KC + k), in1=y[:, c, :],
                                                     op0=ALU.mult, op1=ALU.add)
                    return ins
                kb.op("dve", dv, r=[t_glu, self.t_cst], w=[t_y])

                def ev(e, c=c):
                    e.activation(out=ybf[:, c, :], in_=y[:, c, :], func=AF.Copy)
                    return e.activation(out=ysq[:, c, :], in_=y[:, c, :], func=AF.Square)
                kb.op("act", ev, r=[t_y], w=[t_xn1, nb["t_sq"]])
            kb.op("pool", lambda e: e.tensor_copy(out=glu[:, :, 0:H], in_=glu[:, :, TT:TT + H]), w=[t_glu])

            def st(e):
                for c in range(8):
                    e.matmul(self.ps[6], self.onesb, ybf[:, c, :], start=(c == 0), stop=(c == 7))
                for c in range(8):
                    ins = e.matmul(self.ps[7], self.onesb, ysq[:, c, :], start=(c == 0), stop=(c == 7))
                return ins
            kb.op("pe", st, r=[t_xn1, nb["t_sq"], self.t_cst], w=[t_ps[6], t_ps[7]])

            def s1(e):
                e.tensor_scalar(out=mean, in0=self.ps[6], scalar1=1.0 / D, scalar2=None, op0=ALU.mult)
                e.tensor_tensor(out=mr, in0=mean, in1=mean, op=ALU.mult)
                return e.scalar_tensor_tensor(out=rs, in0=self.ps[7], scalar=1.0 / D, in1=mr, op0=ALU.mult, op1=ALU.subtract)
            kb.op("dve", s1, r=[t_ps[6], t_ps[7]], w=[t_stat])
            def sa(e):
                e.activation(out=rs, in_=rs, func=AF.Ln, bias=self.epscol[:, 1:2])
                return e.activation(out=rs, in_=rs, func=AF.Exp, scale=-0.5)
            kb.op("act", sa, r=[t_stat, self.t_cst], w=[t_stat])
            kb.op("dve", lambda e: e.tensor_tensor(out=mr, in0=mean, in1=rs, op=ALU.mult), r=[t_stat], w=[t_stat])
            for c in range(8):
                def nrm(e, c=c):
                    e.tensor_tensor(out=y[:, c, :], in0=y[:, c, :], in1=rs, op=ALU.mult)
                    return e.tensor_tensor(out=y[:, c, :], in0=y[:, c, :], in1=mr, op=ALU.subtract)
                kb.op("dve", nrm, r=[t_stat], w=[t_y])
                kb.op("act", lambda e, c=c: e.activation(out=sb[:, c, :], in_=y[:, c, :], func=AF.Silu, scale=self.cc("clg", c), bias=self.cc("clb", c)),
                      r=[t_y, self.t_cst], w=[t_s])
            for m in range(8):
                bank = m % 4

                def mo(e, m=m, bank=bank):
                    for k in range(8):
                        ins = e.matmul(self.ps[bank], wout[:, m, k, :], sb[:, k, :], start=(k == 0), stop=(k == 7))
                    return ins
                kb.op("pe", mo, r=[t_wout, t_s], w=[t_ps[bank]])
                kb.op("dve", lambda e, m=m, bank=bank: e.scalar_tensor_tensor(out=h1[:, m, :], in0=self.ps[bank], scalar=self.cc("cbo", m), in1=h1[:, m, :],
                                                                               op0=ALU.add, op1=ALU.add),
                      r=[t_ps[bank], self.t_cst], w=[t_h1])
            kb.dma("sp", self.hT_tile(n), h1, r=[t_h1], w=[self.t_hT[n]])


    def ph_gdnA(self, l):
        kb = self.kb
        nb = self.norm_bufs()
        NWB = 4
        win = [self.bf(4 * 8 * 128).rearrange("p (m k c) -> p m k c", m=4, k=8) for _ in range(NWB)]
        t_win = [Tok() for _ in range(NWB)]
        wba = self.bf(8 * 32).rearrange("p (k c) -> p k c", k=8)
        t_wba = Tok()
        a, _ = self.wview("gba")
        kb.dma("sp", wba.rearrange("p k c -> p (k c)"), a.rearrange("(p x) -> p x", p=128), r=[self.t_w["gba"]], w=[t_wba])
        gi_ap, _ = self.wview("gin")
        gi_v = gi_ap.rearrange("(g m p x) -> g p m x", m=4, p=128, x=1024)
        NU, NY, N3 = 3, 8, 3
        ust = [self.f32(TT + 3) for _ in range(NU)]
        yb = [self.f32(TT) for _ in range(NY)]
        t_ust = [Tok() for _ in range(NU)]
        t_y = [Tok() for _ in range(NY)]
        hb = self.f32(32 * 3).rearrange("p (c t) -> p c t", c=32)
        sqb = [self.bf(TT) for _ in range(N3)]
        rr = [self.f32(TT) for _ in range(N3)]
        t_sq = [Tok() for _ in range(N3)]
        t_rr = [Tok() for _ in range(N3)]
        vs = [self.bf(TT) for _ in range(N3)]
        t_vs = [Tok() for _ in range(N3)]
        qst = self.bf(8 * TT).rearrange("p (c t) -> p c t", c=8)
        kst = self.bf(8 * TT).rearrange("p (c t) -> p c t", c=8)
        ktok = self.bf(4 * 1024).rearrange("p (b c) -> p b c", b=4)
        vtok = self.bf(4 * 2048).rearrange("p (b c) -> p b c", b=4)
        zst = self.bf(16 * TT).rearrange("p (c t) -> p c t", c=16)
        gst = self.f32(4 * 32).rearrange("p (b c) -> p b c", b=4)
        gtmp = self.f32(4 * 16).rearrange("p (b c) -> p b c", b=4)
        negA = self.f32(16)
        t_qst, t_kst, t_ktok, t_vtok, t_zst, t_gst, t_negA = Tok(), Tok(), Tok(), Tok(), Tok(), Tok(), Tok()
        t_ps = [Tok() for _ in range(8)]
        psb = [p.bitcast(BF16) for p in self.ps]
        onecol = self.cc("ones", 0, 1)
        kb.op("act", lambda e: e.activation(out=negA, in_=self.cc("galog", 0, 16), func=AF.Exp), r=[self.t_cst], w=[t_negA])
        kb.op("pool", lambda e: e.tensor_scalar(out=negA, in0=negA, scalar1=-1.0, scalar2=None, op0=ALU.mult), r=[t_negA], w=[t_negA])
        dwo, _ = self.cp.off["gconv"]
        cst = self.cst

        def cw(k, ch):
            return cst[:, dwo + k * 32 + ch:dwo + k * 32 + ch + 1]
        self.t_G = [Tok() for _ in range(self.ntile)]
        cnt = {"g": 0, "c": 0, "t": 0, "n": 0}
        TB = (3, 6)

        def chunk_item(n, ch):
            par = n % 2
            first = (n % self.tps == 0)
            xn = nb["xn"][par]
            c = cnt["c"]
            cnt["c"] += 1
            bank = c % 3
            mm_ = ch % 4
            if mm_ == 0:
                wb = cnt["g"] % NWB
                cnt["g"] += 1
            else:
                wb = (cnt["g"] - 1) % NWB
            g = ch // 4

            def s0():
                if mm_ == 0:
                    kb.dma("sp", win[wb].rearrange("p m k c -> p m (k c)"), gi_v[g], r=[self.t_w["gin"]], w=[t_win[wb]])

                def mm(e):
                    for k in range(8):
                        ins = e.matmul(self.ps[bank], win[wb][:, mm_, k, :], xn[:, k, :], start=(k == 0), stop=(k == 7))
                    return ins
                kb.op("pe", mm, r=[t_win[wb], nb["t_xn"][par]], w=[t_ps[bank]])
            if ch >= 32:
                def s1z():
                    kb.op("act", lambda e: e.activation(out=zst[:, ch - 32, :], in_=self.ps[bank], func=AF.Silu), r=[t_ps[bank]], w=[t_zst])
                return [s0, s1z]
            ub = c % NU
            yi = c % NY
            u, y = ust[ub], yb[yi]

            def s1():
                if first:
                    kb.op("act", lambda e: e.activation(out=u[:, 0:3], in_=u[:, 0:3], func=AF.Copy, scale=0.0), w=[t_ust[ub]])
                else:
                    kb.op("act", lambda e: e.copy(out=u[:, 0:3], in_=hb[:, ch, :]), w=[t_ust[ub]])

                kb.op("act", lambda e: e.copy(out=u[:, 3:3 + TT], in_=self.ps[bank]), r=[t_ps[bank], self.t_cst], w=[t_ust[ub]])
                kb.op("act", lambda e: e.copy(out=hb[:, ch, :], in_=u[:, TT:TT + 3]), r=[t_ust[ub]], strict=True)

            def s2():
                def cv(e):
                    e.tensor_scalar(out=y, in0=u[:, 3:3 + TT], scalar1=cw(3, ch), scalar2=None, op0=ALU.mult)
                    e.scalar_tensor_tensor(out=y, in0=u[:, 2:2 + TT], scalar=cw(2, ch), in1=y, op0=ALU.mult, op1=ALU.add)
                    e.scalar_tensor_tensor(out=y, in0=u[:, 1:1 + TT], scalar=cw(1, ch), in1=y, op0=ALU.mult, op1=ALU.add)
                    return e.scalar_tensor_tensor(out=y, in0=u[:, 0:TT], scalar=cw(0, ch), in1=y, op0=ALU.mult, op1=ALU.add)
                kb.op("dve", cv, r=[t_ust[ub], self.t_cst], w=[t_y[yi]])
            i3 = c % N3
            if ch >= 16:
                tb = 6

                def s3v():
                    kb.op("act", lambda e: e.activation(out=vs[i3], in_=y, func=AF.Silu), r=[t_y[yi]], w=[t_vs[i3]])

                def s4v():
                    def tr(e):
                        for b in range(4):
                            ins = e.transpose(psb[tb][:, b * 128:(b + 1) * 128], vs[i3][:, b * 128:(b + 1) * 128], self.identb)
                        return ins
                    kb.op("pe", tr, r=[t_vs[i3], self.t_cst], w=[t_ps[tb]])

                def s5v():
                    kb.op("dve", lambda e: e.tensor_copy(out=vtok[:, :, (ch - 16) * 128:(ch - 15) * 128],
                                                         in_=psb[tb][:, 0:512].rearrange("p (b c) -> p b c", b=4)),
                          r=[t_ps[tb]], w=[t_vtok])
                return [s0, s1, s2, s3v, s4v, s5v]
            nbk = 4 + cnt["n"] % 2
            cnt["n"] += 1
            scale = float(128 ** -0.5) if ch < 8 else 1.0
            dst = qst[:, ch, :] if ch < 8 else kst[:, ch - 8, :]
            t_dst = t_qst if ch < 8 else t_kst

            def s3():
                def f(e):
                    e.activation(out=y, in_=y, func=AF.Silu)
                    return e.activation(out=sqb[i3], in_=y, func=AF.Square)
                kb.op("act", f, r=[t_y[yi]], w=[t_y[yi], t_sq[i3]])

            def s4():
                kb.op("pe", lambda e: e.matmul(self.ps[nbk], self.onesb, sqb[i3], start=True, stop=True), r=[t_sq[i3], self.t_cst], w=[t_ps[nbk]])

            def s5():
                def f(e):
                    e.activation(out=rr[i3], in_=self.ps[nbk], func=AF.Ln, bias=self.epscol[:, 0:1])
                    return e.activation(out=rr[i3], in_=rr[i3], func=AF.Exp, scale=-0.5)
                kb.op("act", f, r=[t_ps[nbk], self.t_cst], w=[t_rr[i3]])

            def s6():
                kb.op("dve", lambda e: e.scalar_tensor_tensor(out=dst, in0=y, scalar=scale, in1=rr[i3], op0=ALU.mult, op1=ALU.mult),
                      r=[t_rr[i3], t_y[yi]], w=[t_dst])
            if ch < 8:
                return [s0, s1, s2, s3, s4, s5, s6]
            tb = 3

            def s7():
                def trk(e):
                    for b in range(4):
                        ins = e.transpose(psb[tb][:, b * 128:(b + 1) * 128], kst[:, ch - 8, b * 128:(b + 1) * 128], self.identb)
                    return ins
                kb.op("pe", trk, r=[t_kst, self.t_cst], w=[t_ps[tb]])

            def s8():
                kb.op("dve", lambda e: e.tensor_copy(out=ktok[:, :, (ch - 8) * 128:(ch - 7) * 128],
                                                     in_=psb[tb][:, 0:512].rearrange("p (b c) -> p b c", b=4)),
                      r=[t_ps[tb]], w=[t_ktok])
            return [s0, s1, s2, s3, s4, s5, s6, s7, s8]

        def gate_item(n):
            par = n % 2
            xn = nb["xn"][par]
            pg = self.ps[7][:, 0:128].rearrange("p (b c) -> p b c", b=4)

            def g0():
                def mg(e):
                    for b in range(4):
                        for k in range(8):
                            ins = e.matmul(self.ps[7][:, b * 32:(b + 1) * 32], xn[:, k, b * 128:(b + 1) * 128], wba[:, k, :], start=(k == 0), stop=(k == 7))
                    return ins
                kb.op("pe", mg, r=[t_wba, nb["t_xn"][par]], w=[t_ps[7]])

            def g1():
                kb.op("act", lambda e: e.activation(out=gst[:, :, 0:16], in_=pg[:, :, 0:16], func=AF.Sigmoid), r=[t_ps[7]], w=[t_gst, t_ps[7]])
                kb.op("dve", lambda e: e.tensor_tensor(out=gtmp, in0=pg[:, :, 16:32], in1=self.cc("gdtb", 0, 16).unsqueeze(1).to_broadcast([128, 4, 16]), op=ALU.add),
                      r=[self.t_cst], w=[t_gst, t_ps[7]])

            def g2():
                kb.op("act", lambda e: e.activation(out=gtmp, in_=gtmp, func=AF.Exp), r=[t_gst], w=[t_gst])
                kb.op("act", lambda e: e.activation(out=gtmp, in_=gtmp, func=AF.Ln, bias=onecol), r=[t_gst, self.t_cst], w=[t_gst], strict=True)

            def g3():
                kb.op("pool", lambda e: e.tensor_tensor(out=gst[:, :, 16:32], in0=gtmp, in1=negA.unsqueeze(1).to_broadcast([128, 4, 16]), op=ALU.mult),
                      r=[t_gst, t_negA], w=[t_gst])
                cols = slice(n * TT, (n + 1) * TT)
                kb.dma("sp", self.Gtok[cols, :].rearrange("(b p) e -> p b e", p=128), gst, r=[t_gst], w=[self.t_G[n]])
            return [g0, g1, g2, g3]

        def store_item(n):
            cols = slice(n * TT, (n + 1) * TT)

            def f():
                kb.dma("sp", self.QT[:, cols].rearrange("(c p) t -> p c t", p=128), qst, r=[t_qst], w=[Tok()])
                kb.dma("sp", self.KT[:, cols].rearrange("(c p) t -> p c t", p=128), kst, r=[t_kst], w=[Tok()])
                kb.dma("sp", self.Vt[cols, :].rearrange("(b p) e -> p b e", p=128), ktok, r=[t_ktok], w=[Tok()])
                kb.dma("sp", self.Vtok[cols, :].rearrange("(b p) e -> p b e", p=128), vtok, r=[t_vtok], w=[Tok()])
                kb.dma("sp", self.ZsT[:, cols].rearrange("(c p) t -> p c t", p=128), zst, r=[t_zst], w=[Tok()])
            return [None, None, f]

        items = [[lambda: self.load_h(nb, 0)], [lambda: self.rmsnorm(nb, 0, "nm%d" % l, 7, t_ps[7])]]
        for n in range(self.ntile):
            items.append(gate_item(n))
            for ch in range(48):
                items.append(chunk_item(n, ch))
                if ch == 8 and n + 1 < self.ntile:
                    items.append([lambda n=n: self.load_h(nb, n + 1)])
                if ch == 20 and n + 1 < self.ntile:
                    items.append([lambda n=n: self.rmsnorm(nb, n + 1, "nm%d" % l, 7, t_ps[7])])
            items.append(store_item(n))
        self.run_pipe(items, 9)

    def ph_gdnB(self, l):
        kb = self.kb
        S, NSEQ = self.S, self.NSEQ
        NCH = S // 128
        ps = self.ps
        psb = [p.bitcast(BF16) for p in ps]
        H4 = lambda ap: ap.rearrange("p (j d) -> p j d", j=4)
        c_ = self.cc
        identf = c_("ident", 0, 128)
        bc4 = lambda ap: ap.unsqueeze(1).to_broadcast([128, 4, 128])
        col4 = lambda ap: ap.unsqueeze(2).to_broadcast([128, 4, 128])
        self.t_AO = [Tok() for _ in range(self.ntile)]

        def T4(dt="f"):
            return H4(self.f32(512)) if dt == "f" else H4(self.bf(512))

        class St:
            pass
        streams = []
        for si in range(NSEQ):
            st = St()
            st.s = si
            st.pb = (0, 1) if si % 2 == 0 else (2, 3)
            st.rb = (4, 5) if si % 2 == 0 else (6, 7)
            st.gt = self.f32(NCH * 32).rearrange("p (c e) -> p c e", c=NCH)
            st.t_gt = Tok()
            NL = 3
            st.kf = [self.bf(256).rearrange("p (a t) -> p a t", a=2) for _ in range(NL)]
            st.qf = [self.bf(256).rearrange("p (a t) -> p a t", a=2) for _ in range(NL)]
            st.ktk = [self.bf(256).rearrange("p (a d) -> p a d", a=2) for _ in range(NL)]
            st.vtk = [T4("b") for _ in range(NL)]
            st.zs = [T4("b") for _ in range(NL)]
            st.t_ld = [Tok() for _ in range(NL)]
            st.NL = NL
            for nm in ("DLt", "DUt", "ELt", "EUt", "Lf", "PT32", "S32", "osb", "osq"):
                setattr(st, nm, T4())
            for nm in ("Lbd", "Lo1", "Lo2", "LbdTb", "PTb", "Tbdb", "Yb", "vbb", "kbgb", "Sb", "vnb", "onb"):
                setattr(st, nm, T4("b"))
            st.Ab = [T4("b"), T4("b")]
            st.ATb = [T4("b"), T4("b")]
            st.gsm = self.f32(16)
            st.bg = self.f32(4)
            st.ss = self.f32(4)
            st.aTb = [T4("b"), T4("b")]
            st.TTb = [T4("b"), T4("b")]
            st.u32 = [T4(), T4()]
            st.wTb = [T4("b"), T4("b")]
            st.kdecb = [T4("b"), T4("b")]
            st.esm = [self.f32(16), self.f32(16)]
            st.oTb = [T4("b"), T4("b")]
            st.tk = {n: Tok(n) for n in ("DL", "DU", "EL", "EU", "Lf", "Lbd", "Lo1", "Lo2", "LbdT", "PT32", "PTb", "A0", "A1", "AT0", "AT1",
                                         "Tbd", "Y", "vb", "kbg", "gsm", "bg", "S32", "Sb", "vnb", "osb", "osq", "ss", "onb")}
            st.t2 = {n: [Tok(), Tok()] for n in ("aT", "TT", "u", "wT", "kdec", "esm", "oT")}
            streams.append(st)
        t_ps = [Tok() for _ in range(8)]

        def load(st, g4, n):
            li = n % st.NL
            s = st.s
            tc = slice(s * S + n * 128, s * S + (n + 1) * 128)
            t = st.t_ld[li]
            kb.dma("sp", st.kf[li], self.KT[2 * g4 * 128:(2 * g4 + 2) * 128, tc].rearrange("(a p) t -> p a t", p=128), w=[t])
            kb.dma("sp", st.qf[li], self.QT[2 * g4 * 128:(2 * g4 + 2) * 128, tc].rearrange("(a p) t -> p a t", p=128), w=[t])
            kb.dma("sp", st.ktk[li], self.Vt[tc, 2 * g4 * 128:(2 * g4 + 2) * 128].rearrange("p (a d) -> p a d", a=2), w=[t])
            kb.dma("sp", st.vtk[li], self.Vtok[tc, 4 * g4 * 128:(4 * g4 + 4) * 128].rearrange("p (j d) -> p j d", j=4), w=[t])
            kb.dma("sp", st.zs[li], self.ZsT[4 * g4 * 128:(4 * g4 + 4) * 128, tc].rearrange("(j p) t -> p j t", p=128), w=[t])

        def tr4(bank, src):
            def f(e):
                for j in range(4):
                    ins = e.transpose(psb[bank][:, j * 128:(j + 1) * 128], src[:, j, :], self.identb)
                return ins
            return f

        def mm4(bank, lhs, rhs):
            def f(e):
                for j in range(4):
                    ins = e.matmul(ps[bank][:, j * 128:(j + 1) * 128], lhs[:, j, :], rhs[:, j, :], start=True, stop=True)
                return ins
            return f

        def pre(st, g4, n):
            par = n % 2
            li = n % st.NL
            tk, t2 = st.tk, st.t2
            b0, b1 = st.pb
            t_ld = st.t_ld[li]
            kf, qf, ktk, vtk = st.kf[li], st.qf[li], st.ktk[li], st.vtk[li]
            gn = st.gt[:, n, 16 + 4 * g4:16 + 4 * g4 + 4]
            bn = st.gt[:, n, 4 * g4:4 * g4 + 4]
            esm = st.esm[par]

            def p1(e):
                for j in range(4):
                    gb = gn[:, j:j + 1].to_broadcast([128, 128])
                    e.matmul(ps[b0][:, j * 128:(j + 1) * 128], c_("triT", 0, 128), gb, start=True, stop=False)
                    e.matmul(ps[b0][:, j * 128:(j + 1) * 128], gb, c_("negtriT", 0, 128), start=False, stop=True)
                e.matmul(ps[b1][:, 384:388], c_("triT", 0, 128), gn, start=True, stop=True)
                for j in range(4):
                    ins = e.matmul(ps[b1][:, 392 + j:393 + j], gn[:, j:j + 1].to_broadcast([128, 128]), c_("ones", 0, 1), start=True, stop=True)
                return ins
            kb.op("pe", p1, r=[st.t_gt, self.t_cst], w=[t_ps[b0], t_ps[b1]])

            def p2(e):
                e.tensor_tensor(out=st.DLt, in0=H4(ps[b0]), in1=bc4(c_("maskL", 0, 128)), op=ALU.add)
                e.tensor_tensor(out=st.DUt, in0=H4(ps[b0]), in1=bc4(c_("maskU", 0, 128)), op=ALU.add)
                e.tensor_copy(out=st.gsm[:, 0:12], in_=ps[b1][:, 384:396])
                return e.tensor_copy(out=st.gsm[:, 12:16], in_=ps[b0][:, 127:512:128])
            kb.op("dve", p2, r=[self.t_cst], w=[tk["DL"], tk["DU"], tk["gsm"], t_ps[b0], t_ps[b1]])
            yield

            def p3(e):
                e.activation(out=st.ELt, in_=st.DLt, func=AF.Exp)
                e.activation(out=st.EUt, in_=st.DUt, func=AF.Exp, scale=-1.0)
                e.activation(out=esm[:, 0:12], in_=st.gsm[:, 0:12], func=AF.Exp)
                return e.activation(out=esm[:, 12:16], in_=st.gsm[:, 12:16], func=AF.Exp, scale=-1.0)
            kb.op("act", p3, r=[tk["DL"], tk["DU"], tk["gsm"]], w=[tk["EL"], tk["EU"], t2["esm"][par]])
            yield
            kb.op("pool", lambda e: e.tensor_tensor(out=st.bg, in0=bn, in1=esm[:, 0:4], op=ALU.mult), r=[st.t_gt, t2["esm"][par]], w=[tk["bg"]])

            def p11(e):
                e.tensor_tensor(out=st.vbb, in0=vtk, in1=col4(bn), op=ALU.mult)
                k4 = ktk.unsqueeze(2).to_broadcast([128, 2, 2, 128])
                e.tensor_tensor(out=st.kbgb.rearrange("p (a r) d -> p a r d", a=2), in0=k4,
                                in1=st.bg.rearrange("p (a r) -> p a r", a=2).unsqueeze(3).to_broadcast([128, 2, 2, 128]), op=ALU.mult)
                return e.tensor_tensor(out=st.kdecb[par].rearrange("p (a r) d -> p a r d", a=2), in0=k4,
                                       in1=esm[:, 12:16].rearrange("p (a r) -> p a r", a=2).unsqueeze(3).to_broadcast([128, 2, 2, 128]), op=ALU.mult)
            kb.op("pool", p11, r=[t_ld, st.t_gt, tk["bg"], t2["esm"][par]], w=[tk["vb"], tk["kbg"], t2["kdec"][par]])

            def p4(e):
                for a in range(2):
                    e.matmul(ps[b0][:, a * 128:(a + 1) * 128], kf[:, a, :], kf[:, a, :], start=True, stop=True)
                for a in range(2):
                    ins = e.matmul(ps[b0][:, 256 + a * 128:256 + (a + 1) * 128], kf[:, a, :], qf[:, a, :], start=True, stop=True)
                return ins
            kb.op("pe", p4, r=[t_ld], w=[t_ps[b0]])

            def p5(e):
                for j in range(4):
                    a = j // 2
                    e.scalar_tensor_tensor(out=st.Lf[:, j, :], in0=ps[b0][:, a * 128:(a + 1) * 128], scalar=bn[:, j:j + 1], in1=st.ELt[:, j, :],
                                           op0=ALU.mult, op1=ALU.mult)
                mq = ps[b0][:, 256:512].rearrange("p (a d) -> p a d", a=2).unsqueeze(2).to_broadcast([128, 2, 2, 128])
                e.tensor_tensor(out=st.aTb[par].rearrange("p (a r) d -> p a r d", a=2), in0=mq,
                                in1=st.EUt.rearrange("p (a r) d -> p a r d", a=2), op=ALU.mult)
                return e.tensor_tensor(out=st.Lbd, in0=st.Lf, in1=bc4(c_("mbd32", 0, 128)), op=ALU.mult)
            kb.op("dve", p5, r=[tk["EL"], tk["EU"], st.t_gt, self.t_cst], w=[tk["Lf"], t2["aT"][par], tk["Lbd"], t_ps[b0]])
            yield

            def p6(e):
                e.tensor_tensor(out=st.Lo1, in0=st.Lf, in1=bc4(c_("moff64", 0, 128)), op=ALU.mult)
                return e.tensor_tensor(out=st.Lo2, in0=st.Lf, in1=bc4(c_("moff128", 0, 128)), op=ALU.mult)
            kb.op("pool", p6, r=[tk["Lf"], self.t_cst], w=[tk["Lo1"], tk["Lo2"]])
            pT = H4(psb[b1][:, 0:512])
            kb.op("pe", tr4(b1, st.Lbd), r=[tk["Lbd"], self.t_cst], w=[t_ps[b1]])
            kb.op("act", lambda e: e.copy(out=st.LbdTb, in_=pT), r=[t_ps[b1]], w=[tk["LbdT"], t_ps[b1]])
            yield
            kb.op("dve", lambda e: e.scalar_tensor_tensor(out=st.PT32, in0=st.LbdTb, scalar=-1.0, in1=bc4(identf), op0=ALU.mult, op1=ALU.add),
                  r=[tk["LbdT"], self.t_cst], w=[tk["PT32"]])
            kb.op("act", lambda e: e.copy(out=st.PTb, in_=st.PT32), r=[tk["PT32"]], w=[tk["PTb"]])
            yield
            curA, curAT, tA, tAT = st.Lbd, st.LbdTb, tk["Lbd"], tk["LbdT"]
            for lev in range(4):
                i = lev % 2
                nA, nAT, tnA, tnAT = st.Ab[i], st.ATb[i], tk["A%d" % i], tk["AT%d" % i]
                kb.op("pe", mm4(b0, curAT, curA), r=[tA, tAT], w=[t_ps[b0]])
                kb.op("act", lambda e, nA=nA: e.copy(out=nA, in_=H4(ps[b0])), r=[t_ps[b0]], w=[tnA, t_ps[b0]])
                if lev < 3:
                    kb.op("pe", mm4(b1, curA, curAT), r=[tA, tAT], w=[t_ps[b1]])
                    kb.op("dve", lambda e, nAT=nAT: e.tensor_copy(out=nAT, in_=H4(ps[b1])), r=[t_ps[b1]], w=[tnAT, t_ps[b1]])
                yield
                kb.op("pe", mm4(b0, nA, st.PTb), r=[tnA, tk["PTb"]], w=[t_ps[b0]])
                kb.op("dve", lambda e: e.tensor_tensor(out=st.PT32, in0=st.PT32, in1=H4(ps[b0]), op=ALU.add), r=[t_ps[b0]], w=[tk["PT32"], t_ps[b0]])
                kb.op("act", lambda e: e.copy(out=st.PTb, in_=st.PT32), r=[tk["PT32"]], w=[tk["PTb"]])
                yield
                curA, curAT, tA, tAT = nA, nAT, tnA, tnAT
            for (Lo, tLo, last) in ((st.Lo1, tk["Lo1"], False), (st.Lo2, tk["Lo2"], True)):
                kb.op("pe", tr4(b1, st.PTb), r=[tk["PTb"], self.t_cst], w=[t_ps[b1]])
                kb.op("act", lambda e: e.copy(out=st.Tbdb, in_=pT), r=[t_ps[b1]], w=[tk["Tbd"], t_ps[b1]])
                kb.op("pe", mm4(b0, Lo, st.PTb), r=[tLo, tk["PTb"]], w=[t_ps[b0]])
                kb.op("act", lambda e: e.copy(out=st.Yb, in_=H4(ps[b0])), r=[t_ps[b0]], w=[tk["Y"], t_ps[b0]])
                yield
                kb.op("pe", mm4(b1, st.Tbdb, st.Yb), r=[tk["Tbd"], tk["Y"]], w=[t_ps[b1]])
                kb.op("dve", lambda e: e.tensor_tensor(out=st.PT32, in0=st.PT32, in1=H4(ps[b1]), op=ALU.subtract), r=[t_ps[b1]], w=[tk["PT32"], t_ps[b1]])
                if not last:
                    kb.op("act", lambda e: e.copy(out=st.PTb, in_=st.PT32), r=[tk["PT32"]], w=[tk["PTb"]])
                else:
                    kb.op("act", lambda e: e.copy(out=st.TTb[par], in_=st.PT32), r=[tk["PT32"]], w=[t2["TT"][par]])
                yield
            kb.op("pe", mm4(b0, st.TTb[par], st.vbb), r=[t2["TT"][par], tk["vb"]], w=[t_ps[b0]])
            kb.op("dve", lambda e: e.tensor_copy(out=st.u32[par], in_=H4(ps[b0])), r=[t_ps[b0]], w=[t2["u"][par], t_ps[b0]])
            kb.op("pe", mm4(b1, st.kbgb, st.TTb[par]), r=[tk["kbg"], t2["TT"][par]], w=[t_ps[b1]])
            kb.op("act", lambda e: e.copy(out=st.wTb[par], in_=H4(ps[b1])), r=[t_ps[b1]], w=[t2["wT"][par], t_ps[b1]])
            yield

        def rec(st, g4, n):
            par = n % 2
            li = n % st.NL
            tk, t2 = st.tk, st.t2
            b0, b1 = st.rb
            qf, zs = st.qf[li], st.zs[li]
            t_ld = st.t_ld[li]
            esm = st.esm[par]
            s = st.s
            kb.op("pe", mm4(b0, st.wTb[par], st.Sb), r=[t2["wT"][par], tk["Sb"]], w=[t_ps[b0]])
            kb.op("dve", lambda e: e.tensor_tensor(out=st.vnb, in0=st.u32[par], in1=H4(ps[b0]), op=ALU.subtract), r=[t2["u"][par]], w=[tk["vnb"], t_ps[b0]])
            yield

            def r3(e):
                for j in range(4):
                    e.matmul(ps[b0][:, j * 128:(j + 1) * 128], qf[:, j // 2, :], st.Sb[:, j, :], start=True, stop=True)
                for j in range(4):
                    ins = e.matmul(ps[b1][:, j * 128:(j + 1) * 128], st.aTb[par][:, j, :], st.vnb[:, j, :], start=True, stop=True)
                return ins
            kb.op("pe", r3, r=[t2["aT"][par], tk["vnb"], tk["Sb"], t_ld], w=[t_ps[b0], t_ps[b1]])

            def r5(e):
                e.tensor_tensor(out=st.osb, in0=H4(ps[b0]), in1=col4(esm[:, 0:4]), op=ALU.mult)
                return e.tensor_tensor(out=st.osb, in0=st.osb, in1=H4(ps[b1]), op=ALU.add)
            kb.op("dve", r5, r=[t2["esm"][par]], w=[tk["osb"], t_ps[b0], t_ps[b1]])
            yield
            kb.op("pe", mm4(b0, st.kdecb[par], st.vnb), r=[t2["kdec"][par], tk["vnb"]], w=[t_ps[b0]])

            def r4(e):
                e.tensor_tensor(out=st.S32, in0=st.S32, in1=col4(esm[:, 8:12]), op=ALU.mult)
                return e.tensor_tensor(out=st.S32, in0=st.S32, in1=H4(ps[b0]), op=ALU.add)
            kb.op("dve", r4, r=[t2["esm"][par]], w=[tk["S32"], t_ps[b0]])
            kb.op("act", lambda e: e.copy(out=st.Sb, in_=st.S32), r=[tk["S32"]], w=[tk["Sb"]])
            yield
            kb.op("act", lambda e: e.activation(out=st.osq, in_=st.osb, func=AF.Square), r=[tk["osb"]], w=[tk["osq"]])
            kb.op("dve", lambda e: e.reduce_sum(out=st.ss, in_=st.osq, axis=AX.X), r=[tk["osq"]], w=[tk["ss"]])
            yield
            kb.op("act", lambda e: e.activation(out=st.ss, in_=st.ss, func=AF.Ln, scale=1.0 / 128, bias=self.epscol[:, 0:1]), r=[tk["ss"], self.t_cst], w=[tk["ss"]])
            kb.op("act", lambda e: e.activation(out=st.ss, in_=st.ss, func=AF.Exp, scale=-0.5), r=[tk["ss"]], w=[tk["ss"]], strict=True)
            yield
            kb.op("dve", lambda e: e.tensor_tensor(out=st.osb, in0=st.osb, in1=col4(st.ss), op=ALU.mult), r=[tk["ss"]], w=[tk["osb"]], strict=True)
            kb.op("pool", lambda e: e.tensor_tensor(out=st.onb, in0=st.osb, in1=bc4(c_("gnorm", 0, 128)), op=ALU.mult), r=[tk["osb"], self.t_cst], w=[tk["onb"]])
            yield
            kb.op("pe", tr4(b1, st.onb), r=[tk["onb"], self.t_cst], w=[t_ps[b1]])
            kb.op("dve", lambda e: e.tensor_tensor(out=st.oTb[par], in0=H4(psb[b1][:, 0:512]), in1=zs, op=ALU.mult), r=[t_ld], w=[t2["oT"][par], t_ps[b1]])
            dst = self.AOT[4 * g4 * 128:(4 * g4 + 4) * 128, s * S + n * 128:s * S + (n + 1) * 128].rearrange("(j p) t -> p j t", p=128)
            kb.dma("sp", dst, st.oTb[par], r=[t2["oT"][par]], w=[Tok()])
            yield

        def drive(gens):
            gens = list(gens)
            while gens:
                nxt = []
                for g in gens:
                    try:
                        next(g)
                        nxt.append(g)
                    except StopIteration:
                        pass
                gens = nxt

        for st in streams:
            cols = slice(st.s * S, (st.s + 1) * S)
            kb.dma("sp", st.gt, self.Gtok[cols, :].rearrange("(c p) e -> p c e", p=128), w=[st.t_gt])
        for g4 in range(4):
            for st in streams:
                def init(e, st=st):
                    e.memset(st.S32, 0.0)
                    return e.memset(st.Sb, 0.0)
                kb.op("dve", init, w=[st.tk["S32"], st.tk["Sb"]])
                load(st, g4, 0)
                if NCH > 1:
                    load(st, g4, 1)
            drive([pre(st, g4, 0) for st in streams])
            for n in range(NCH):
                gens = []
                for st in streams:
                    if n + 2 < NCH:
                        load(st, g4, n + 2)
                for st in streams:
                    if n + 1 < NCH:
                        gens.append(pre(st, g4, n + 1))
                    gens.append(rec(st, g4, n))
                drive(gens)


FULL_PLAN = [("pro",),
             ("attnA", 0, 0), ("attnB", 0, 0), ("proj", "wo0", 8), ("ffn", 0),
             ("conf", 1), ("ffn", 1),
             ("gdnA", 2), ("gdnB", 2), ("proj", "gwo", 16), ("ffn", 2),
             ("attnA", 3, 1), ("attnB", 3, 1), ("proj", "wo1", 8), ("ffn", 3),
             ("fin",)]


def run_prog(inputs, S, NSEQ, ncores, plan):
    cp, wp = host_pack(inputs)
    prog = Prog(S, NSEQ, cp, wp, plan)
    nc = prog.build()
    consts = cp.build()
    wts = wp.build()
    x = np.asarray(inputs["x"], np.float32)
    pos = np.asarray(inputs["positions"], np.int32)
    in_maps = []
    for c in range(ncores):
        xs = np.ascontiguousarray(x[c * NSEQ:(c + 1) * NSEQ, :S].reshape(NSEQ * S, D))
        ps = np.ascontiguousarray(pos[c * NSEQ:(c + 1) * NSEQ, :S].reshape(1, NSEQ * S))
        in_maps.append({"x": xs, "pos": ps, "consts": consts, "wts": wts})
    res = run_bass_kernel_spmd(nc, in_maps, core_ids=list(range(ncores)))
    outs = [np.asarray(r["out"]).reshape(NSEQ, S, D) for r in res.results]
    prog.res = res
    return np.concatenate(outs, axis=0), prog


def kernel(**inputs):
    out, _ = run_prog(inputs, 4096, 2, 8, FULL_PLAN)
    return out.astype(np.float32)
```

```python
import contextlib
import math
import os
DBG = os.environ.get('KDBG', '')
import numpy as np
import concourse.bass as bass
import concourse.mybir as mybir
from concourse.bass_utils import run_bass_kernel_spmd

F32 = mybir.dt.float32
BF16 = mybir.dt.bfloat16
I32 = mybir.dt.int32
AF = mybir.ActivationFunctionType
ALU = mybir.AluOpType
AX = mybir.AxisListType

D = 1024
DEPTH = 4
DFF = 2816
TT = 512
ENGS = ("pe", "act", "dve", "pool", "sp")
NDSEM = 12


class Tok:
    __slots__ = ("w", "r", "name")

    def __init__(self, name=""):
        self.w = None
        self.r = {}
        self.name = name


class Op:
    __slots__ = ("fn", "waits", "signal", "dma")

    def __init__(self, fn):
        self.fn = fn
        self.waits = []
        self.signal = False
        self.dma = None


class KB:
    def __init__(self, nc):
        self.nc = nc
        self.q = {e: [] for e in ENGS}
        self.seen = {e: {} for e in ENGS}
        self.dma_cnt = {e: 0 for e in ENGS}
        self.dma_val = {e: [0] * NDSEM for e in ENGS}
        self.dma_bg = {e: [False] * NDSEM for e in ENGS}
        self.last_real = {e: None for e in ENGS}

    def _deps(self, r, w):
        deps = []
        for t in r:
            if t.w is not None:
                deps.append(t.w)
        for t in w:
            if t.w is not None:
                deps.append(t.w)
            deps.extend(t.r.values())
        return deps

    def _add_waits(self, eng, op, deps, strict=False):
        seen = self.seen[eng]
        best = {}
        for d in deps:
            if d[0] == "e":
                _, e, i = d
                if e == eng and not strict:
                    continue
                key = ("e", e)
                val = i
            else:
                _, e, slot, val = d
                key = ("d", e, slot)
            if seen.get(key, -1) >= val:
                continue
            if best.get(key, -1) < val:
                best[key] = val
        for key, val in best.items():
            seen[key] = val
            if key[0] == "e":
                self.q[key[1]][val].signal = True
                op.waits.append(("e", key[1], val))
            else:
                op.waits.append(("d", key[1], key[2], val))

    def op(self, eng, fn, r=(), w=(), strict=False):
        o = Op(fn)
        self._add_waits(eng, o, self._deps(r, w), strict)
        idx = len(self.q[eng])
        self.q[eng].append(o)
        self.last_real[eng] = idx
        me = ("e", eng, idx)
        for t in r:
            t.r[("e", eng)] = me
        for t in w:
            t.w = me
            t.r = {}
        return me

    def dma(self, eng, out, in_, r=(), w=(), bg=False, **kw):
        n = self.dma_cnt[eng]
        self.dma_cnt[eng] += 1
        slot = n % NDSEM
        prev = self.dma_val[eng][slot]
        val = prev + 16
        self.dma_val[eng][slot] = val
        self.dma_bg[eng][slot] = bg
        o = Op(lambda e: e.dma_start(out=out, in_=in_, **kw))
        o.dma = (slot, val)
        deps = self._deps(r, w)
        if prev > 0:
            deps.append(("d", eng, slot, prev))
        self._add_waits(eng, o, deps)
        self.q[eng].append(o)
        me = ("d", eng, slot, val)
        for t in r:
            t.r[("d", eng, slot)] = me
        for t in w:
            t.w = me
            t.r = {}
        return me

    def barrier(self):
        deps = []
        for e in ENGS:
            if self.last_real[e] is not None:
                deps.append(("e", e, self.last_real[e]))
            for s in range(NDSEM):
                if self.dma_val[e][s] > 0 and not self.dma_bg[e][s]:
                    deps.append(("d", e, s, self.dma_val[e][s]))
        for e in ENGS:
            o = Op(None)
            self._add_waits(e, o, deps)
            self.q[e].append(o)

    def emit(self, final_waits=()):
        nc = self.nc
        fo = Op(None)
        self._add_waits("sp", fo, list(final_waits))
        self.q["sp"].append(fo)
        sigval = {}
        for e in ENGS:
            c = 0
            vals = []
            for o in self.q[e]:
                if o.signal:
                    c += 1
                vals.append(c)
            sigval[e] = vals
        with contextlib.ExitStack() as st:
            esem = {e: st.enter_context(nc.semaphore("s_" + e)) for e in ENGS}
            dsem = {e: [st.enter_context(nc.semaphore("d_%s%d" % (e, i))) for i in range(NDSEM)]
                    for e in ("sp", "act", "pool") if self.dma_cnt[e] > 0}
            block = st.enter_context(nc.Block())

            def run(e, eng):
                for o in self.q[e]:
                    for wt in o.waits:
                        if wt[0] == "e":
                            eng.wait_ge(esem[wt[1]], sigval[wt[1]][wt[2]])
                        else:
                            eng.wait_ge(dsem[wt[1]][wt[2]], wt[3])
                    if o.fn is None:
                        continue
                    ins = o.fn(eng)
                    if o.dma is not None:
                        ins.then_inc(dsem[e][o.dma[0]], 16)
                    elif o.signal:
                        ins.then_inc(esem[e], 1)

            @block.tensor
            def _(eng):
                run("pe", eng)

            @block.scalar
            def _(eng):
                run("act", eng)

            @block.vector
            def _(eng):
                run("dve", eng)

            @block.gpsimd
            def _(eng):
                run("pool", eng)

            @block.sync
            def _(eng):
                run("sp", eng)
        self.ninstr = {e: len(self.q[e]) for e in ENGS}
        self.sigmax = {e: (sigval[e][-1] if sigval[e] else 0) for e in ENGS}


def pack_lhsT(W):
    K, N = W.shape
    return np.ascontiguousarray(W.reshape(K // 128, 128, N // 128, 128).transpose(2, 1, 0, 3))


def pack_rhs(W):
    K, N = W.shape
    return np.ascontiguousarray(W.reshape(K // 128, 128, N).transpose(1, 0, 2))


def colpack(v):
    v = np.asarray(v, np.float32).reshape(-1)
    return np.ascontiguousarray(v.reshape(-1, 128).T)


class ConstPack:
    def __init__(self):
        self.cols = []
        self.off = {}
        self.n = 0

    def add(self, name, arr):
        arr = np.asarray(arr, np.float32)
        assert arr.shape[0] == 128
        self.off[name] = (self.n, arr.shape[1])
        self.cols.append(arr)
        self.n += arr.shape[1]

    def build(self):
        return np.ascontiguousarray(np.concatenate(self.cols, axis=1))


class WPack:
    def __init__(self):
        self.parts = []
        self.off = {}
        self.n = 0

    def add(self, name, arr):
        arr = np.ascontiguousarray(arr, dtype=np.float32)
        self.off[name] = (self.n, arr.shape)
        self.parts.append(arr.reshape(-1))
        self.n += arr.size

    def build(self):
        return np.concatenate(self.parts)


def lambda_init_fn(layer):
    return 0.8 - 0.6 * math.exp(-0.3 * layer)


def rope_perm_lhsT():
    P = np.zeros((128, 128), np.float32)
    for blk in (0, 64):
        for i in range(8):
            P[blk + i, blk + i + 8] = -1.0
            P[blk + 8 + i, blk + i] = 1.0
    return np.ascontiguousarray(P.T)


def inv_freq_col():
    half = 8
    f = np.power(np.float32(500000.0), -np.arange(half, dtype=np.float32) * np.float32(2.0) / np.float32(16)).astype(np.float32)
    col = np.zeros((128, 1), np.float32)
    for blk in (0, 64):
        for i in range(8):
            col[blk + i, 0] = f[i]
            col[blk + 8 + i, 0] = f[i]
    return col


def host_pack(inp):
    cp = ConstPack()
    wp = WPack()
    cp.add("ident", np.eye(128, dtype=np.float32))
    cp.add("ones", np.ones((128, 128), np.float32))
    cp.add("ropeP", rope_perm_lhsT())
    cp.add("invf", inv_freq_col())
    ii = np.arange(128)[:, None]
    jj = np.arange(128)[None, :]
    cp.add("triT", (ii <= jj).astype(np.float32))
    cp.add("negtriT", -(ii <= jj).astype(np.float32))
    cp.add("maskL", np.where(ii > jj, 0.0, -30000.0).astype(np.float32))
    cp.add("maskU", np.where(jj >= ii, 0.0, 30000.0).astype(np.float32))
    cp.add("mstrict", (ii > jj).astype(np.float32))
    cp.add("mbd32", ((ii // 32) == (jj // 32)).astype(np.float32))
    cp.add("moff64", (((ii // 64) == (jj // 64)) & ((ii // 32) != (jj // 32))).astype(np.float32))
    cp.add("moff128", ((ii // 64) != (jj // 64)).astype(np.float32))
    for l in range(DEPTH):
        cp.add("nm%d" % l, colpack(inp["norm_mix"][l]))
        cp.add("nf%d" % l, colpack(inp["norm_ffn"][l]))
        dw = np.asarray(inp["ffn_dw"][l], np.float32)
        cp.add("fdw%d" % l, np.concatenate([colpack(dw[k]) for k in range(3)], axis=1))
        cp.add("fdb%d" % l, colpack(inp["ffn_dw_b"][l]))
    cp.add("nfin", colpack(inp["norm_final"]))
    for j in range(2):
        cp.add("sln%d" % j, colpack(inp["attn_subln"][j]))
        cp.add("lam%d" % j, np.broadcast_to(np.asarray(inp["attn_lambda"][j], np.float32).reshape(1, 256), (128, 256)))
    cp.add("cbin", colpack(inp["conv_b_in"][0]))
    cdw = np.asarray(inp["conv_dw"][0], np.float32)
    cp.add("cdw", np.ascontiguousarray(np.stack([colpack(cdw[k]) for k in range(31)], axis=2).reshape(128, 8 * 31)))
    cp.add("cdwb", colpack(inp["conv_dw_b"][0]))
    cp.add("clg", colpack(inp["conv_ln_g"][0]))
    cp.add("clb", colpack(inp["conv_ln_b"][0]))
    cp.add("cbo", colpack(inp["conv_b_out"][0]))
    gcv = np.asarray(inp["gdn_conv"][0], np.float32)
    cp.add("gconv", np.concatenate([colpack(gcv[k]) for k in range(4)], axis=1))
    cp.add("galog", np.broadcast_to(np.asarray(inp["gdn_a_log"][0], np.float32).reshape(1, 16), (128, 16)))
    cp.add("gdtb", np.broadcast_to(np.asarray(inp["gdn_dt_bias"][0], np.float32).reshape(1, 16), (128, 16)))
    cp.add("gnorm", np.broadcast_to(np.asarray(inp["gdn_norm"][0], np.float32).reshape(1, 128), (128, 128)))

    def add_ffn(l):
        win = pack_lhsT(np.asarray(inp["ffn_w_in"][l], np.float32))
        wp.add("fin%d" % l, np.concatenate([win[:22], win[22:]], axis=3))
        wp.add("fout%d" % l, pack_lhsT(np.asarray(inp["ffn_w_out"][l], np.float32)))

    def add_attn(j):
        w = np.asarray(inp["attn_w_qkv"][j], np.float32)
        wp.add("qk%d" % j, pack_lhsT(w[:, :2048]))
        wp.add("wv%d" % j, pack_rhs(w[:, 2048:]))
        wp.add("wo%d" % j, pack_lhsT(np.asarray(inp["attn_w_o"][j], np.float32)))

    add_attn(0)
    add_ffn(0)
    wp.add("cin", pack_lhsT(np.asarray(inp["conv_w_in"][0], np.float32)))
    wp.add("cout", pack_lhsT(np.asarray(inp["conv_w_out"][0], np.float32)))
    add_ffn(1)
    gw = np.asarray(inp["gdn_w_in"][0], np.float32)
    wp.add("gin", pack_lhsT(gw[:, :6144]))
    wp.add("gba", pack_rhs(gw[:, 6144:]))
    wp.add("gwo", pack_lhsT(np.asarray(inp["gdn_w_o"][0], np.float32)))
    add_ffn(2)
    add_attn(1)
    add_ffn(3)
    return cp, wp


class Prog:
    def __init__(self, S, NSEQ, cp, wp, plan):
        self.S, self.NSEQ = S, NSEQ
        self.NT = S * NSEQ
        self.cp, self.wp, self.plan = cp, wp, plan
        self.ntile = self.NT // TT
        self.tps = S // TT

    def alloc(self, nf32):
        off = self.sb_off
        self.sb_off += int(nf32)
        assert self.sb_off <= self.NBIG, ("sbuf overflow", self.sb_off, self.NBIG)
        return off

    def f32(self, nf32, shape=None):
        off = self.alloc(nf32)
        ap = self.big[:, off:off + nf32]
        return ap

    def bf(self, nbf):
        nf = (nbf + 1) // 2
        off = self.alloc(nf)
        return self.big[:, off:off + nf].bitcast(BF16)

    def phase_reset(self):
        self.kb.barrier()
        self.sb_off = self.sb_persist

    def cc(self, name, i=0, n=1):
        o, w = self.cp.off[name]
        return self.cst[:, o + i:o + i + n]

    def wview(self, name):
        off, shape = self.wp.off[name]
        n = int(np.prod(shape))
        return self.wbf[off:off + n], shape

    def build(self):
        nc = bass.Bass("TRN2", target_bir_lowering=False)
        self.nc = nc
        NT = self.NT
        self.x = nc.dram_tensor("x", [NT, D], F32, kind="ExternalInput").ap()
        self.pos = nc.dram_tensor("pos", [1, NT], I32, kind="ExternalInput").ap()
        self.cst_d = nc.dram_tensor("consts", [128, self.cp.n], F32, kind="ExternalInput").ap()
        self.wts = nc.dram_tensor("wts", [self.wp.n], F32, kind="ExternalInput").ap()
        self.out = nc.dram_tensor("out", [NT, D], F32, kind="ExternalOutput").ap()
        self.wbf = nc.dram_tensor("wbf", [self.wp.n], BF16, kind="Internal").ap()
        self.hT = nc.dram_tensor("hT", [D, NT], F32, kind="Internal").ap()
        self.dram_extra()
        with contextlib.ExitStack() as st:
            self.NBIG = 51800
            self.big = st.enter_context(nc.sbuf_tensor("big", [128, self.NBIG], F32))[:]
            self.ps = [st.enter_context(nc.psum_tensor("ps%d" % i, [128, 512], F32))[:] for i in range(8)]
            self.kb = KB(nc)
            self.sb_off = 0
            self.setup()
            self.sb_persist = self.sb_off
            fin = []
            for ph in self.plan:
                self.phase_reset()
                r = getattr(self, "ph_" + ph[0])(*ph[1:])
                if r:
                    fin += r
            self.kb.emit(final_waits=fin)
        return nc

    def dram_extra(self):
        nc, NT = self.nc, self.NT
        kind = "ExternalOutput" if 'dump' in DBG else "Internal"
        self.QT = nc.dram_tensor("QT", [D, NT], BF16, kind=kind).ap()
        self.KT = nc.dram_tensor("KT", [D, NT], BF16, kind=kind).ap()
        self.Vt = nc.dram_tensor("Vt", [NT, D], BF16, kind=kind).ap()
        self.AOT = nc.dram_tensor("AOT", [2 * D, NT], BF16, kind=kind).ap()
        self.Vtok = nc.dram_tensor("Vtok", [NT, 2 * D], BF16, kind=kind).ap()
        self.ZsT = nc.dram_tensor("ZsT", [2 * D, NT], BF16, kind=kind).ap()
        self.Gtok = nc.dram_tensor("Gtok", [NT, 32], F32, kind=kind).ap()
        if 'dump' in DBG:
            self.dbgf = nc.dram_tensor("dbgf", [128, 4096], F32, kind=kind).ap()
            self.dbgb = nc.dram_tensor("dbgb", [128, 4096], BF16, kind=kind).ap()

    def setup(self):
        kb = self.kb
        ncst = self.cp.n
        self.cst = self.f32(ncst)
        self.t_cst = Tok("cst")
        kb.dma("sp", self.cst, self.cst_d, w=[self.t_cst])
        self.identb = self.bf(128)
        self.onesb = self.bf(128)
        self.ropePb = self.bf(128)
        self.epscol = self.f32(4)

        def mk(e):
            e.tensor_copy(out=self.identb, in_=self.cc("ident", 0, 128))
            e.tensor_copy(out=self.onesb, in_=self.cc("ones", 0, 128))
            e.memset(self.epscol[:, 0:1], 1e-6)
            e.memset(self.epscol[:, 1:2], 1e-5)
            e.memset(self.epscol[:, 2:3], -math.pi)
            e.memset(self.epscol[:, 3:4], 0.0)
            return e.tensor_copy(out=self.ropePb, in_=self.cc("ropeP", 0, 128))
        kb.op("dve", mk, r=[self.t_cst], w=[self.t_cst])
        self.t_w = {}
        CH = 128 * 16384
        self.cast_later = []
        for name, (off, shape) in self.wp.off.items():
            n = int(np.prod(shape))
            t = Tok("w_" + name)
            self.t_w[name] = t
            if name not in ("qk0", "wv0", "wo0"):
                self.cast_later.append((name, off, n, t))
                continue
            o = 0
            while o < n:
                m = min(CH, n - o)
                assert m % 128 == 0
                src = self.wts[off + o:off + o + m].rearrange("(p x) -> p x", p=128)
                dst = self.wbf[off + o:off + o + m].rearrange("(p x) -> p x", p=128)
                kb.dma("pool", dst, src, w=[t], bg=True)
                o += m

    @staticmethod
    def run_pipe(items, depth):
        for t in range(len(items) + depth):
            for k in range(depth - 1, -1, -1):
                i = t - k
                if 0 <= i < len(items) and k < len(items[i]) and items[i][k] is not None:
                    items[i][k]()

    def norm_bufs(self):
        nb = {}
        nb["h"] = [self.f32(8 * TT).rearrange("p (c t) -> p c t", c=8) for _ in range(2)]
        nb["xn"] = [self.bf(8 * TT).rearrange("p (c t) -> p c t", c=8) for _ in range(2)]
        nb["sq"] = self.bf(8 * TT).rearrange("p (c t) -> p c t", c=8)
        nb["rstd"] = self.f32(TT)
        nb["t_h"] = [Tok("h0"), Tok("h1")]
        nb["t_xn"] = [Tok("xn0"), Tok("xn1")]
        nb["t_sq"] = Tok("sq")
        nb["t_rstd"] = Tok("rstd")
        return nb

    def hT_tile(self, n):
        return self.hT[:, n * TT:(n + 1) * TT].rearrange("(c p) t -> p c t", p=128)

    def load_h(self, nb, n, eng="sp"):
        par = n % 2
        self.kb.dma(eng, nb["h"][par], self.hT_tile(n), r=[self.t_hT[n]], w=[nb["t_h"][par]])

    def rmsnorm(self, nb, n, gname, psbank, t_ps, out_f32=None, eps_i=0):
        kb = self.kb
        par = n % 2
        h, xn, sq, rstd = nb["h"][par], nb["xn"][par], nb["sq"], nb["rstd"]
        ps = self.ps[psbank]
        kb.op("act", lambda e: e.activation(out=sq, in_=h, func=AF.Square), r=[nb["t_h"][par]], w=[nb["t_sq"]])

        def mm(e):
            for c in range(8):
                ins = e.matmul(ps, self.onesb, sq[:, c, :], start=(c == 0), stop=(c == 7))
            return ins
        kb.op("pe", mm, r=[nb["t_sq"], self.t_cst], w=[t_ps])
        def rs_(e):
            e.activation(out=rstd, in_=ps, func=AF.Ln, scale=1.0 / D, bias=self.epscol[:, eps_i:eps_i + 1])
            return e.activation(out=rstd, in_=rstd, func=AF.Exp, scale=-0.5)
        kb.op("act", rs_, r=[t_ps, self.t_cst], w=[nb["t_rstd"]])
        dst = xn if out_f32 is None else out_f32

        def sc(e):
            for c in range(8):
                ins = e.scalar_tensor_tensor(out=dst[:, c, :], in0=h[:, c, :], scalar=self.cc(gname, c), in1=rstd,
                                             op0=ALU.mult, op1=ALU.mult)
            return ins
        kb.op("dve", sc, r=[nb["t_h"][par], nb["t_rstd"], self.t_cst], w=[nb["t_xn"][par]])

    def ph_pro(self):
        kb = self.kb
        self.t_hT = [Tok("hT%d" % n) for n in range(self.ntile)]
        xin = [self.f32(4 * D).rearrange("p (b d) -> p b d", b=4) for _ in range(2)]
        hst = [self.f32(8 * TT).rearrange("p (c t) -> p c t", c=8) for _ in range(2)]
        t_xin = [Tok(), Tok()]
        t_hst = [Tok(), Tok()]
        t_ps = [Tok() for _ in range(8)]
        ident = self.cc("ident", 0, 128)
        for n in range(self.ntile):
            par = n % 2
            src = self.x[n * TT:(n + 1) * TT, :].rearrange("(b p) d -> p b d", p=128)
            kb.dma("sp", xin[par], src, w=[t_xin[par]])
            for c in range(8):
                bank = c % 8

                def tr(e, c=c, par=par, bank=bank):
                    for b in range(4):
                        ins = e.transpose(self.ps[bank][:, b * 128:(b + 1) * 128], xin[par][:, b, c * 128:(c + 1) * 128], ident)
                    return ins
                kb.op("pe", tr, r=[t_xin[par], self.t_cst], w=[t_ps[bank]])
                eng = "act" if c % 2 == 0 else "dve"
                if eng == "act":
                    kb.op("act", lambda e, c=c, par=par, bank=bank: e.copy(out=hst[par][:, c, :], in_=self.ps[bank]),
                          r=[t_ps[bank]], w=[t_hst[par]])
                else:
                    kb.op("dve", lambda e, c=c, par=par, bank=bank: e.tensor_copy(out=hst[par][:, c, :], in_=self.ps[bank]),
                          r=[t_ps[bank]], w=[t_hst[par]])
            kb.dma("sp", self.hT_tile(n), hst[par], r=[t_hst[par]], w=[self.t_hT[n]])
        CH = 128 * 16384
        for name, off, nel, t in self.cast_later:
            o = 0
            first = True
            while o < nel:
                m = min(CH, nel - o)
                src = self.wts[off + o:off + o + m].rearrange("(p x) -> p x", p=128)
                dst = self.wbf[off + o:off + o + m].rearrange("(p x) -> p x", p=128)
                kb.dma("pool", dst, src, r=(self.t_hT if first else []), w=[t], bg=True)
                first = False
                o += m

    def ph_fin(self):
        kb = self.kb
        nb = self.norm_bufs()
        xf = [self.f32(8 * TT).rearrange("p (c t) -> p c t", c=8) for _ in range(2)]
        ost = [self.f32(4 * D).rearrange("p (b d) -> p b d", b=4) for _ in range(2)]
        t_ost = [Tok(), Tok()]
        t_ps = [Tok() for _ in range(8)]
        ident = self.cc("ident", 0, 128)
        fin = []
        self.load_h(nb, 0)
        for n in range(self.ntile):
            par = n % 2
            if n + 1 < self.ntile:
                self.load_h(nb, n + 1)
            self.rmsnorm(nb, n, "nfin", 7, t_ps[7], out_f32=xf[par])
            k = 0
            for b in range(4):
                for half in range(2):
                    bank = k % 6
                    k += 1

                    def tr(e, b=b, half=half, bank=bank, par=par):
                        for cc_ in range(4):
                            c = half * 4 + cc_
                            ins = e.transpose(self.ps[bank][:, cc_ * 128:(cc_ + 1) * 128], xf[par][:, c, b * 128:(b + 1) * 128], ident)
                        return ins
                    kb.op("pe", tr, r=[nb["t_xn"][par], self.t_cst], w=[t_ps[bank]])
                    dst = ost[par][:, b, half * 512:(half + 1) * 512]
                    if k % 2 == 0:
                        kb.op("act", lambda e, dst=dst, bank=bank: e.copy(out=dst, in_=self.ps[bank]), r=[t_ps[bank]], w=[t_ost[par]])
                    else:
                        kb.op("dve", lambda e, dst=dst, bank=bank: e.tensor_copy(out=dst, in_=self.ps[bank]), r=[t_ps[bank]], w=[t_ost[par]])
            dstd = self.out[n * TT:(n + 1) * TT, :].rearrange("(b p) d -> p b d", p=128)
            fin.append(kb.dma("sp", dstd, ost[par], r=[t_ost[par]], w=[Tok()]))
        return fin

    def ph_ffn(self, l):
        kb = self.kb
        nb = self.norm_bufs()
        NJ = 22
        act = self.bf(NJ * TT).rearrange("p (j t) -> p j t", j=NJ)
        t_act = [Tok() for _ in range(NJ)]
        NU, NY = 3, 6
        ust = [self.f32(2 * (TT + 2)).rearrange("p (a t) -> p a t", a=2) for _ in range(NU)]
        yb = [self.f32(2 * TT).rearrange("p (a t) -> p a t", a=2) for _ in range(NY)]
        t_ust = [Tok() for _ in range(NU)]
        t_y = [Tok() for _ in range(NY)]
        hb = self.f32(NJ * 4).rearrange("p (j a t) -> p j a t", j=NJ, a=2)
        NWB = 4
        win = [self.bf(2 * 8 * 256).rearrange("p (j k c) -> p j k c", j=2, k=8) for _ in range(NWB)]
        t_win = [Tok() for _ in range(NWB)]
        NOB = 3
        wout = [self.bf(NJ * 128).rearrange("p (k c) -> p k c", k=NJ) for _ in range(NOB)]
        t_wout = [Tok() for _ in range(NOB)]
        t_ps = [Tok() for _ in range(8)]
        wi_ap, _ = self.wview("fin%d" % l)
        wo_ap, _ = self.wview("fout%d" % l)
        wi_v = wi_ap.rearrange("(g j p x) -> g p j x", j=2, p=128, x=8 * 256)
        wo_v = wo_ap.rearrange("(m p x) -> m p x", p=128, x=NJ * 128)
        tw_in, tw_out = self.t_w["fin%d" % l], self.t_w["fout%d" % l]
        dwo, _ = self.cp.off["fdw%d" % l]
        dbo, _ = self.cp.off["fdb%d" % l]
        cst = self.cst

        def dwc(k, ch):
            return cst[:, dwo + k * 44 + ch:dwo + k * 44 + ch + 1]

        def dbc(ch):
            return cst[:, dbo + ch:dbo + ch + 1]
        cnt = {"g": 0, "o": 0, "c": 0}
        items = []

        def rms_item(n):
            def f():
                self.rmsnorm(nb, n, "nf%d" % l, 7, t_ps[7])
            return [f]

        def load_item(n):
            def f():
                self.load_h(nb, n)
            return [f]

        def wout_item(n):
            par = n % 2
            h = nb["h"][par]

            def f():
                for m in range(8):
                    ob = cnt["o"] % NOB
                    cnt["o"] += 1
                    kb.dma("sp", wout[ob].rearrange("p k c -> p (k c)"), wo_v[m], r=[tw_out], w=[t_wout[ob]])
                    bank = 6 + m % 2

                    def mo(e, ob=ob, bank=bank):
                        for k in range(NJ):
                            ins = e.matmul(self.ps[bank], wout[ob][:, k, :], act[:, k, :], start=(k == 0), stop=(k == NJ - 1))
                        return ins
                    kb.op("pe", mo, r=[t_wout[ob]] + t_act, w=[t_ps[bank]])
                    kb.op("dve", lambda e, m=m, bank=bank: e.tensor_tensor(out=h[:, m, :], in0=h[:, m, :], in1=self.ps[bank], op=ALU.add),
                          r=[t_ps[bank]], w=[nb["t_h"][par]])
                kb.dma("pool", self.hT_tile(n), h, r=[nb["t_h"][par]], w=[self.t_hT[n]])
            return [f]

        def chunk_item(n, j):
            par = n % 2
            first = (n % self.tps == 0)
            c = cnt["c"]
            cnt["c"] += 1
            pa, pb = (c % 3) * 2, (c % 3) * 2 + 1
            ub = c % NU
            yi = c % NY
            u, y = ust[ub], yb[yi]
            xn = nb["xn"][par]
            jj = j % 2
            if jj == 0:
                wbx = cnt["g"] % NWB
                cnt["g"] += 1
            else:
                wbx = (cnt["g"] - 1) % NWB
            g = j // 2

            def s0():
                if jj == 0:
                    kb.dma("sp", win[wbx].rearrange("p j k c -> p j (k c)"), wi_v[g], r=[tw_in], w=[t_win[wbx]])

                def mm(e):
                    for a, bank in ((0, pa), (1, pb)):
                        for k in range(8):
                            ins = e.matmul(self.ps[bank], win[wbx][:, jj, k, a * 128:(a + 1) * 128], xn[:, k, :],
                                           start=(k == 0), stop=(k == 7))
                    return ins
                kb.op("pe", mm, r=[t_win[wbx], nb["t_xn"][par]], w=[t_ps[pa], t_ps[pb]])

            def s1():
                if first:
                    kb.op("act", lambda e: e.activation(out=u[:, :, 0:2], in_=u[:, :, 0:2], func=AF.Copy, scale=0.0), w=[t_ust[ub]])
                else:
                    kb.op("act", lambda e: e.copy(out=u[:, :, 0:2], in_=hb[:, j, :, :]), w=[t_ust[ub]])

                def ev(e):
                    for a, bank in ((0, pa), (1, pb)):
                        ch = j + 22 * a
                        e.copy(out=u[:, a, 2:2 + TT], in_=self.ps[bank])
                        ins = e.activation(out=y[:, a, :], in_=self.ps[bank], func=AF.Identity, scale=dwc(2, ch), bias=dbc(ch))
                    return ins
                kb.op("act", ev, r=[t_ps[pa], t_ps[pb], self.t_cst], w=[t_ust[ub], t_y[yi]])
                kb.op("act", lambda e: e.copy(out=hb[:, j, :, :], in_=u[:, :, TT:TT + 2]), r=[t_ust[ub]])

            def s2():
                def cv(e):
                    for a in range(2):
                        ch = j + 22 * a
                        e.scalar_tensor_tensor(out=y[:, a, :], in0=u[:, a, 1:1 + TT], scalar=dwc(1, ch), in1=y[:, a, :], op0=ALU.mult, op1=ALU.add)
                        ins = e.scalar_tensor_tensor(out=y[:, a, :], in0=u[:, a, 0:TT], scalar=dwc(0, ch), in1=y[:, a, :], op0=ALU.mult, op1=ALU.add)
                    return ins
                kb.op("dve", cv, r=[t_ust[ub], self.t_cst], w=[t_y[yi]])

            def s3():
                kb.op("act", lambda e: e.activation(out=y[:, 0, :], in_=y[:, 0, :], func=AF.Silu), r=[t_y[yi]], w=[t_y[yi]])

            def s4():
                kb.op("pool", lambda e: e.tensor_tensor(out=act[:, j, :], in0=y[:, 0, :], in1=y[:, 1, :], op=ALU.mult), r=[t_y[yi]], w=[t_act[j]])
            return [s0, s1, s2, s3, s4]

        items.append(load_item(0))
        items.append(rms_item(0))
        for n in range(self.ntile):
            for j in range(NJ):
                items.append(chunk_item(n, j))
                if j == 2 and n > 0:
                    items.append(wout_item(n - 1))
                if j == 4 and n + 1 < self.ntile:
                    items.append(load_item(n + 1))
                if j == 10 and n + 1 < self.ntile:
                    items.append(rms_item(n + 1))
        for _ in range(4):
            items.append([])
        items.append(wout_item(self.ntile - 1))
        depth = 5
        for t in range(len(items) + depth):
            for k in range(depth - 1, -1, -1):
                i = t - k
                if 0 <= i < len(items) and k < len(items[i]):
                    items[i][k]()

    def ph_attnA(self, l, j):
        kb = self.kb
        nb = self.norm_bufs()
        wqk = self.bf(16 * 8 * 128).rearrange("p (m k c) -> p m k c", m=16, k=8)
        wv = self.bf(8 * 1024).rearrange("p (k c) -> p k c", k=8)
        t_wqk, t_wv = Tok(), Tok()
        a, _ = self.wview("qk%d" % j)
        kb.dma("sp", wqk.rearrange("p m k c -> p m (k c)"), a.rearrange("(m p x) -> p m x", p=128, x=1024), r=[self.t_w["qk%d" % j]], w=[t_wqk])
        a, _ = self.wview("wv%d" % j)
        kb.dma("sp", wv.rearrange("p k c -> p (k c)"), a.rearrange("(p x) -> p x", p=128), r=[self.t_w["wv%d" % j]], w=[t_wv])
        posi = self.f32(TT).bitcast(I32)
        ang = self.f32(TT)
        kf = self.f32(TT)
        ki = self.f32(TT).bitcast(I32)
        rr = self.f32(TT)
        msk = self.f32(TT)
        cosT = self.f32(TT)
        sinT = self.f32(TT)
        t_pos, t_tab, t_tmp = Tok(), Tok(), Tok()
        qst = [self.bf(16 * TT).rearrange("p (c t) -> p c t", c=16) for _ in range(2)]
        vst = [self.bf(4 * 1024).rearrange("p (b c) -> p b c", b=4) for _ in range(2)]
        t_qst = [Tok(), Tok()]
        t_vst = [Tok(), Tok()]
        t_ps = [Tok() for _ in range(8)]
        TWO_PI = 2.0 * math.pi
        C1 = 6.28125
        C2 = TWO_PI - C1
        self.t_QT = [Tok() for _ in range(self.ntile)]
        cosT2 = [cosT, self.f32(TT)]
        sinT2 = [sinT, self.f32(TT)]
        t_tab2 = [Tok(), Tok()]
        qraw = [self.bf(TT) for _ in range(3)]
        t1 = [self.f32(TT) for _ in range(3)]
        t2_one = self.f32(TT)
        t2 = [t2_one, t2_one, t2_one]
        t_qraw = [Tok() for _ in range(3)]
        t_t1 = [Tok() for _ in range(3)]
        cnt = {"c": 0, "v": 0}

        def tab_item(n):
            par = n % 2
            cT, sT, tt = cosT2[par], sinT2[par], t_tab2[par]

            def f():
                kb.dma("sp", posi, self.pos[0:1, n * TT:(n + 1) * TT].to_broadcast([128, TT]), w=[t_pos])

                def tab(e):
                    e.tensor_copy(out=ang, in_=posi)
                    e.tensor_scalar(out=ang, in0=ang, scalar1=self.cc("invf"), scalar2=None, op0=ALU.mult)
                    for which, dst in ((0, sT), (1, cT)):
                        src = ang
                        if which == 1:
                            e.tensor_scalar(out=rr, in0=ang, scalar1=float(math.pi / 2), scalar2=None, op0=ALU.add)
                            src = rr
                        e.tensor_scalar(out=kf, in0=src, scalar1=float(1.0 / TWO_PI), scalar2=None, op0=ALU.mult)
                        e.tensor_copy(out=ki, in_=kf)
                        e.tensor_copy(out=kf, in_=ki)
                        e.scalar_tensor_tensor(out=rr, in0=kf, scalar=-C1, in1=src, op0=ALU.mult, op1=ALU.add)
                        e.scalar_tensor_tensor(out=rr, in0=kf, scalar=-C2, in1=rr, op0=ALU.mult, op1=ALU.add)
                        e.tensor_single_scalar(out=msk, in_=rr, scalar=float(math.pi), op=ALU.is_gt)
                        e.scalar_tensor_tensor(out=rr, in0=msk, scalar=-TWO_PI, in1=rr, op0=ALU.mult, op1=ALU.add)
                        e.tensor_single_scalar(out=msk, in_=rr, scalar=float(-math.pi), op=ALU.is_lt)
                        ins = e.scalar_tensor_tensor(out=dst, in0=msk, scalar=TWO_PI, in1=rr, op0=ALU.mult, op1=ALU.add)
                    return ins
                kb.op("dve", tab, r=[t_pos, self.t_cst], w=[tt, t_tmp])

                def sinact(e):
                    e.activation(out=sT, in_=sT, func=AF.Sin)
                    return e.activation(out=cT, in_=cT, func=AF.Sin)
                kb.op("act", sinact, r=[tt], w=[tt])
            return [f]

        def qk_item(n, c):
            par = n % 2
            xn = nb["xn"][par]
            cT, sT, tt = cosT2[par], sinT2[par], t_tab2[par]
            i = cnt["c"]
            cnt["c"] += 1
            b1 = i % 3
            b2 = 3 + i % 2
            rb = i % 3
            sc = 0.125 if c < 8 else 1.0

            def s0():
                def mm(e):
                    for k in range(8):
                        ins = e.matmul(self.ps[b1], wqk[:, c, k, :], xn[:, k, :], start=(k == 0), stop=(k == 7))
                    return ins
                kb.op("pe", mm, r=[t_wqk, nb["t_xn"][par]], w=[t_ps[b1]])

            def s1():
                kb.op("act", lambda e: e.copy(out=qraw[rb], in_=self.ps[b1]), r=[t_ps[b1]], w=[t_qraw[rb], t_ps[b1]])

            def s2():
                kb.op("pe", lambda e: e.matmul(self.ps[b2], self.ropePb, qraw[rb], start=True, stop=True),
                      r=[t_qraw[rb], self.t_cst], w=[t_ps[b2]])
                kb.op("dve", lambda e: e.scalar_tensor_tensor(out=t1[rb], in0=self.ps[b1], scalar=sc, in1=cT, op0=ALU.mult, op1=ALU.mult),
                      r=[tt], w=[t_t1[rb], t_ps[b1]])

            def s3():
                def fin(e):
                    e.scalar_tensor_tensor(out=t2[rb], in0=self.ps[b2], scalar=sc, in1=sT, op0=ALU.mult, op1=ALU.mult)
                    return e.tensor_tensor(out=qst[par][:, c, :], in0=t1[rb], in1=t2[rb], op=ALU.add)
                kb.op("dve", fin, r=[t_ps[b2], tt, t_t1[rb]], w=[t_t1[rb], t_qst[par]])
            return [s0, s1, s2, s3]

        def v_item(n, b, half):
            par = n % 2
            xn = nb["xn"][par]
            bank = 5 + cnt["v"] % 2
            cnt["v"] += 1

            def s0():
                def mv(e):
                    for k in range(8):
                        ins = e.matmul(self.ps[bank], xn[:, k, b * 128:(b + 1) * 128], wv[:, k, half * 512:(half + 1) * 512],
                                       start=(k == 0), stop=(k == 7))
                    return ins
                kb.op("pe", mv, r=[t_wv, nb["t_xn"][par]], w=[t_ps[bank]])

            def s1():
                kb.op("act", lambda e: e.copy(out=vst[par][:, b, half * 512:(half + 1) * 512], in_=self.ps[bank]), r=[t_ps[bank]], w=[t_vst[par]])
            return [s0, s1]

        def store_item(n):
            par = n % 2
            cols = slice(n * TT, (n + 1) * TT)

            def f():
                kb.dma("sp", self.QT[:, cols].rearrange("(c p) t -> p c t", p=128), qst[par][:, 0:8, :], r=[t_qst[par]], w=[self.t_QT[n]])
                kb.dma("sp", self.KT[:, cols].rearrange("(c p) t -> p c t", p=128), qst[par][:, 8:16, :], r=[t_qst[par]], w=[Tok()])
                kb.dma("sp", self.Vt[cols, :].rearrange("(b p) e -> p b e", p=128), vst[par], r=[t_vst[par]], w=[Tok()])
            return [None, None, None, None, f]

        items = [[lambda: self.load_h(nb, 0)], [lambda: self.rmsnorm(nb, 0, "nm%d" % l, 7, t_ps[7])], tab_item(0)]
        for n in range(self.ntile):
            for c in range(16):
                items.append(qk_item(n, c))
                if c == 3 and n + 1 < self.ntile:
                    items.append([lambda n=n: self.load_h(nb, n + 1)])
                if c == 8 and n + 1 < self.ntile:
                    items.append([lambda n=n: self.rmsnorm(nb, n + 1, "nm%d" % l, 7, t_ps[7])])
                if c == 10 and n + 1 < self.ntile:
                    items.append(tab_item(n + 1))
            for b in range(4):
                for half in range(2):
                    items.append(v_item(n, b, half))
            items.append(store_item(n))
        self.run_pipe(items, 5)

    def ph_attnB(self, l, j):
        kb = self.kb
        S, NSEQ = self.S, self.NSEQ
        nblk = S // 128
        tq = S // TT
        li = lambda_init_fn(l)
        qT = [self.bf(S) for _ in range(2)]
        kT = [self.bf(S) for _ in range(2)]
        vv = [self.bf(nblk * 128).rearrange("p (b e) -> p b e", b=nblk) for _ in range(2)]
        t_qq = [Tok(), Tok()]
        t_kk = [Tok(), Tok()]
        t_vv = [Tok(), Tok()]
        pT = [self.bf(2 * TT).rearrange("p (m t) -> p m t", m=2) for _ in range(2)]
        t_pT = [Tok(), Tok()]
        t_S = [Tok(), Tok()]
        t_acc = Tok()
        rrec = self.f32(2 * TT).rearrange("p (m t) -> p m t", m=2)
        of = self.f32(TT)
        o1f = self.f32(TT)
        sqb = self.bf(TT)
        rstd = self.f32(TT)
        aob = [self.bf(TT) for _ in range(2)]
        t_fin = Tok()
        t_sq = Tok()
        t_rstd = Tok()
        t_ao = [Tok(), Tok()]
        small = self.f32(128 + 8)
        t_small = Tok()
        lp = self.cc("lam%d" % j, 0, 256)
        neglam = small[:, 130:131]
        gcol = small[:, 131:132]

        lp4 = lp.rearrange("p (a b d) -> p a b d", a=2, b=2)
        kb.op("pool", lambda e: e.tensor_tensor(out=small[:, 0:128].rearrange("p (a d) -> p a d", a=2), in0=lp4[:, :, 0, :], in1=lp4[:, :, 1, :], op=ALU.mult),
              r=[self.t_cst], w=[t_small])
        kb.op("dve", lambda e: e.reduce_sum(out=small[:, 128:130], in_=small[:, 0:128].rearrange("p (a d) -> p a d", a=2), axis=AX.X),
              r=[t_small], w=[t_small])
        kb.op("act", lambda e: e.activation(out=small[:, 128:130], in_=small[:, 128:130], func=AF.Exp), r=[t_small], w=[t_small])

        def lam2(e):
            e.scalar_tensor_tensor(out=neglam, in0=small[:, 129:130], scalar=float(-li), in1=small[:, 128:129], op0=ALU.add, op1=ALU.subtract)
            return e.tensor_scalar(out=gcol, in0=self.cc("sln%d" % j), scalar1=float(1.0 - li), scalar2=None, op0=ALU.mult)
        kb.op("dve", lam2, r=[t_small, self.t_cst], w=[t_small])
        if 'dump' in DBG:
            kb.dma("sp", self.dbgf[:, 2048:2048 + 136], small, r=[t_small], w=[Tok()])
            kb.dma("sp", self.dbgf[:, 2304:2304 + 256], lp, r=[self.t_cst], w=[Tok()])

        units = [(s, h) for s in range(NSEQ) for h in range(8)]

        def load(u):
            s, h = units[u]
            par = u % 2
            rows = slice(h * 128, (h + 1) * 128)
            cols = slice(s * S, (s + 1) * S)
            rq = [self.t_QT[n] for n in range(s * tq, (s + 1) * tq)]
            kb.dma("sp", qT[par], self.QT[rows, cols], r=rq, w=[t_qq[par]])
            kb.dma("sp", kT[par], self.KT[rows, cols], r=rq, w=[t_kk[par]])
            kb.dma("sp", vv[par], self.Vt[cols, rows].rearrange("(b p) e -> p b e", p=128), r=rq, w=[t_vv[par]])
        self.t_AO = [Tok() for _ in range(self.ntile)]
        pending = []
        fi = [0]

        O0s = self.f32(TT)
        O1s = self.f32(TT)
        lnl = self.f32(2 * TT).rearrange("p (m t) -> p m t", m=2)
        t_os, t_lnl = Tok(), Tok()

        def finalize_a(u, qt):
            def fa(e):
                e.activation(out=lnl[:, 0, :], in_=self.ps[6], func=AF.Ln)
                return e.activation(out=lnl[:, 1, :], in_=self.ps[7], func=AF.Ln)
            kb.op("act", fa, r=[t_acc], w=[t_lnl])

            def fd(e):
                e.tensor_copy(out=O0s, in_=self.ps[4])
                return e.tensor_copy(out=O1s, in_=self.ps[5])
            kb.op("dve", fd, r=[t_acc], w=[t_os])
            kb.op("act", lambda e: e.activation(out=rrec, in_=lnl, func=AF.Exp, scale=-1.0), r=[t_lnl], w=[t_lnl])

            def f1(e):
                e.tensor_tensor(out=of, in0=O0s, in1=rrec[:, 0, :], op=ALU.mult)
                e.tensor_tensor(out=o1f, in0=O1s, in1=rrec[:, 1, :], op=ALU.mult)
                return e.scalar_tensor_tensor(out=of, in0=o1f, scalar=neglam, in1=of, op0=ALU.mult, op1=ALU.add)
            kb.op("dve", f1, r=[t_os, t_lnl, t_small], w=[t_fin, t_os])
            kb.op("act", lambda e: e.activation(out=sqb, in_=of, func=AF.Square), r=[t_fin], w=[t_sq])
            pending.append((u, qt))

        def finalize_b(set_):
            if not pending:
                return
            u, qt = pending.pop(0)
            s, h = units[u]
            bank = 2 * set_
            kb.op("pe", lambda e: e.matmul(self.ps[bank], self.onesb, sqb, start=True, stop=True), r=[t_sq, self.t_cst], w=[t_S[set_]])

            def fr(e):
                e.activation(out=rstd, in_=self.ps[bank], func=AF.Ln, scale=1.0 / 128, bias=self.epscol[:, 1:2])
                return e.activation(out=rstd, in_=rstd, func=AF.Exp, scale=-0.5)
            kb.op("act", fr, r=[t_S[set_], self.t_cst], w=[t_rstd])
            ab = fi[0] % 2
            fi[0] += 1
            kb.op("dve", lambda e: e.scalar_tensor_tensor(out=aob[ab], in0=of, scalar=gcol, in1=rstd, op0=ALU.mult, op1=ALU.mult),
                  r=[t_rstd, t_fin, t_small], w=[t_ao[ab]])
            n = s * tq + qt
            kb.dma("sp", self.AOT[h * 128:(h + 1) * 128, n * TT:(n + 1) * TT], aob[ab], r=[t_ao[ab]], w=[self.t_AO[n]])

        load(0)
        for u in range(len(units)):
            par = u % 2
            if u + 1 < len(units):
                load(u + 1)
            q_, k_, v_ = qT[par], kT[par], vv[par]
            for qt in range(tq):
                nk = 4 * qt + 4

                def c0_of(kbk):
                    jd = kbk - 4 * qt
                    return 128 * jd if jd > 0 else 0

                def issue_S(kbk, qt=qt, q_=q_, k_=k_, par=par):
                    set_ = kbk % 2
                    c0 = c0_of(kbk)

                    def f(e):
                        for m in range(2):
                            ins = e.matmul(self.ps[2 * set_ + m][:, c0:TT], k_[m * 64:(m + 1) * 64, kbk * 128:(kbk + 1) * 128],
                                           q_[m * 64:(m + 1) * 64, qt * TT + c0:(qt + 1) * TT], start=True, stop=True)
                        return ins
                    kb.op("pe", f, r=[t_qq[par], t_kk[par]], w=[t_S[set_]])

                    def ex(e):
                        for m in range(2):
                            ins = e.activation(out=pT[set_][:, m, c0:TT], in_=self.ps[2 * set_ + m][:, c0:TT], func=AF.Exp)
                        return ins
                    kb.op("act", ex, r=[t_S[set_]], w=[t_pT[set_]])
                    if kbk >= 4 * qt:
                        kb.op("dve", lambda e: e.memset(pT[set_][64:128, :, c0:c0 + 64], 0.0), w=[t_pT[set_]])
                    if 'dump' in DBG and u == 1 and qt == 0:
                        kb.dma("sp", self.dbgb[:, kbk * 1024:(kbk + 1) * 1024], pT[set_].rearrange("p m t -> p (m t)"), r=[t_pT[set_]], w=[Tok()])

                def issue_PV(kbk, qt=qt, nk=nk, v_=v_, par=par):
                    set_ = kbk % 2
                    c0 = c0_of(kbk)

                    def f(e):
                        for m in range(2):
                            e.matmul(self.ps[4 + m][:, c0:TT], v_[:, kbk, :], pT[set_][:, m, c0:TT], start=(kbk == 0), stop=(kbk == nk - 1))
                        for m in range(2):
                            ins = e.matmul(self.ps[6 + m][:, c0:TT], self.onesb, pT[set_][:, m, c0:TT], start=(kbk == 0), stop=(kbk == nk - 1))
                        return ins
                    kb.op("pe", f, r=[t_pT[set_], t_vv[par], self.t_cst], w=[t_acc])
                issue_S(0)
                for kbk in range(nk):
                    if kbk + 1 < nk:
                        issue_S(kbk + 1)
                    if kbk == 2:
                        finalize_b(0)
                    issue_PV(kbk)
                finalize_a(u, qt)
        self.kb.barrier() if False else None
        while pending:
            finalize_b(0)

    def ph_proj(self, wname, Kc, bias=None):
        kb = self.kb
        hb_ = [self.f32(8 * TT).rearrange("p (c t) -> p c t", c=8) for _ in range(2)]
        src = [self.bf(Kc * TT).rearrange("p (k t) -> p k t", k=Kc) for _ in range(2)]
        w = self.bf(8 * Kc * 128).rearrange("p (m k c) -> p m k c", m=8, k=Kc)
        t_h = [Tok(), Tok()]
        t_src = [Tok(), Tok()]
        t_w = Tok()
        t_ps = [Tok() for _ in range(8)]
        a, _ = self.wview(wname)
        kb.dma("sp", w.rearrange("p m k c -> p m (k c)"), a.rearrange("(m p x) -> p m x", p=128, x=Kc * 128), r=[self.t_w[wname]], w=[t_w])

        def load(n):
            par = n % 2
            kb.dma("sp", hb_[par], self.hT_tile(n), r=[self.t_hT[n]], w=[t_h[par]])
            kb.dma("sp", src[par], self.AOT[0:Kc * 128, n * TT:(n + 1) * TT].rearrange("(k p) t -> p k t", p=128), r=[self.t_AO[n]], w=[t_src[par]])
        load(0)
        for n in range(self.ntile):
            par = n % 2
            if n + 1 < self.ntile:
                load(n + 1)
            for m in range(8):
                bank = m % 4

                def mm(e, m=m, bank=bank, par=par):
                    for k in range(Kc):
                        ins = e.matmul(self.ps[bank], w[:, m, k, :], src[par][:, k, :], start=(k == 0), stop=(k == Kc - 1))
                    return ins
                kb.op("pe", mm, r=[t_w, t_src[par]], w=[t_ps[bank]])
                if bias is None:
                    kb.op("dve", lambda e, m=m, bank=bank, par=par: e.tensor_tensor(out=hb_[par][:, m, :], in0=hb_[par][:, m, :], in1=self.ps[bank], op=ALU.add),
                          r=[t_ps[bank]], w=[t_h[par]])
                else:
                    kb.op("dve", lambda e, m=m, bank=bank, par=par: e.scalar_tensor_tensor(out=hb_[par][:, m, :], in0=self.ps[bank], scalar=self.cc(bias, m), in1=hb_[par][:, m, :], op0=ALU.add, op1=ALU.add),
                          r=[t_ps[bank], self.t_cst], w=[t_h[par]])
            kb.dma("sp", self.hT_tile(n), hb_[par], r=[t_h[par]], w=[self.t_hT[n]])


    def ph_conf(self, l):
        kb = self.kb
        KC = 31
        H = KC - 1
        h1 = self.f32(8 * TT).rearrange("p (c t) -> p c t", c=8)
        xn1 = self.bf(8 * TT).rearrange("p (c t) -> p c t", c=8)
        sq1 = self.bf(8 * TT).rearrange("p (c t) -> p c t", c=8)
        t_h1, t_xn1 = Tok(), Tok()
        nb = {"h": [h1, h1], "xn": [xn1, xn1], "sq": sq1, "rstd": self.f32(TT), "t_h": [t_h1, t_h1],
              "t_xn": [t_xn1, t_xn1], "t_sq": Tok(), "t_rstd": Tok()}
        win = self.bf(16 * 8 * 128).rearrange("p (m k c) -> p m k c", m=16, k=8)
        wout = self.bf(8 * 8 * 128).rearrange("p (m k c) -> p m k c", m=8, k=8)
        diag = self.bf(8 * KC * 128).rearrange("p (c k q) -> p c k q", c=8, k=KC)
        glu = self.bf(8 * (TT + H)).rearrange("p (c t) -> p c t", c=8)
        y = self.f32(8 * TT).rearrange("p (c t) -> p c t", c=8)
        sb = self.bf(8 * TT).rearrange("p (c t) -> p c t", c=8)
        sg = [self.f32(TT) for _ in range(2)]
        mean = self.f32(TT)
        rs = self.f32(TT)
        mr = self.f32(TT)
        t_win, t_wout, t_diag, t_glu, t_y, t_s, t_stat = Tok(), Tok(), Tok(), Tok(), Tok(), Tok(), Tok()
        t_sg = [Tok(), Tok()]
        t_ps = [Tok() for _ in range(8)]
        a, _ = self.wview("cin")
        kb.dma("sp", win.rearrange("p m k c -> p m (k c)"), a.rearrange("(m p x) -> p m x", p=128, x=1024), r=[self.t_w["cin"]], w=[t_win])
        a, _ = self.wview("cout")
        kb.dma("sp", wout.rearrange("p m k c -> p m (k c)"), a.rearrange("(m p x) -> p m x", p=128, x=1024), r=[self.t_w["cout"]], w=[t_wout])
        ident = self.cc("ident", 0, 128)
        for c in range(8):
            kb.op("dve", lambda e, c=c: e.tensor_tensor(out=diag[:, c, :, :], in0=ident.unsqueeze(1).to_broadcast([128, KC, 128]),
                                                          in1=self.cc("cdw", c * KC, KC).unsqueeze(2).to_broadcast([128, KC, 128]), op=ALU.mult),
                  r=[self.t_cst], w=[t_diag])
        ybf, ysq = xn1, sq1
        for n in range(self.ntile):
            first = (n % self.tps == 0)
            self.load_h(nb, n)
            self.rmsnorm(nb, n, "nm%d" % l, 6, t_ps[6])
            if first:
                kb.op("pool", lambda e: e.memset(glu[:, :, 0:H], 0.0), w=[t_glu])
            for c in range(8):
                pa, pg = (c % 2) * 2, (c % 2) * 2 + 1
                si = c % 2

                def mm(e, c=c, pa=pa, pg=pg):
                    for (mch, bank) in ((c, pa), (8 + c, pg)):
                        for k in range(8):
                            ins = e.matmul(self.ps[bank], win[:, mch, k, :], xn1[:, k, :], start=(k == 0), stop=(k == 7))
                    return ins
                kb.op("pe", mm, r=[t_win, t_xn1], w=[t_ps[pa], t_ps[pg]])
                kb.op("act", lambda e, c=c, pg=pg, si=si: e.activation(out=sg[si], in_=self.ps[pg], func=AF.Sigmoid, bias=self.cc("cbin", 8 + c)),
                      r=[t_ps[pg], self.t_cst], w=[t_sg[si]])
                kb.op("dve", lambda e, c=c, pa=pa, si=si: e.scalar_tensor_tensor(out=glu[:, c, H:H + TT], in0=self.ps[pa], scalar=self.cc("cbin", c), in1=sg[si],
                                                                                 op0=ALU.add, op1=ALU.mult),
                      r=[t_ps[pa], t_sg[si], self.t_cst], w=[t_glu])
            for c in range(8):
                bank = 4 + c % 2

                def cv(e, c=c, bank=bank):
                    for k in range(KC):
                        ins = e.matmul(self.ps[bank], diag[:, c, k, :], glu[:, c, k:k + TT], start=(k == 0), stop=(k == KC - 1))
                    return ins
                kb.op("pe", cv, r=[t_diag, t_glu], w=[t_ps[bank]])

                def ev(e, c=c, bank=bank):
                    b = self.cc("cdwb", c)
                    e.activation(out=y[:, c, :], in_=self.ps[bank], func=AF.Identity, bias=b)
                    e.activation(out=ybf[:, c, :], in_=self.ps[bank], func=AF.Identity, bias=b)
                    return e.activation(out=ysq[:, c, :], in_=self.ps[bank], func=AF.Square, bias=b)
                kb.op("act", ev, r=[t_ps[bank], self.t_cst], w=[t_y, t_xn1, nb["t_sq"]])
            kb.op("pool", lambda e: e.tensor_copy(out=glu[:, :, 0:H], in_=glu[:, :, TT:TT + H]), w=[t_glu])

            def st(e):
                for c in range(8):
                    e.matmul(self.ps[6], self.onesb, ybf[:, c, :], start=(c == 0), stop=(c == 7))
                for c in range(8):
                    ins = e.matmul(self.ps[7], self.onesb, ysq[:, c, :], start=(c == 0), stop=(c == 7))
                return ins
            kb.op("pe", st, r=[t_xn1, nb["t_sq"], self.t_cst], w=[t_ps[6], t_ps[7]])

            def s1(e):
                e.tensor_scalar(out=mean, in0=self.ps[6], scalar1=1.0 / D, scalar2=None, op0=ALU.mult)
                e.tensor_tensor(out=mr, in0=mean, in1=mean, op=ALU.mult)
                return e.scalar_tensor_tensor(out=rs, in0=self.ps[7], scalar=1.0 / D, in1=mr, op0=ALU.mult, op1=ALU.subtract)
            kb.op("dve", s1, r=[t_ps[6], t_ps[7]], w=[t_stat])
            def sa(e):
                e.activation(out=rs, in_=rs, func=AF.Ln, bias=self.epscol[:, 1:2])
                return e.activation(out=rs, in_=rs, func=AF.Exp, scale=-0.5)
            kb.op("act", sa, r=[t_stat, self.t_cst], w=[t_stat])
            kb.op("dve", lambda e: e.tensor_tensor(out=mr, in0=mean, in1=rs, op=ALU.mult), r=[t_stat], w=[t_stat])
            for c in range(8):
                def nrm(e, c=c):
                    e.tensor_tensor(out=y[:, c, :], in0=y[:, c, :], in1=rs, op=ALU.mult)
                    return e.tensor_tensor(out=y[:, c, :], in0=y[:, c, :], in1=mr, op=ALU.subtract)
                kb.op("dve", nrm, r=[t_stat], w=[t_y])
                kb.op("act", lambda e, c=c: e.activation(out=sb[:, c, :], in_=y[:, c, :], func=AF.Silu, scale=self.cc("clg", c), bias=self.cc("clb", c)),
                      r=[t_y, self.t_cst], w=[t_s])
            for m in range(8):
                bank = m % 4

                def mo(e, m=m, bank=bank):
                    for k in range(8):
                        ins = e.matmul(self.ps[bank], wout[:, m, k, :], sb[:, k, :], start=(k == 0), stop=(k == 7))
                    return ins
                kb.op("pe", mo, r=[t_wout, t_s], w=[t_ps[bank]])
                kb.op("dve", lambda e, m=m, bank=bank: e.scalar_tensor_tensor(out=h1[:, m, :], in0=self.ps[bank], scalar=self.cc("cbo", m), in1=h1[:, m, :],
                                                                               op0=ALU.add, op1=ALU.add),
                      r=[t_ps[bank], self.t_cst], w=[t_h1])
            kb.dma("sp", self.hT_tile(n), h1, r=[t_h1], w=[self.t_hT[n]])


    def ph_gdnA(self, l):
        kb = self.kb
        nb = self.norm_bufs()
        NWB = 4
        win = [self.bf(4 * 8 * 128).rearrange("p (m k c) -> p m k c", m=4, k=8) for _ in range(NWB)]
        t_win = [Tok() for _ in range(NWB)]
        wba = self.bf(8 * 32).rearrange("p (k c) -> p k c", k=8)
        t_wba = Tok()
        a, _ = self.wview("gba")
        kb.dma("sp", wba.rearrange("p k c -> p (k c)"), a.rearrange("(p x) -> p x", p=128), r=[self.t_w["gba"]], w=[t_wba])
        gi_ap, _ = self.wview("gin")
        gi_v = gi_ap.rearrange("(g m p x) -> g p m x", m=4, p=128, x=1024)
        NU, NY, N3 = 3, 8, 3
        ust = [self.f32(TT + 3) for _ in range(NU)]
        yb = [self.f32(TT) for _ in range(NY)]
        t_ust = [Tok() for _ in range(NU)]
        t_y = [Tok() for _ in range(NY)]
        hb = self.f32(32 * 3).rearrange("p (c t) -> p c t", c=32)
        sqb = [self.bf(TT) for _ in range(N3)]
        rr = [self.f32(TT) for _ in range(N3)]
        t_sq = [Tok() for _ in range(N3)]
        t_rr = [Tok() for _ in range(N3)]
        vs = [self.bf(TT) for _ in range(N3)]
        t_vs = [Tok() for _ in range(N3)]
        qst = self.bf(8 * TT).rearrange("p (c t) -> p c t", c=8)
        kst = self.bf(8 * TT).rearrange("p (c t) -> p c t", c=8)
        ktok = self.bf(4 * 1024).rearrange("p (b c) -> p b c", b=4)
        vtok = self.bf(4 * 2048).rearrange("p (b c) -> p b c", b=4)
        zst = self.bf(16 * TT).rearrange("p (c t) -> p c t", c=16)
        gst = self.f32(4 * 32).rearrange("p (b c) -> p b c", b=4)
        gtmp = self.f32(4 * 16).rearrange("p (b c) -> p b c", b=4)
        negA = self.f32(16)
        t_qst, t_kst, t_ktok, t_vtok, t_zst, t_gst, t_negA = Tok(), Tok(), Tok(), Tok(), Tok(), Tok(), Tok()
        t_ps = [Tok() for _ in range(8)]
        psb = [p.bitcast(BF16) for p in self.ps]
        onecol = self.cc("ones", 0, 1)
        kb.op("act", lambda e: e.activation(out=negA, in_=self.cc("galog", 0, 16), func=AF.Exp), r=[self.t_cst], w=[t_negA])
        kb.op("pool", lambda e: e.tensor_scalar(out=negA, in0=negA, scalar1=-1.0, scalar2=None, op0=ALU.mult), r=[t_negA], w=[t_negA])
        dwo, _ = self.cp.off["gconv"]
        cst = self.cst

        def cw(k, ch):
            return cst[:, dwo + k * 32 + ch:dwo + k * 32 + ch + 1]
        self.t_G = [Tok() for _ in range(self.ntile)]
        cnt = {"g": 0, "c": 0, "t": 0, "n": 0}
        TB = (3, 6)

        def chunk_item(n, ch):
            par = n % 2
            first = (n % self.tps == 0)
            xn = nb["xn"][par]
            c = cnt["c"]
            cnt["c"] += 1
            bank = c % 3
            mm_ = ch % 4
            if mm_ == 0:
                wb = cnt["g"] % NWB
                cnt["g"] += 1
            else:
                wb = (cnt["g"] - 1) % NWB
            g = ch // 4

            def s0():
                if mm_ == 0:
                    kb.dma("sp", win[wb].rearrange("p m k c -> p m (k c)"), gi_v[g], r=[self.t_w["gin"]], w=[t_win[wb]])

                def mm(e):
                    for k in range(8):
                        ins = e.matmul(self.ps[bank], win[wb][:, mm_, k, :], xn[:, k, :], start=(k == 0), stop=(k == 7))
                    return ins
                kb.op("pe", mm, r=[t_win[wb], nb["t_xn"][par]], w=[t_ps[bank]])
            if ch >= 32:
                def s1z():
                    kb.op("act", lambda e: e.activation(out=zst[:, ch - 32, :], in_=self.ps[bank], func=AF.Silu), r=[t_ps[bank]], w=[t_zst])
                return [s0, s1z]
            ub = c % NU
            yi = c % NY
            u, y = ust[ub], yb[yi]

            def s1():
                if first:
                    kb.op("act", lambda e: e.activation(out=u[:, 0:3], in_=u[:, 0:3], func=AF.Copy, scale=0.0), w=[t_ust[ub]])
                else:
                    kb.op("act", lambda e: e.copy(out=u[:, 0:3], in_=hb[:, ch, :]), w=[t_ust[ub]])

                def ev(e):
                    e.copy(out=hb[:, ch, :], in_=self.ps[bank][:, TT - 3:TT])
                    return e.copy(out=u[:, 3:3 + TT], in_=self.ps[bank])
                kb.op("act", ev, r=[t_ps[bank], self.t_cst], w=[t_ust[ub]])

            def s2():
                def cv(e):
                    e.tensor_scalar(out=y, in0=u[:, 3:3 + TT], scalar1=cw(3, ch), scalar2=None, op0=ALU.mult)
                    e.scalar_tensor_tensor(out=y, in0=u[:, 2:2 + TT], scalar=cw(2, ch), in1=y, op0=ALU.mult, op1=ALU.add)
                    e.scalar_tensor_tensor(out=y, in0=u[:, 1:1 + TT], scalar=cw(1, ch), in1=y, op0=ALU.mult, op1=ALU.add)
                    return e.scalar_tensor_tensor(out=y, in0=u[:, 0:TT], scalar=cw(0, ch), in1=y, op0=ALU.mult, op1=ALU.add)
                kb.op("dve", cv, r=[t_ust[ub], self.t_cst], w=[t_y[yi]])
            i3 = c % N3
            if ch >= 16:
                tb = 6

                def s3v():
                    kb.op("act", lambda e: e.activation(out=vs[i3], in_=y, func=AF.Silu), r=[t_y[yi]], w=[t_vs[i3]])

                def s4v():
                    def tr(e):
                        for b in range(4):
                            ins = e.transpose(psb[tb][:, b * 128:(b + 1) * 128], vs[i3][:, b * 128:(b + 1) * 128], self.identb)
                        return ins
                    kb.op("pe", tr, r=[t_vs[i3], self.t_cst], w=[t_ps[tb]])

                def s5v():
                    kb.op("dve", lambda e: e.tensor_copy(out=vtok[:, :, (ch - 16) * 128:(ch - 15) * 128],
                                                         in_=psb[tb][:, 0:512].rearrange("p (b c) -> p b c", b=4)),
                          r=[t_ps[tb]], w=[t_vtok])
                return [s0, s1, s2, s3v, s4v, s5v]
            nbk = 4 + cnt["n"] % 2
            cnt["n"] += 1
            scale = float(128 ** -0.5) if ch < 8 else 1.0
            dst = qst[:, ch, :] if ch < 8 else kst[:, ch - 8, :]
            t_dst = t_qst if ch < 8 else t_kst

            def s3():
                def f(e):
                    e.activation(out=y, in_=y, func=AF.Silu)
                    return e.activation(out=sqb[i3], in_=y, func=AF.Square)
                kb.op("act", f, r=[t_y[yi]], w=[t_y[yi], t_sq[i3]])

            def s4():
                kb.op("pe", lambda e: e.matmul(self.ps[nbk], self.onesb, sqb[i3], start=True, stop=True), r=[t_sq[i3], self.t_cst], w=[t_ps[nbk]])

            def s5():
                def f(e):
                    e.activation(out=rr[i3], in_=self.ps[nbk], func=AF.Ln, bias=self.epscol[:, 0:1])
                    return e.activation(out=rr[i3], in_=rr[i3], func=AF.Exp, scale=-0.5)
                kb.op("act", f, r=[t_ps[nbk], self.t_cst], w=[t_rr[i3]])

            def s6():
                kb.op("dve", lambda e: e.scalar_tensor_tensor(out=dst, in0=y, scalar=scale, in1=rr[i3], op0=ALU.mult, op1=ALU.mult),
                      r=[t_rr[i3], t_y[yi]], w=[t_dst])
            if ch < 8:
                return [s0, s1, s2, s3, s4, s5, s6]
            tb = 3

            def s7():
                def trk(e):
                    for b in range(4):
                        ins = e.transpose(psb[tb][:, b * 128:(b + 1) * 128], kst[:, ch - 8, b * 128:(b + 1) * 128], self.identb)
                    return ins
                kb.op("pe", trk, r=[t_kst, self.t_cst], w=[t_ps[tb]])

            def s8():
                kb.op("dve", lambda e: e.tensor_copy(out=ktok[:, :, (ch - 8) * 128:(ch - 7) * 128],
                                                     in_=psb[tb][:, 0:512].rearrange("p (b c) -> p b c", b=4)),
                      r=[t_ps[tb]], w=[t_ktok])
            return [s0, s1, s2, s3, s4, s5, s6, s7, s8]

        def gate_item(n):
            par = n % 2
            xn = nb["xn"][par]
            pg = self.ps[7][:, 0:128].rearrange("p (b c) -> p b c", b=4)

            def g0():
                def mg(e):
                    for b in range(4):
                        for k in range(8):
                            ins = e.matmul(self.ps[7][:, b * 32:(b + 1) * 32], xn[:, k, b * 128:(b + 1) * 128], wba[:, k, :], start=(k == 0), stop=(k == 7))
                    return ins
                kb.op("pe", mg, r=[t_wba, nb["t_xn"][par]], w=[t_ps[7]])

            def g1():
                kb.op("act", lambda e: e.activation(out=gst[:, :, 0:16], in_=pg[:, :, 0:16], func=AF.Sigmoid), r=[t_ps[7]], w=[t_gst, t_ps[7]])
                kb.op("dve", lambda e: e.tensor_tensor(out=gtmp, in0=pg[:, :, 16:32], in1=self.cc("gdtb", 0, 16).unsqueeze(1).to_broadcast([128, 4, 16]), op=ALU.add),
                      r=[self.t_cst], w=[t_gst, t_ps[7]])

            def g2():
                kb.op("act", lambda e: e.activation(out=gtmp, in_=gtmp, func=AF.Exp), r=[t_gst], w=[t_gst])
                kb.op("act", lambda e: e.activation(out=gtmp, in_=gtmp, func=AF.Ln, bias=onecol), r=[t_gst, self.t_cst], w=[t_gst], strict=True)

            def g3():
                kb.op("pool", lambda e: e.tensor_tensor(out=gst[:, :, 16:32], in0=gtmp, in1=negA.unsqueeze(1).to_broadcast([128, 4, 16]), op=ALU.mult),
                      r=[t_gst, t_negA], w=[t_gst])
                cols = slice(n * TT, (n + 1) * TT)
                kb.dma("sp", self.Gtok[cols, :].rearrange("(b p) e -> p b e", p=128), gst, r=[t_gst], w=[self.t_G[n]])
            return [g0, g1, g2, g3]

        def store_item(n):
            cols = slice(n * TT, (n + 1) * TT)

            def f():
                kb.dma("sp", self.QT[:, cols].rearrange("(c p) t -> p c t", p=128), qst, r=[t_qst], w=[Tok()])
                kb.dma("sp", self.KT[:, cols].rearrange("(c p) t -> p c t", p=128), kst, r=[t_kst], w=[Tok()])
                kb.dma("sp", self.Vt[cols, :].rearrange("(b p) e -> p b e", p=128), ktok, r=[t_ktok], w=[Tok()])
                kb.dma("sp", self.Vtok[cols, :].rearrange("(b p) e -> p b e", p=128), vtok, r=[t_vtok], w=[Tok()])
                kb.dma("sp", self.ZsT[:, cols].rearrange("(c p) t -> p c t", p=128), zst, r=[t_zst], w=[Tok()])
            return [None, None, f]

        items = [[lambda: self.load_h(nb, 0)], [lambda: self.rmsnorm(nb, 0, "nm%d" % l, 7, t_ps[7])]]
        for n in range(self.ntile):
            items.append(gate_item(n))
            for ch in range(48):
                items.append(chunk_item(n, ch))
                if ch == 8 and n + 1 < self.ntile:
                    items.append([lambda n=n: self.load_h(nb, n + 1)])
                if ch == 20 and n + 1 < self.ntile:
                    items.append([lambda n=n: self.rmsnorm(nb, n + 1, "nm%d" % l, 7, t_ps[7])])
            items.append(store_item(n))
        self.run_pipe(items, 9)

    def ph_gdnB(self, l):
        kb = self.kb
        S, NSEQ = self.S, self.NSEQ
        NCH = S // 128
        ps = self.ps
        psb = [p.bitcast(BF16) for p in ps]
        H4 = lambda ap: ap.rearrange("p (j d) -> p j d", j=4)
        c_ = self.cc
        identf = c_("ident", 0, 128)
        bc4 = lambda ap: ap.unsqueeze(1).to_broadcast([128, 4, 128])
        col4 = lambda ap: ap.unsqueeze(2).to_broadcast([128, 4, 128])
        self.t_AO = [Tok() for _ in range(self.ntile)]

        def T4(dt="f"):
            return H4(self.f32(512)) if dt == "f" else H4(self.bf(512))

        class St:
            pass
        streams = []
        for si in range(NSEQ):
            st = St()
            st.s = si
            st.pb = (0, 1) if si % 2 == 0 else (2, 3)
            st.rb = (4, 5) if si % 2 == 0 else (6, 7)
            st.gt = self.f32(NCH * 32).rearrange("p (c e) -> p c e", c=NCH)
            st.t_gt = Tok()
            NL = 3
            st.kf = [self.bf(256).rearrange("p (a t) -> p a t", a=2) for _ in range(NL)]
            st.qf = [self.bf(256).rearrange("p (a t) -> p a t", a=2) for _ in range(NL)]
            st.ktk = [self.bf(256).rearrange("p (a d) -> p a d", a=2) for _ in range(NL)]
            st.vtk = [T4("b") for _ in range(NL)]
            st.zs = [T4("b") for _ in range(NL)]
            st.t_ld = [Tok() for _ in range(NL)]
            st.NL = NL
            for nm in ("DLt", "DUt", "ELt", "EUt", "Lf", "PT32", "S32", "osb", "osq"):
                setattr(st, nm, T4())
            for nm in ("Lbd", "Lo1", "Lo2", "LbdTb", "PTb", "Tbdb", "Yb", "vbb", "kbgb", "Sb", "vnb", "onb"):
                setattr(st, nm, T4("b"))
            st.Ab = [T4("b"), T4("b")]
            st.ATb = [T4("b"), T4("b")]
            st.gsm = self.f32(16)
            st.bg = self.f32(4)
            st.ss = self.f32(4)
            st.aTb = [T4("b"), T4("b")]
            st.TTb = [T4("b"), T4("b")]
            st.u32 = [T4(), T4()]
            st.wTb = [T4("b"), T4("b")]
            st.kdecb = [T4("b"), T4("b")]
            st.esm = [self.f32(16), self.f32(16)]
            st.oTb = [T4("b"), T4("b")]
            st.tk = {n: Tok(n) for n in ("DL", "DU", "EL", "EU", "Lf", "Lbd", "Lo1", "Lo2", "LbdT", "PT32", "PTb", "A0", "A1", "AT0", "AT1",
                                         "Tbd", "Y", "vb", "kbg", "gsm", "bg", "S32", "Sb", "vnb", "osb", "osq", "ss", "onb")}
            st.t2 = {n: [Tok(), Tok()] for n in ("aT", "TT", "u", "wT", "kdec", "esm", "oT")}
            streams.append(st)
        t_ps = [Tok() for _ in range(8)]

        def load(st, g4, n):
            li = n % st.NL
            s = st.s
            tc = slice(s * S + n * 128, s * S + (n + 1) * 128)
            t = st.t_ld[li]
            kb.dma("sp", st.kf[li], self.KT[2 * g4 * 128:(2 * g4 + 2) * 128, tc].rearrange("(a p) t -> p a t", p=128), w=[t])
            kb.dma("sp", st.qf[li], self.QT[2 * g4 * 128:(2 * g4 + 2) * 128, tc].rearrange("(a p) t -> p a t", p=128), w=[t])
            kb.dma("sp", st.ktk[li], self.Vt[tc, 2 * g4 * 128:(2 * g4 + 2) * 128].rearrange("p (a d) -> p a d", a=2), w=[t])
            kb.dma("sp", st.vtk[li], self.Vtok[tc, 4 * g4 * 128:(4 * g4 + 4) * 128].rearrange("p (j d) -> p j d", j=4), w=[t])
            kb.dma("sp", st.zs[li], self.ZsT[4 * g4 * 128:(4 * g4 + 4) * 128, tc].rearrange("(j p) t -> p j t", p=128), w=[t])

        def tr4(bank, src):
            def f(e):
                for j in range(4):
                    ins = e.transpose(psb[bank][:, j * 128:(j + 1) * 128], src[:, j, :], self.identb)
                return ins
            return f

        def mm4(bank, lhs, rhs):
            def f(e):
                for j in range(4):
                    ins = e.matmul(ps[bank][:, j * 128:(j + 1) * 128], lhs[:, j, :], rhs[:, j, :], start=True, stop=True)
                return ins
            return f

        def pre(st, g4, n):
            par = n % 2
            li = n % st.NL
            tk, t2 = st.tk, st.t2
            b0, b1 = st.pb
            t_ld = st.t_ld[li]
            kf, qf, ktk, vtk = st.kf[li], st.qf[li], st.ktk[li], st.vtk[li]
            gn = st.gt[:, n, 16 + 4 * g4:16 + 4 * g4 + 4]
            bn = st.gt[:, n, 4 * g4:4 * g4 + 4]
            esm = st.esm[par]

            def p1(e):
                for j in range(4):
                    gb = gn[:, j:j + 1].to_broadcast([128, 128])
                    e.matmul(ps[b0][:, j * 128:(j + 1) * 128], c_("triT", 0, 128), gb, start=True, stop=False)
                    e.matmul(ps[b0][:, j * 128:(j + 1) * 128], gb, c_("negtriT", 0, 128), start=False, stop=True)
                e.matmul(ps[b1][:, 384:388], c_("triT", 0, 128), gn, start=True, stop=True)
                for j in range(4):
                    ins = e.matmul(ps[b1][:, 392 + j:393 + j], gn[:, j:j + 1].to_broadcast([128, 128]), c_("ones", 0, 1), start=True, stop=True)
                return ins
            kb.op("pe", p1, r=[st.t_gt, self.t_cst], w=[t_ps[b0], t_ps[b1]])

            def p2(e):
                e.tensor_tensor(out=st.DLt, in0=H4(ps[b0]), in1=bc4(c_("maskL", 0, 128)), op=ALU.add)
                e.tensor_tensor(out=st.DUt, in0=H4(ps[b0]), in1=bc4(c_("maskU", 0, 128)), op=ALU.add)
                e.tensor_copy(out=st.gsm[:, 0:12], in_=ps[b1][:, 384:396])
                return e.tensor_copy(out=st.gsm[:, 12:16], in_=ps[b0][:, 127:512:128])
            kb.op("dve", p2, r=[self.t_cst], w=[tk["DL"], tk["DU"], tk["gsm"], t_ps[b0], t_ps[b1]])
            yield

            def p3(e):
                e.activation(out=st.ELt, in_=st.DLt, func=AF.Exp)
                e.activation(out=st.EUt, in_=st.DUt, func=AF.Exp, scale=-1.0)
                e.activation(out=esm[:, 0:12], in_=st.gsm[:, 0:12], func=AF.Exp)
                return e.activation(out=esm[:, 12:16], in_=st.gsm[:, 12:16], func=AF.Exp, scale=-1.0)
            kb.op("act", p3, r=[tk["DL"], tk["DU"], tk["gsm"]], w=[tk["EL"], tk["EU"], t2["esm"][par]])
            yield
            kb.op("pool", lambda e: e.tensor_tensor(out=st.bg, in0=bn, in1=esm[:, 0:4], op=ALU.mult), r=[st.t_gt, t2["esm"][par]], w=[tk["bg"]])

            def p11(e):
                e.tensor_tensor(out=st.vbb, in0=vtk, in1=col4(bn), op=ALU.mult)
                k4 = ktk.unsqueeze(2).to_broadcast([128, 2, 2, 128])
                e.tensor_tensor(out=st.kbgb.rearrange("p (a r) d -> p a r d", a=2), in0=k4,
                                in1=st.bg.rearrange("p (a r) -> p a r", a=2).unsqueeze(3).to_broadcast([128, 2, 2, 128]), op=ALU.mult)
                return e.tensor_tensor(out=st.kdecb[par].rearrange("p (a r) d -> p a r d", a=2), in0=k4,
                                       in1=esm[:, 12:16].rearrange("p (a r) -> p a r", a=2).unsqueeze(3).to_broadcast([128, 2, 2, 128]), op=ALU.mult)
            kb.op("pool", p11, r=[t_ld, st.t_gt, tk["bg"], t2["esm"][par]], w=[tk["vb"], tk["kbg"], t2["kdec"][par]])

            def p4(e):
                for a in range(2):
                    e.matmul(ps[b0][:, a * 128:(a + 1) * 128], kf[:, a, :], kf[:, a, :], start=True, stop=True)
                for a in range(2):
                    ins = e.matmul(ps[b0][:, 256 + a * 128:256 + (a + 1) * 128], kf[:, a, :], qf[:, a, :], start=True, stop=True)
                return ins
            kb.op("pe", p4, r=[t_ld], w=[t_ps[b0]])

            def p5(e):
                for j in range(4):
                    a = j // 2
                    e.scalar_tensor_tensor(out=st.Lf[:, j, :], in0=ps[b0][:, a * 128:(a + 1) * 128], scalar=bn[:, j:j + 1], in1=st.ELt[:, j, :],
                                           op0=ALU.mult, op1=ALU.mult)
                mq = ps[b0][:, 256:512].rearrange("p (a d) -> p a d", a=2).unsqueeze(2).to_broadcast([128, 2, 2, 128])
                e.tensor_tensor(out=st.aTb[par].rearrange("p (a r) d -> p a r d", a=2), in0=mq,
                                in1=st.EUt.rearrange("p (a r) d -> p a r d", a=2), op=ALU.mult)
                return e.tensor_tensor(out=st.Lbd, in0=st.Lf, in1=bc4(c_("mbd32", 0, 128)), op=ALU.mult)
            kb.op("dve", p5, r=[tk["EL"], tk["EU"], st.t_gt, self.t_cst], w=[tk["Lf"], t2["aT"][par], tk["Lbd"], t_ps[b0]])
            yield

            def p6(e):
                e.tensor_tensor(out=st.Lo1, in0=st.Lf, in1=bc4(c_("moff64", 0, 128)), op=ALU.mult)
                return e.tensor_tensor(out=st.Lo2, in0=st.Lf, in1=bc4(c_("moff128", 0, 128)), op=ALU.mult)
            kb.op("pool", p6, r=[tk["Lf"], self.t_cst], w=[tk["Lo1"], tk["Lo2"]])
            pT = H4(psb[b1][:, 0:512])
            kb.op("pe", tr4(b1, st.Lbd), r=[tk["Lbd"], self.t_cst], w=[t_ps[b1]])
            kb.op("act", lambda e: e.copy(out=st.LbdTb, in_=pT), r=[t_ps[b1]], w=[tk["LbdT"], t_ps[b1]])
            yield
            kb.op("dve", lambda e: e.scalar_tensor_tensor(out=st.PT32, in0=st.LbdTb, scalar=-1.0, in1=bc4(identf), op0=ALU.mult, op1=ALU.add),
                  r=[tk["LbdT"], self.t_cst], w=[tk["PT32"]])
            kb.op("act", lambda e: e.copy(out=st.PTb, in_=st.PT32), r=[tk["PT32"]], w=[tk["PTb"]])
            yield
            curA, curAT, tA, tAT = st.Lbd, st.LbdTb, tk["Lbd"], tk["LbdT"]
            for lev in range(4):
                i = lev % 2
                nA, nAT, tnA, tnAT = st.Ab[i], st.ATb[i], tk["A%d" % i], tk["AT%d" % i]
                kb.op("pe", mm4(b0, curAT, curA), r=[tA, tAT], w=[t_ps[b0]])
                kb.op("act", lambda e, nA=nA: e.copy(out=nA, in_=H4(ps[b0])), r=[t_ps[b0]], w=[tnA, t_ps[b0]])
                if lev < 3:
                    kb.op("pe", mm4(b1, curA, curAT), r=[tA, tAT], w=[t_ps[b1]])
                    kb.op("dve", lambda e, nAT=nAT: e.tensor_copy(out=nAT, in_=H4(ps[b1])), r=[t_ps[b1]], w=[tnAT, t_ps[b1]])
                yield
                kb.op("pe", mm4(b0, nA, st.PTb), r=[tnA, tk["PTb"]], w=[t_ps[b0]])
                kb.op("dve", lambda e: e.tensor_tensor(out=st.PT32, in0=st.PT32, in1=H4(ps[b0]), op=ALU.add), r=[t_ps[b0]], w=[tk["PT32"], t_ps[b0]])
                kb.op("act", lambda e: e.copy(out=st.PTb, in_=st.PT32), r=[tk["PT32"]], w=[tk["PTb"]])
                yield
                curA, curAT, tA, tAT = nA, nAT, tnA, tnAT
            for (Lo, tLo, last) in ((st.Lo1, tk["Lo1"], False), (st.Lo2, tk["Lo2"], True)):
                kb.op("pe", tr4(b1, st.PTb), r=[tk["PTb"], self.t_cst], w=[t_ps[b1]])
                kb.op("act", lambda e: e.copy(out=st.Tbdb, in_=pT), r=[t_ps[b1]], w=[tk["Tbd"], t_ps[b1]])
                kb.op("pe", mm4(b0, Lo, st.PTb), r=[tLo, tk["PTb"]], w=[t_ps[b0]])
                kb.op("act", lambda e: e.copy(out=st.Yb, in_=H4(ps[b0])), r=[t_ps[b0]], w=[tk["Y"], t_ps[b0]])
                yield
                kb.op("pe", mm4(b1, st.Tbdb, st.Yb), r=[tk["Tbd"], tk["Y"]], w=[t_ps[b1]])
                kb.op("dve", lambda e: e.tensor_tensor(out=st.PT32, in0=st.PT32, in1=H4(ps[b1]), op=ALU.subtract), r=[t_ps[b1]], w=[tk["PT32"], t_ps[b1]])
                if not last:
                    kb.op("act", lambda e: e.copy(out=st.PTb, in_=st.PT32), r=[tk["PT32"]], w=[tk["PTb"]])
                else:
                    kb.op("act", lambda e: e.copy(out=st.TTb[par], in_=st.PT32), r=[tk["PT32"]], w=[t2["TT"][par]])
                yield
            kb.op("pe", mm4(b0, st.TTb[par], st.vbb), r=[t2["TT"][par], tk["vb"]], w=[t_ps[b0]])
            kb.op("dve", lambda e: e.tensor_copy(out=st.u32[par], in_=H4(ps[b0])), r=[t_ps[b0]], w=[t2["u"][par], t_ps[b0]])
            kb.op("pe", mm4(b1, st.kbgb, st.TTb[par]), r=[tk["kbg"], t2["TT"][par]], w=[t_ps[b1]])
            kb.op("act", lambda e: e.copy(out=st.wTb[par], in_=H4(ps[b1])), r=[t_ps[b1]], w=[t2["wT"][par], t_ps[b1]])
            yield

        def rec(st, g4, n):
            par = n % 2
            li = n % st.NL
            tk, t2 = st.tk, st.t2
            b0, b1 = st.rb
            qf, zs = st.qf[li], st.zs[li]
            t_ld = st.t_ld[li]
            esm = st.esm[par]
            s = st.s
            kb.op("pe", mm4(b0, st.wTb[par], st.Sb), r=[t2["wT"][par], tk["Sb"]], w=[t_ps[b0]])
            kb.op("dve", lambda e: e.tensor_tensor(out=st.vnb, in0=st.u32[par], in1=H4(ps[b0]), op=ALU.subtract), r=[t2["u"][par]], w=[tk["vnb"], t_ps[b0]])
            yield

            def r3(e):
                for j in range(4):
                    e.matmul(ps[b0][:, j * 128:(j + 1) * 128], qf[:, j // 2, :], st.Sb[:, j, :], start=True, stop=True)
                for j in range(4):
                    ins = e.matmul(ps[b1][:, j * 128:(j + 1) * 128], st.aTb[par][:, j, :], st.vnb[:, j, :], start=True, stop=True)
                return ins
            kb.op("pe", r3, r=[t2["aT"][par], tk["vnb"], tk["Sb"], t_ld], w=[t_ps[b0], t_ps[b1]])

            def r5(e):
                e.tensor_tensor(out=st.osb, in0=H4(ps[b0]), in1=col4(esm[:, 0:4]), op=ALU.mult)
                return e.tensor_tensor(out=st.osb, in0=st.osb, in1=H4(ps[b1]), op=ALU.add)
            kb.op("dve", r5, r=[t2["esm"][par]], w=[tk["osb"], t_ps[b0], t_ps[b1]])
            yield
            kb.op("pe", mm4(b0, st.kdecb[par], st.vnb), r=[t2["kdec"][par], tk["vnb"]], w=[t_ps[b0]])

            def r4(e):
                e.tensor_tensor(out=st.S32, in0=st.S32, in1=col4(esm[:, 8:12]), op=ALU.mult)
                return e.tensor_tensor(out=st.S32, in0=st.S32, in1=H4(ps[b0]), op=ALU.add)
            kb.op("dve", r4, r=[t2["esm"][par]], w=[tk["S32"], t_ps[b0]])
            kb.op("act", lambda e: e.copy(out=st.Sb, in_=st.S32), r=[tk["S32"]], w=[tk["Sb"]])
            yield
            kb.op("act", lambda e: e.activation(out=st.osq, in_=st.osb, func=AF.Square), r=[tk["osb"]], w=[tk["osq"]])
            kb.op("dve", lambda e: e.reduce_sum(out=st.ss, in_=st.osq, axis=AX.X), r=[tk["osq"]], w=[tk["ss"]])
            yield
            kb.op("act", lambda e: e.activation(out=st.ss, in_=st.ss, func=AF.Ln, scale=1.0 / 128, bias=self.epscol[:, 0:1]), r=[tk["ss"], self.t_cst], w=[tk["ss"]])
            kb.op("act", lambda e: e.activation(out=st.ss, in_=st.ss, func=AF.Exp, scale=-0.5), r=[tk["ss"]], w=[tk["ss"]], strict=True)
            yield
            kb.op("dve", lambda e: e.tensor_tensor(out=st.osb, in0=st.osb, in1=col4(st.ss), op=ALU.mult), r=[tk["ss"]], w=[tk["osb"]], strict=True)
            kb.op("pool", lambda e: e.tensor_tensor(out=st.onb, in0=st.osb, in1=bc4(c_("gnorm", 0, 128)), op=ALU.mult), r=[tk["osb"], self.t_cst], w=[tk["onb"]])
            yield
            kb.op("pe", tr4(b1, st.onb), r=[tk["onb"], self.t_cst], w=[t_ps[b1]])
            kb.op("dve", lambda e: e.tensor_tensor(out=st.oTb[par], in0=H4(psb[b1][:, 0:512]), in1=zs, op=ALU.mult), r=[t_ld], w=[t2["oT"][par], t_ps[b1]])
            dst = self.AOT[4 * g4 * 128:(4 * g4 + 4) * 128, s * S + n * 128:s * S + (n + 1) * 128].rearrange("(j p) t -> p j t", p=128)
            kb.dma("sp", dst, st.oTb[par], r=[t2["oT"][par]], w=[Tok()])
            yield

        def drive(gens):
            gens = list(gens)
            while gens:
                nxt = []
                for g in gens:
                    try:
                        next(g)
                        nxt.append(g)
                    except StopIteration:
                        pass
                gens = nxt

        for st in streams:
            cols = slice(st.s * S, (st.s + 1) * S)
            kb.dma("sp", st.gt, self.Gtok[cols, :].rearrange("(c p) e -> p c e", p=128), w=[st.t_gt])
        for g4 in range(4):
            for st in streams:
                def init(e, st=st):
                    e.memset(st.S32, 0.0)
                    return e.memset(st.Sb, 0.0)
                kb.op("dve", init, w=[st.tk["S32"], st.tk["Sb"]])
                load(st, g4, 0)
                if NCH > 1:
                    load(st, g4, 1)
            drive([pre(st, g4, 0) for st in streams])
            for n in range(NCH):
                gens = []
                for st in streams:
                    if n + 2 < NCH:
                        load(st, g4, n + 2)
                for st in streams:
                    if n + 1 < NCH:
                        gens.append(pre(st, g4, n + 1))
                    gens.append(rec(st, g4, n))
                drive(gens)


FULL_PLAN = [("pro",),
             ("attnA", 0, 0), ("attnB", 0, 0), ("proj", "wo0", 8), ("ffn", 0),
             ("conf", 1), ("ffn", 1),
             ("gdnA", 2), ("gdnB", 2), ("proj", "gwo", 16), ("ffn", 2),
             ("attnA", 3, 1), ("attnB", 3, 1), ("proj", "wo1", 8), ("ffn", 3),
             ("fin",)]


def run_prog(inputs, S, NSEQ, ncores, plan):
    cp, wp = host_pack(inputs)
    prog = Prog(S, NSEQ, cp, wp, plan)
    nc = prog.build()
    consts = cp.build()
    wts = wp.build()
    x = np.asarray(inputs["x"], np.float32)
    pos = np.asarray(inputs["positions"], np.int32)
    in_maps = []
    for c in range(ncores):
        xs = np.ascontiguousarray(x[c * NSEQ:(c + 1) * NSEQ, :S].reshape(NSEQ * S, D))
        ps = np.ascontiguousarray(pos[c * NSEQ:(c + 1) * NSEQ, :S].reshape(1, NSEQ * S))
        in_maps.append({"x": xs, "pos": ps, "consts": consts, "wts": wts})
    res = run_bass_kernel_spmd(nc, in_maps, core_ids=list(range(ncores)))
    outs = [np.asarray(r["out"]).reshape(NSEQ, S, D) for r in res.results]
    prog.res = res
    return np.concatenate(outs, axis=0), prog


def kernel(**inputs):
    out, _ = run_prog(inputs, 4096, 2, 8, FULL_PLAN)
    return out.astype(np.float32)
```
